# Optimizing a Trainium2 kernel written in Bass

```python
import math
import jax, jax.numpy as jnp
from jax import lax
import numpy as np

D_MODEL = 1024
BATCH = 2
SEQ = 8192
DEPTH = 2

N_MIXERS = 2
N_POOL_LAYERS = (DEPTH + 1) // 2
N_GDN_LAYERS = DEPTH // 2
PLE_DIM = 256
D_FF = 4 * D_MODEL
POOL_WINDOWS = (2, 4, 8, 16)
N_POOL_GROUPS = len(POOL_WINDOWS)
POOL_GROUP = D_MODEL // N_POOL_GROUPS
GDN_HEADS = 8
GDN_HEAD_DIM = 128
GDN_KEY_DIM = GDN_HEADS * GDN_HEAD_DIM
GDN_VAL_DIM = GDN_HEADS * GDN_HEAD_DIM
GDN_CONV_DIM = 2 * GDN_KEY_DIM + GDN_VAL_DIM
GDN_IN_DIM = GDN_CONV_DIM + GDN_VAL_DIM + 2 * GDN_HEADS
CONV_WIDTH = 4
CHUNK = 64
DEEPNORM_ALPHA = (2.0 * DEPTH) ** 0.25
DEEPNORM_BETA = (8.0 * DEPTH) ** -0.25
LN_EPS = 1e-5
RMS_EPS = 1e-6
L2_EPS = 1e-6

kernel_name = "pool_gdn_deepnorm_hybrid"


def layer_norm(x, g, b):
    xf = x.astype(jnp.float32)
    mu = jnp.mean(xf, axis=-1, keepdims=True)
    var = jnp.mean(jnp.square(xf - mu), axis=-1, keepdims=True)
    return ((xf - mu) * lax.rsqrt(var + LN_EPS) * g + b).astype(x.dtype)


def pool_mixer(x, w_grp, b_grp, scale):
    B, S, _ = x.shape
    xg = x.astype(jnp.float32).reshape(B, S, N_POOL_GROUPS, POOL_GROUP)
    cs = jnp.cumsum(xg, axis=1)
    pos = jnp.arange(1, S + 1, dtype=jnp.float32)[None, :, None]
    outs = []
    for gi, w in enumerate(POOL_WINDOWS):
        c = cs[:, :, gi]
        c_prev = jnp.pad(c, ((0, 0), (w, 0), (0, 0)))[:, :S]
        mean = (c - c_prev) / jnp.minimum(pos, float(w))
        outs.append(mean - xg[:, :, gi])
    pooled = jnp.stack(outs, axis=2).astype(x.dtype)
    y = jnp.einsum('bsgc,gcd->bsgd', pooled, w_grp) + b_grp
    return y.reshape(B, S, D_MODEL) * scale


def causal_depthwise_conv(x, w):
    C = x.shape[-1]
    return lax.conv_general_dilated(
        x, w[:, None, :].astype(x.dtype), window_strides=(1,),
        padding=[(CONV_WIDTH - 1, 0)], dimension_numbers=('NWC', 'WIO', 'NWC'),
        feature_group_count=C)


def l2_normalize(t):
    return t * lax.rsqrt(jnp.sum(jnp.square(t), axis=-1, keepdims=True) + L2_EPS)


def chunk_gated_delta_rule(q, k, v, g, beta):
    B, S, H, Dk = q.shape
    Dv = v.shape[-1]
    N = S // CHUNK

    def to_chunks(t):
        return t.reshape(B, N, CHUNK, H, -1).transpose(0, 3, 1, 2, 4)

    q, k, v = to_chunks(q), to_chunks(k), to_chunks(v)
    g = g.reshape(B, N, CHUNK, H).transpose(0, 3, 1, 2)
    beta = beta.reshape(B, N, CHUNK, H).transpose(0, 3, 1, 2)
    g = jnp.cumsum(g, axis=-1)

    idx = jnp.arange(CHUNK)
    causal = idx[:, None] >= idx[None, :]
    strict = idx[:, None] > idx[None, :]
    decay = jnp.exp(jnp.where(causal, g[..., :, None] - g[..., None, :], -jnp.inf))

    k_beta = k * beta[..., None]
    a = jnp.einsum('bhncd,bhnmd->bhncm', k_beta, k) * decay
    m = jnp.where(strict, a, 0.0) + jnp.eye(CHUNK, dtype=jnp.float32)
    rhs = jnp.concatenate([v * beta[..., None], k_beta * jnp.exp(g)[..., None]], axis=-1)
    sol = lax.linalg.triangular_solve(m, rhs, left_side=True, lower=True, unit_diagonal=True)
    u, w = sol[..., :Dv], sol[..., Dv:]

    qk = jnp.einsum('bhncd,bhnmd->bhncm', q, k) * decay
    q_dec = q * jnp.exp(g)[..., None]
    k_dec = k * jnp.exp(g[..., -1:] - g)[..., None]
    g_last = jnp.exp(g[..., -1])

    def step(state, xs):
        u_c, w_c, qk_c, qd_c, kd_c, gl_c = xs
        v_new = u_c - jnp.einsum('bhcd,bhde->bhce', w_c, state)
        o_c = (jnp.einsum('bhcd,bhde->bhce', qd_c, state)
               + jnp.einsum('bhcm,bhme->bhce', qk_c, v_new))
        state = state * gl_c[..., None, None] + jnp.einsum('bhcd,bhce->bhde', kd_c, v_new)
        return state, o_c

    xs = tuple(jnp.moveaxis(t, 2, 0) for t in (u, w, qk, q_dec, k_dec, g_last))
    s0 = jnp.zeros((B, H, Dk, Dv), jnp.float32)
    _, o = lax.scan(step, s0, xs)
    return o.transpose(1, 0, 3, 2, 4).reshape(B, S, H, Dv)


def gated_deltanet(x, w_in, conv_w, a_log, dt_bias, norm_w, w_out):
    B, S, _ = x.shape
    H, Dh = GDN_HEADS, GDN_HEAD_DIM
    proj = x @ w_in
    qkv, z, b_logit, a_logit = jnp.split(
        proj, [GDN_CONV_DIM, GDN_CONV_DIM + GDN_VAL_DIM, GDN_CONV_DIM + GDN_VAL_DIM + H], axis=-1)
    qkv = jax.nn.silu(causal_depthwise_conv(qkv, conv_w))
    q, k, v = jnp.split(qkv, [GDN_KEY_DIM, 2 * GDN_KEY_DIM], axis=-1)
    q = l2_normalize(q.reshape(B, S, H, Dh).astype(jnp.float32)) * (Dh ** -0.5)
    k = l2_normalize(k.reshape(B, S, H, Dh).astype(jnp.float32))
    v = v.reshape(B, S, H, Dh).astype(jnp.float32)
    beta = jax.nn.sigmoid(b_logit.astype(jnp.float32))
    g = -jnp.exp(a_log.astype(jnp.float32)) * jax.nn.softplus(
        a_logit.astype(jnp.float32) + dt_bias.astype(jnp.float32))
    o = chunk_gated_delta_rule(q, k, v, g, beta)
    zf = z.reshape(B, S, H, Dh).astype(jnp.float32)
    o = o * lax.rsqrt(jnp.mean(jnp.square(o), axis=-1, keepdims=True) + RMS_EPS) * norm_w * jax.nn.silu(zf)
    return o.reshape(B, S, GDN_VAL_DIM).astype(x.dtype) @ w_out


def squared_relu_mlp(x, w1, w2):
    return jnp.square(jax.nn.relu(x @ w1)) @ w2


def setup_inputs(seed: int = 0) -> dict:
    key = jax.random.key(seed)
    ks = jax.random.split(key, 20)
    f32 = jnp.float32
    nrm = lambda k, s: jax.random.normal(k, s, f32)
    dt = jnp.exp(jax.random.uniform(ks[9], (N_GDN_LAYERS, GDN_HEADS), f32,
                                    math.log(1e-3), math.log(1e-1)))
    return {
        "x": nrm(ks[0], (BATCH, SEQ, D_MODEL)),
        "p": nrm(ks[1], (DEPTH, BATCH, SEQ, PLE_DIM)),
        "ln_gain": 1.0 + 0.02 * nrm(ks[2], (DEPTH, 2, D_MODEL)),
        "ln_bias": 0.02 * nrm(ks[3], (DEPTH, 2, D_MODEL)),
        "pool_w": nrm(ks[4], (N_POOL_LAYERS, N_POOL_GROUPS, POOL_GROUP, POOL_GROUP)) * (POOL_GROUP ** -0.5) * DEEPNORM_BETA,
        "pool_b": 0.02 * nrm(ks[5], (N_POOL_LAYERS, N_POOL_GROUPS, POOL_GROUP)),
        "pool_scale": 1.0 + 0.1 * nrm(ks[6], (N_POOL_LAYERS, D_MODEL)),
        "gdn_w_in": nrm(ks[7], (N_GDN_LAYERS, D_MODEL, GDN_IN_DIM)) * (D_MODEL ** -0.5),
        "gdn_conv": nrm(ks[8], (N_GDN_LAYERS, CONV_WIDTH, GDN_CONV_DIM)) * (CONV_WIDTH ** -0.5),
        "gdn_a_log": jnp.log(jax.random.uniform(ks[10], (N_GDN_LAYERS, GDN_HEADS), f32, 1.0, 16.0)),
        "gdn_dt_bias": dt + jnp.log(-jnp.expm1(-dt)),
        "gdn_norm_w": 1.0 + 0.02 * nrm(ks[11], (N_GDN_LAYERS, GDN_HEAD_DIM)),
        "gdn_w_out": nrm(ks[12], (N_GDN_LAYERS, GDN_VAL_DIM, D_MODEL)) * (GDN_VAL_DIM ** -0.5) * DEEPNORM_BETA,
        "mlp_w1": nrm(ks[13], (DEPTH, D_MODEL, D_FF)) * (D_MODEL ** -0.5),
        "mlp_w2": nrm(ks[14], (DEPTH, D_FF, D_MODEL)) * (D_FF ** -0.5) * DEEPNORM_BETA,
        "ple_gate_w": nrm(ks[15], (DEPTH, D_MODEL, D_MODEL)) * (D_MODEL ** -0.5),
        "ple_gate_b": 0.02 * nrm(ks[16], (DEPTH, D_MODEL)),
        "ple_proj": nrm(ks[17], (DEPTH, PLE_DIM, D_MODEL)) * (PLE_DIM ** -0.5),
    }


def reference(x, p, ln_gain, ln_bias, pool_w, pool_b, pool_scale, gdn_w_in, gdn_conv,
              gdn_a_log, gdn_dt_bias, gdn_norm_w, gdn_w_out, mlp_w1, mlp_w2,
              ple_gate_w, ple_gate_b, ple_proj):
    for i in range(DEPTH):
        j = i // N_MIXERS
        if i % N_MIXERS == 0:
            mix = pool_mixer(x, pool_w[j], pool_b[j], pool_scale[j])
        else:
            mix = gated_deltanet(x, gdn_w_in[j], gdn_conv[j], gdn_a_log[j], gdn_dt_bias[j],
                                 gdn_norm_w[j], gdn_w_out[j])
        x = layer_norm(DEEPNORM_ALPHA * x + mix, ln_gain[i, 0], ln_bias[i, 0])
        ff = squared_relu_mlp(x, mlp_w1[i], mlp_w2[i])
        pe = jax.nn.sigmoid(x @ ple_gate_w[i] + ple_gate_b[i]) * (p[i] @ ple_proj[i])
        x = layer_norm(DEEPNORM_ALPHA * x + ff + pe, ln_gain[i, 1], ln_bias[i, 1])
    return x
```

```python
import numpy as np
from contextlib import ExitStack
import concourse.bass as bass
import concourse.mybir as mybir
from concourse.bass_utils import run_bass_kernel_spmd

F32 = mybir.dt.float32
BF16 = mybir.dt.bfloat16
I32 = mybir.dt.int32
AF = mybir.ActivationFunctionType
ALU = mybir.AluOpType

NCORES = 8
D = 1024
TOK = 2048
NT = 16
SEQ = 8192
DFF = 4096
ALPHA = (2.0 * 2) ** 0.25
LN_EPS = 1e-5
RMS_EPS = 1e-6
L2_EPS = 1e-6
DH = 128
FUSED = True


class R:
    __slots__ = ("name", "w", "r", "excl")

    def __init__(self, name, excl=False):
        self.name = name
        self.w = None
        self.r = {}
        self.excl = excl


class Sched:
    def __init__(self, nc, es, nd=24):
        self.nc = nc
        self._es = es
        self.eng = {"pe": nc.tensor, "act": nc.scalar, "dve": nc.vector, "pool": nc.gpsimd, "sp": nc.sync}
        self.sem = {k: es.enter_context(nc.semaphore("s_" + k)) for k in self.eng}
        self.cnt = {k: 0 for k in self.eng}
        self.ND = nd
        self.dsem = [es.enter_context(nc.semaphore(f"dq{i}")) for i in range(nd)]
        self.dcnt = [0] * nd
        self.dnext = 0
        self.waited = {}

    def _semof(self, src):
        if src == "cc":
            return self.csem
        return self.sem[src] if src in self.sem else self.dsem[int(src[1:])]

    def _wait(self, eng, deps):
        for src, val in deps.items():
            if src == eng and eng == "pe":
                continue
            if self.waited.get((eng, src), 0) < val:
                self.eng[eng].wait_ge(self._semof(src), val)
                self.waited[(eng, src)] = val

    @staticmethod
    def _deps(reads, writes, eng=None):
        deps = {}
        for r in reads:
            if r.w is not None:
                s, v = r.w
                if deps.get(s, 0) < v:
                    deps[s] = v
        for w in writes:
            if w.w is not None:
                s, v = w.w
                if deps.get(s, 0) < v:
                    deps[s] = v
            for s, v in w.r.items():
                if deps.get(s, 0) < v:
                    deps[s] = v
        return deps

    def op(self, eng, fn, reads=(), writes=()):
        ex = [r for r in reads if r.excl]
        if ex:
            writes = list(writes) + ex
        self._wait(eng, self._deps(reads, writes, eng))
        ins = fn(self.eng[eng])
        self.cnt[eng] += 1
        v = self.cnt[eng]
        ins.then_inc(self.sem[eng], 1)
        for r in reads:
            r.r[eng] = v
        for w in writes:
            w.w = (eng, v)
            w.r = {}
        return ins

    def dma(self, q, out, in_, reads=(), writes=(), fn=None):
        k = self.dnext
        self.dnext = (k + 1) % self.ND
        deps = self._deps(reads, writes)
        src = f"d{k}"
        if self.dcnt[k] > 0:
            deps[src] = max(deps.get(src, 0), 16 * self.dcnt[k])
        self._wait(q, deps)
        if fn is None:
            ins = self.eng[q].dma_start(out=out, in_=in_)
        else:
            ins = fn(self.eng[q])
        self.dcnt[k] += 1
        v = 16 * self.dcnt[k]
        ins.then_inc(self.dsem[k], 16)
        for r in reads:
            r.r[src] = v
        for w in writes:
            w.w = (src, v)
            w.r = {}
        return ins

    def collective(self, fn, reads=(), writes=()):
        if not hasattr(self, "csem"):
            self.csem = self._es.enter_context(self.nc.semaphore("s_cc"))
            self.ccnt = 0
        self._wait("pool", self._deps(reads, writes, "pool"))
        ins = fn(self.eng["pool"])
        self.ccnt += 1
        ins.then_inc(self.csem)
        for r in reads:
            r.r["cc"] = self.ccnt
        for w in writes:
            w.w = ("cc", self.ccnt)
            w.r = {}
        return ins

    def barrier(self):
        tgt = {k: v for k, v in self.cnt.items() if v > 0}
        for k in range(self.ND):
            if self.dcnt[k] > 0:
                tgt[f"d{k}"] = 16 * self.dcnt[k]
        if getattr(self, "ccnt", 0) > 0:
            tgt["cc"] = self.ccnt
        for e in self.eng:
            self._wait(e, tgt)

    def finish(self):
        tgt = {k: v for k, v in self.cnt.items() if v > 0}
        for k in range(self.ND):
            if self.dcnt[k] > 0:
                tgt[f"d{k}"] = 16 * self.dcnt[k]
        self._wait("sp", tgt)


class Ring:
    def __init__(self, es, nc, name, shape, dtype, n, psum=False):
        self.t = []
        for i in range(n):
            if psum:
                t = es.enter_context(nc.psum_tensor(f"{name}{i}", shape, dtype))
            else:
                t = es.enter_context(nc.sbuf_tensor(f"{name}{i}", shape, dtype))
            self.t.append((t, R(f"{name}{i}", excl=psum)))
        self.i = 0

    def next(self):
        x = self.t[self.i]
        self.i = (self.i + 1) % len(self.t)
        return x


def bcast_ap(handle, off, n):
    return bass.AP(handle, off, [[0, 128], [1, n]])


class Ctx:
    pass


def emit_layernorm(C, i, gain, bias):
    S = C.S
    a = C.acc[:, i, :]
    aR = C.accR[i]
    st, stR = C.small.next()
    S.op("dve", lambda e: e.bn_stats(st[:, 0:6], C.acc[:, i, 0:512]), reads=[aR], writes=[stR])
    S.op("dve", lambda e: e.bn_stats(st[:, 6:12], C.acc[:, i, 512:1024]), reads=[aR], writes=[stR])
    S.op("dve", lambda e: e.bn_aggr(st[:, 12:14], st[:, 0:12]), reads=[stR], writes=[stR])
    S.op("act", lambda e: e.activation(out=st[:, 14:15], in_=st[:, 13:14], func=AF.Sqrt, bias=C.eps[:, 0:1]),
         reads=[stR, C.constR], writes=[stR])
    S.op("dve", lambda e: e.reciprocal(st[:, 14:15], st[:, 14:15]), reads=[stR], writes=[stR])
    S.op("dve", lambda e: e.scalar_tensor_tensor(st[:, 15:16], st[:, 12:13], -1.0, st[:, 14:15], ALU.mult, ALU.mult),
         reads=[stR], writes=[stR])
    S.op("act", lambda e: e.activation(out=a, in_=a, func=AF.Identity, bias=st[:, 15:16], scale=st[:, 14:15]),
         reads=[stR, aR], writes=[aR])
    S.op("dve", lambda e: e.tensor_tensor(a, a, gain[0][:], ALU.mult), reads=[aR, gain[1]], writes=[aR])
    S.op("dve", lambda e: e.tensor_tensor(a, a, bias[0][:], ALU.add), reads=[aR, bias[1]], writes=[aR])


def emit_transpose_to_xT(C, i, scale_after=None):
    S = C.S
    for half in range(2):
        ps, psR = C.psum.next()
        for q in range(4):
            fc = half * 4 + q
            S.op("pe", lambda e: e.transpose(ps[:, q * 128:(q + 1) * 128], C.acc[:, i, fc * 128:(fc + 1) * 128],
                                             C.ident[:]),
                 reads=[C.accR[i], C.constR], writes=[psR])
        S.op("act", lambda e: e.activation(
            out=C.xT[:, half * 4:(half + 1) * 4, i * 128:(i + 1) * 128],
            in_=ps[:, 0:512].rearrange("p (q t) -> p q t", q=4), func=AF.Copy),
            reads=[psR], writes=[C.xTR[i]])
    if scale_after is not None:
        S.op("dve", lambda e: e.tensor_scalar(C.acc[:, i, :], C.acc[:, i, :], float(scale_after), None, ALU.mult),
             reads=[C.accR[i]], writes=[C.accR[i]])


def alloc_tok(C, es, tag):
    C.xT = es.enter_context(C.nc.sbuf_tensor(f"xT{tag}", [128, 8, TOK], BF16))
    C.xTR = [R(f"xT{i}") for i in range(NT)]
    C.bc = Ring(es, C.nc, f"bc{tag}", [128, D], F32, 5)


def load_bcast(C, row):
    t, r = C.bc.next()
    C.S.dma("sp", t[:], bcast_ap(C.vecs_h, row * D, D), writes=[r])
    return (t, r)


def emit_ple(C, es, layer, p_d, wg_d, wp_d, bg_row):
    S, nc = C.S, C.nc
    wg = es.enter_context(nc.sbuf_tensor(f"wg{layer}", [128, 8, D], BF16))
    wgR = R("wg")
    wp = es.enter_context(nc.sbuf_tensor(f"wp{layer}", [128, 2, D], BF16))
    wpR = R("wp")
    pring = Ring(es, nc, f"pt{layer}", [128, 4, 256], F32, 2)
    pTring = Ring(es, nc, f"pT{layer}", [128, 2, 128], BF16, 2)
    for kc2 in range(2):
        S.dma("pool", wg[:, kc2 * 4:(kc2 + 1) * 4, :],
              wg_d[kc2 * 512:(kc2 + 1) * 512, :].rearrange("(k p) c -> p k c", p=128), writes=[wgR])
    S.dma("pool", wp[:], wp_d.rearrange("(k p) c -> p k c", p=128), writes=[wpR])
    bg = load_bcast(C, bg_row)
    pt = None
    for i in range(NT):
        if i % 4 == 0:
            pt = pring.next()
            S.dma("sp", pt[0][:], p_d[i * 128:(i + 4) * 128, :].rearrange("(j t) f -> t j f", t=128), writes=[pt[1]])
        ps, psR = C.psum.next()
        for k in range(2):
            S.op("pe", lambda e: e.transpose(ps[:, k * 128:(k + 1) * 128], pt[0][:, i % 4, k * 128:(k + 1) * 128],
                                             C.ident[:]), reads=[pt[1], C.constR], writes=[psR])
        pT, pTR = pTring.next()
        S.op("act", lambda e: e.activation(out=pT[:], in_=ps[:, 0:256].rearrange("p (k t) -> p k t", k=2),
                                           func=AF.Copy), reads=[psR], writes=[pTR])
        for half in range(2):
            cs = slice(half * 512, (half + 1) * 512)
            pg, pgR = C.psum.next()
            for kc in range(8):
                S.op("pe", lambda e: e.matmul(pg[:, 0:512], C.xT[:, kc, i * 128:(i + 1) * 128], wg[:, kc, cs],
                                              start=(kc == 0), stop=(kc == 7)),
                     reads=[C.xTR[i], wgR], writes=[pgR])
            pp, ppR = C.psum.next()
            for k in range(2):
                S.op("pe", lambda e: e.matmul(pp[:, 0:512], pT[:, k, :], wp[:, k, cs], start=(k == 0), stop=(k == 1)),
                     reads=[pTR, wpR], writes=[ppR])
            tm, tmR = C.tmp.next()
            S.op("dve", lambda e: e.tensor_tensor(tm[:, 0:512], pg[:, 0:512], bg[0][:, cs], ALU.add),
                 reads=[pgR, bg[1]], writes=[tmR])
            S.op("act", lambda e: e.activation(out=tm[:, 0:512], in_=tm[:, 0:512], func=AF.Sigmoid),
                 reads=[tmR], writes=[tmR])
            S.op("dve", lambda e: e.tensor_tensor(tm[:, 0:512], tm[:, 0:512], pp[:, 0:512], ALU.mult),
                 reads=[tmR, ppR], writes=[tmR])
            S.op("dve", lambda e: e.tensor_tensor(C.acc[:, i, cs], C.acc[:, i, cs], tm[:, 0:512], ALU.add),
                 reads=[tmR, C.accR[i]], writes=[C.accR[i]])


def emit_mlp(C, es, layer, w1_d, w2_d):
    S, nc = C.S, C.nc
    w1ring = Ring(es, nc, f"w1g{layer}", [128, 8, 512], BF16, 2)
    w2ring = Ring(es, nc, f"w2g{layer}", [128, 4, D], BF16, 2)
    hring = Ring(es, nc, f"hT{layer}", [128, 4, TOK], BF16, 2)
    NG = DFF // 512

    def load(gi):
        w1g = w1ring.next()
        w2g = w2ring.next()
        for kc2 in range(2):
            S.dma("pool", w1g[0][:, kc2 * 4:(kc2 + 1) * 4, :],
                  w1_d[kc2 * 512:(kc2 + 1) * 512, gi * 512:(gi + 1) * 512].rearrange("(k p) c -> p k c", p=128),
                  writes=[w1g[1]])
        S.dma("pool", w2g[0][:], w2_d[gi * 512:(gi + 1) * 512, :].rearrange("(k p) c -> p k c", p=128),
              writes=[w2g[1]])
        return w1g, w2g

    nxt = load(0)
    for gi in range(NG):
        w1g, w2g = nxt
        if gi + 1 < NG:
            nxt = load(gi + 1)
        hT, hTR = hring.next()
        for tb in range(4):
            ts_ = slice(tb * 512, (tb + 1) * 512)
            for hcl in range(4):
                ps, psR = C.psum.next()
                for kc in range(8):
                    S.op("pe", lambda e: e.matmul(ps[:, 0:512], w1g[0][:, kc, hcl * 128:(hcl + 1) * 128],
                                                  C.xT[:, kc, ts_], start=(kc == 0), stop=(kc == 7)),
                         reads=[w1g[1]] + C.xTR[tb * 4:(tb + 1) * 4], writes=[psR])
                tm, tmR = C.tmp.next()
                S.op("act", lambda e: e.activation(out=tm[:, 0:512], in_=ps[:, 0:512], func=AF.Relu),
                     reads=[psR], writes=[tmR])
                S.op("dve", lambda e: e.tensor_tensor(hT[:, hcl, ts_], tm[:, 0:512], tm[:, 0:512], ALU.mult),
                     reads=[tmR], writes=[hTR])
        for i in range(NT):
            for half in range(2):
                cs = slice(half * 512, (half + 1) * 512)
                ps, psR = C.psum.next()
                for hcl in range(4):
                    S.op("pe", lambda e: e.matmul(ps[:, 0:512], hT[:, hcl, i * 128:(i + 1) * 128], w2g[0][:, hcl, cs],
                                                  start=(hcl == 0), stop=(hcl == 3)),
                         reads=[hTR, w2g[1]], writes=[psR])
                S.op("dve", lambda e: e.tensor_tensor(C.acc[:, i, cs], C.acc[:, i, cs], ps[:, 0:512], ALU.add),
                     reads=[psR, C.accR[i]], writes=[C.accR[i]])


def setup_common(nc, es, S, vecs_h, cmat_d, tok=True):
    C = Ctx()
    C.nc, C.S = nc, S
    C.vecs_h = vecs_h
    if tok:
        C.acc = es.enter_context(nc.sbuf_tensor("acc", [128, NT, D], F32))
        C.accR = [R(f"acc{i}") for i in range(NT)]
    C.cmat = es.enter_context(nc.sbuf_tensor("cmat_sb", [128, 4, 128], F32))
    C.constR = R("const")
    S.dma("sp", C.cmat[:], cmat_d, writes=[C.constR])
    C.eps = es.enter_context(nc.sbuf_tensor("eps_sb", [128, 4], F32))
    S.op("dve", lambda e: e.memset(C.eps[:, 0:1], LN_EPS), writes=[C.constR])
    S.op("dve", lambda e: e.memset(C.eps[:, 1:2], L2_EPS), writes=[C.constR])
    S.op("dve", lambda e: e.memset(C.eps[:, 2:3], RMS_EPS), writes=[C.constR])
    C.ident = C.cmat[:, 0, :]
    C.ones = C.cmat[:, 1, :]
    C.U = C.cmat[:, 2, :]
    C.L = C.cmat[:, 3, :]
    C.psum = Ring(es, nc, "ps", [128, 512], F32, 6, psum=True)
    C.tmp = Ring(es, nc, "tmp", [128, 512], F32, 3)
    C.small = Ring(es, nc, "sm", [128, 16], F32, 4)
    return C


def emit_phase1(C, dd, x1T_dst):
    with ExitStack() as esx:
        alloc_tok(C, esx, "a")
        _emit_phase1(C, dd, x1T_dst)


def _emit_phase1(C, dd, x1T_dst):
    S, nc = C.S, C.nc
    with ExitStack() as es:
        xh = es.enter_context(nc.sbuf_tensor("xh", [128, D], F32))
        xhR = R("xh")
        band = es.enter_context(nc.sbuf_tensor("band_sb", [128, 4, 3, 128], F32))
        bandR = R("band")
        pw = es.enter_context(nc.sbuf_tensor("poolw", [128, 8, 256], BF16))
        pwR = R("pw")
        S.dma("sp", band[:], dd["band"], writes=[bandR])
        S.dma("pool", pw[:], dd["pool_w"].rearrange("g (k p) o -> p (g k) o", p=128), writes=[pwR])
        S.dma("sp", xh[:], dd["x"][0:128, :], writes=[xhR])
        for i in range(NT):
            S.dma("sp", C.acc[:, i, :], dd["x"][128 * (i + 1):128 * (i + 2), :], writes=[C.accR[i]])
        for i in range(NT):
            for gh in range(2):
                ps, psR = C.psum.next()
                for gq in range(4):
                    g8 = gh * 4 + gq
                    g, cc = g8 // 2, g8 % 2
                    c0 = 256 * g + 128 * cc
                    if i == 0:
                        prev_ap, prevR = xh[:, c0:c0 + 128], xhR
                    else:
                        prev_ap, prevR = C.acc[:, i - 1, c0:c0 + 128], C.accR[i - 1]
                    o = ps[:, gq * 128:(gq + 1) * 128]
                    S.op("pe", lambda e: e.matmul(o, prev_ap, band[:, g, 0, :], start=True, stop=False),
                         reads=[prevR, bandR], writes=[psR])
                    S.op("pe", lambda e: e.matmul(o, C.acc[:, i, c0:c0 + 128], band[:, g, 1 if i == 0 else 2, :],
                                                  start=False, stop=True),
                         reads=[C.accR[i], bandR], writes=[psR])
                S.op("act", lambda e: e.activation(
                    out=C.xT[:, gh * 4:(gh + 1) * 4, i * 128:(i + 1) * 128],
                    in_=ps[:, 0:512].rearrange("p (q t) -> p q t", q=4), func=AF.Copy),
                    reads=[psR], writes=[C.xTR[i]])
        pb = load_bcast(C, 8)
        psc = load_bcast(C, 9)
        g0 = load_bcast(C, 0)
        b0 = load_bcast(C, 4)
        for i in range(NT):
            for half in range(2):
                ps, psR = C.psum.next()
                for gq in range(2):
                    g = half * 2 + gq
                    for cc in range(2):
                        S.op("pe", lambda e: e.matmul(ps[:, gq * 256:(gq + 1) * 256],
                                                      C.xT[:, 2 * g + cc, i * 128:(i + 1) * 128], pw[:, 2 * g + cc, :],
                                                      start=(cc == 0), stop=(cc == 1)),
                             reads=[C.xTR[i], pwR], writes=[psR])
                cs = slice(half * 512, (half + 1) * 512)
                tm, tmR = C.tmp.next()
                S.op("dve", lambda e: e.tensor_tensor(tm[:, 0:512], ps[:, 0:512], pb[0][:, cs], ALU.add),
                     reads=[psR, pb[1]], writes=[tmR])
                S.op("dve", lambda e: e.tensor_tensor(tm[:, 0:512], tm[:, 0:512], psc[0][:, cs], ALU.mult),
                     reads=[tmR, psc[1]], writes=[tmR])
                S.op("dve", lambda e: e.scalar_tensor_tensor(C.acc[:, i, cs], C.acc[:, i, cs], ALPHA, tm[:, 0:512],
                                                             ALU.mult, ALU.add),
                     reads=[tmR, C.accR[i]], writes=[C.accR[i]])
            emit_layernorm(C, i, g0, b0)
            emit_transpose_to_xT(C, i, scale_after=ALPHA)
        emit_ple(C, es, 0, dd["p0"], dd["ple_gate_w0"], dd["ple_proj0"], 10)
        S.barrier()
    with ExitStack() as es:
        emit_mlp(C, es, 0, dd["mlp_w10"], dd["mlp_w20"])
        g1 = load_bcast(C, 1)
        b1 = load_bcast(C, 5)
        for i in range(NT):
            emit_layernorm(C, i, g1, b1)
            emit_transpose_to_xT(C, i, scale_after=ALPHA)
        for fc in range(8):
            S.dma("sp", x1T_dst[fc], C.xT[:, fc, :], reads=C.xTR)
        S.barrier()


def emit_phase3(C, dd, load_og, out_d):
    with ExitStack() as esx:
        alloc_tok(C, esx, "c")
        _emit_phase3(C, dd, load_og, out_d)


def _emit_phase3(C, dd, load_og, out_d):
    S, nc = C.S, C.nc
    with ExitStack() as es:
        wo = es.enter_context(nc.sbuf_tensor("wo", [128, 8, D], BF16))
        woR = R("wo")
        for kc2 in range(2):
            S.dma("pool", wo[:, kc2 * 4:(kc2 + 1) * 4, :],
                  dd["gdn_w_out"][kc2 * 512:(kc2 + 1) * 512, :].rearrange("(k p) c -> p k c", p=128), writes=[woR])
        load_og(C)
        g0 = load_bcast(C, 2)
        b0 = load_bcast(C, 6)
        for i in range(NT):
            for half in range(2):
                cs = slice(half * 512, (half + 1) * 512)
                ps, psR = C.psum.next()
                for kc in range(8):
                    S.op("pe", lambda e: e.matmul(ps[:, 0:512], C.xT[:, kc, i * 128:(i + 1) * 128], wo[:, kc, cs],
                                                  start=(kc == 0), stop=(kc == 7)),
                         reads=[C.xTR[i], woR], writes=[psR])
                S.op("dve", lambda e: e.tensor_tensor(C.acc[:, i, cs], C.acc[:, i, cs], ps[:, 0:512], ALU.add),
                     reads=[psR, C.accR[i]], writes=[C.accR[i]])
            emit_layernorm(C, i, g0, b0)
            emit_transpose_to_xT(C, i, scale_after=ALPHA)
        emit_ple(C, es, 1, dd["p1"], dd["ple_gate_w1"], dd["ple_proj1"], 11)
        S.barrier()
    with ExitStack() as es:
        emit_mlp(C, es, 1, dd["mlp_w11"], dd["mlp_w21"])
        g1 = load_bcast(C, 3)
        b1 = load_bcast(C, 7)
        for i in range(NT):
            emit_layernorm(C, i, g1, b1)
            S.dma("sp", out_d[i * 128:(i + 1) * 128, :], C.acc[:, i, :], reads=[C.accR[i]])
        S.barrier()


def emit_phase2(C, dd, xg, og_dst):
    S, nc = C.S, C.nc
    with ExitStack() as es:
        def sb(name, shape, dt=F32):
            return es.enter_context(nc.sbuf_tensor(name, shape, dt))

        win = sb("win_sb", [128, 8, 514], BF16)
        winR = R("win")
        S.dma("pool", win[:], dd["win"].rearrange("(k p) c -> p k c", p=128), writes=[winR])
        hp = sb("hp_sb", [128, 16])
        hpR = R("hp")
        S.dma("sp", hp[:], dd["hp"], writes=[hpR])
        S.op("act", lambda e: e.activation(out=hp[:, 15:16], in_=hp[:, 12:13], func=AF.Exp), reads=[hpR], writes=[hpR])
        S.op("dve", lambda e: e.tensor_scalar(hp[:, 15:16], hp[:, 15:16], -1.0, None, ALU.mult),
             reads=[hpR], writes=[hpR])
        mats = Ring(es, nc, "m", [128, 128], F32, 40)
        keep = Ring(es, nc, "kp", [128, 128], F32, 24)
        opsum = Ring(es, nc, "po", [128, 512], F32, 2, psum=True)
        B = []
        for b in range(2):
            bb = Ctx()
            bb.xblk = Ring(es, nc, f"xb{b}", [128, 8, 512], BF16, 2)
            bb.pre = [(sb(f"pre{b}{c}", [128, 515]), R("pre")) for c in range(3)]
            bb.xs = [(sb(f"xs{b}{c}", [128, 512]), R("xs")) for c in range(3)]
            bb.qn = (sb(f"qn{b}", [128, 512]), R("qn"))
            bb.kn = (sb(f"kn{b}", [128, 512]), R("kn"))
            bb.zs = (sb(f"zs{b}", [128, 512]), R("zs"))
            bb.cv = (sb(f"cv{b}", [128, 512]), R("cv"))
            bb.gt = (sb(f"gt{b}", [128, 64]), R("gt"))
            bb.S = (sb(f"S{b}", [128, 128]), R("S"))
            bb.og = Ring(es, nc, f"og{b}", [128, 512], BF16, 2)
            bb.ot = (sb(f"ot{b}", [128, 512]), R("ot"))
            S.op("dve", lambda e: e.memset(bb.S[0][:], 0.0), writes=[bb.S[1]])
            for c in range(3):
                S.op("dve", lambda e: e.memset(bb.pre[c][0][:, 0:3], 0.0), writes=[bb.pre[c][1]])
            B.append(bb)

        import os
        SUB = int(os.environ.get("P2_SUB", 9))
        PSUB = int(os.environ.get("P2_PSUB", 9))

        def block_front(b, k):
            bb = B[b]
            rank = 4 * b + k // 4
            off = (k % 4) * 512
            xb, xbR = bb.xblk.next()
            S.dma("sp", xb[:], xg[rank, :, :, off:off + 512].rearrange("f p t -> p f t"),
                  reads=([C.dramR["xg"]] if hasattr(C, "dramR") else []), writes=[xbR])
            for c in range(4):
                ps, psR = C.psum.next()
                for kc in range(8):
                    S.op("pe", lambda e: e.matmul(ps[:, 0:512], win[:, kc, c * 128:(c + 1) * 128], xb[:, kc, :],
                                                  start=(kc == 0), stop=(kc == 7)), reads=[winR, xbR], writes=[psR])
                if c < 3:
                    S.op("act", lambda e: e.activation(out=bb.pre[c][0][:, 3:515], in_=ps[:, 0:512], func=AF.Copy),
                         reads=[psR], writes=[bb.pre[c][1]])
                else:
                    S.op("act", lambda e: e.activation(out=bb.zs[0][:], in_=ps[:, 0:512], func=AF.Silu),
                         reads=[psR], writes=[bb.zs[1]])
            if SUB < 1:
                return
            pg, pgR = C.psum.next()
            for j in range(4):
                for kc in range(8):
                    S.op("pe", lambda e: e.matmul(pg[:, 2 * j:2 * j + 2], xb[:, kc, j * 128:(j + 1) * 128],
                                                  win[:, kc, 512:514], start=(kc == 0), stop=(kc == 7)),
                         reads=[winR, xbR], writes=[pgR])
            if SUB < 2:
                return
            gt, gtR = bb.gt
            S.op("act", lambda e: e.activation(out=gt[:, 0:4], in_=pg[:, 0:8].rearrange("p (j c) -> p j c", c=2)[:, :, 0],
                                               func=AF.Sigmoid), reads=[pgR], writes=[gtR])
            S.op("act", lambda e: e.activation(out=gt[:, 24:28],
                                               in_=pg[:, 0:8].rearrange("p (j c) -> p j c", c=2)[:, :, 1],
                                               func=AF.Exp, bias=hp[:, 13:14]), reads=[pgR, hpR], writes=[gtR])
            S.op("act", lambda e: e.activation(out=gt[:, 24:28], in_=gt[:, 24:28], func=AF.Ln, bias=1.0),
                 reads=[gtR], writes=[gtR])
            S.op("dve", lambda e: e.tensor_scalar(gt[:, 4:8], gt[:, 24:28], hp[:, 15:16], None, ALU.mult),
                 reads=[gtR, hpR], writes=[gtR])
            if SUB < 3:
                return
            pc, pcR = C.psum.next()
            S.op("pe", lambda e: e.matmul(pc[:, 0:4], C.U, gt[:, 4:8], start=True, stop=True),
                 reads=[gtR, C.constR], writes=[pcR])
            S.op("pe", lambda e: e.matmul(pc[:, 4:8], C.ones, gt[:, 4:8], start=True, stop=True),
                 reads=[gtR, C.constR], writes=[pcR])
            S.op("dve", lambda e: e.tensor_copy(gt[:, 8:12], pc[:, 0:4]), reads=[pcR], writes=[gtR])
            S.op("act", lambda e: e.activation(out=gt[:, 12:16], in_=pc[:, 4:8], func=AF.Exp), reads=[pcR], writes=[gtR])
            S.op("dve", lambda e: e.tensor_tensor(gt[:, 24:28], pc[:, 4:8], gt[:, 8:12], ALU.subtract),
                 reads=[pcR, gtR], writes=[gtR])
            S.op("act", lambda e: e.activation(out=gt[:, 16:20], in_=gt[:, 24:28], func=AF.Exp), reads=[gtR], writes=[gtR])
            S.op("act", lambda e: e.activation(out=gt[:, 20:24], in_=gt[:, 8:12], func=AF.Exp), reads=[gtR], writes=[gtR])
            S.op("dve", lambda e: e.tensor_tensor(gt[:, 20:24], gt[:, 20:24], gt[:, 0:4], ALU.mult),
                 reads=[gtR], writes=[gtR])
            if SUB < 4:
                return
            cv, cvR = bb.cv
            for c in range(3):
                pre, preR = bb.pre[c]
                S.op("dve", lambda e: e.tensor_scalar(cv[:], pre[:, 0:512], hp[:, 4 * c:4 * c + 1], None, ALU.mult),
                     reads=[preR, hpR], writes=[cvR])
                for j in range(1, 4):
                    S.op("dve", lambda e: e.scalar_tensor_tensor(cv[:], pre[:, j:j + 512], hp[:, 4 * c + j:4 * c + j + 1],
                                                                 cv[:], ALU.mult, ALU.add),
                         reads=[preR, hpR, cvR], writes=[cvR])
                S.op("act", lambda e: e.activation(out=bb.xs[c][0][:], in_=cv[:], func=AF.Silu),
                     reads=[cvR], writes=[bb.xs[c][1]])
                S.op("dve", lambda e: e.tensor_copy(pre[:, 0:3], pre[:, 512:515]), reads=[preR], writes=[preR])
            if SUB < 5:
                return
            for c, dst in ((0, bb.qn), (1, bb.kn)):
                S.op("act", lambda e: e.activation(out=cv[:], in_=bb.xs[c][0][:], func=AF.Square),
                     reads=[bb.xs[c][1]], writes=[cvR])
                pn, pnR = C.psum.next()
                for q4 in range(4):
                    S.op("pe", lambda e: e.matmul(pn[:, q4 * 128:(q4 + 1) * 128], C.ones, cv[:, q4 * 128:(q4 + 1) * 128],
                                                  start=True, stop=True), reads=[cvR, C.constR], writes=[pnR])
                S.op("act", lambda e: e.activation(out=cv[:], in_=pn[:, 0:512], func=AF.Sqrt, bias=C.eps[:, 1:2]),
                     reads=[pnR, C.constR], writes=[cvR])
                S.op("dve", lambda e: e.reciprocal(cv[:], cv[:]), reads=[cvR], writes=[cvR])
                if c == 0:
                    S.op("dve", lambda e: e.scalar_tensor_tensor(dst[0][:], bb.xs[c][0][:], DH ** -0.5, cv[:],
                                                                 ALU.mult, ALU.mult),
                         reads=[bb.xs[c][1], cvR], writes=[dst[1]])
                else:
                    S.op("dve", lambda e: e.tensor_tensor(dst[0][:], bb.xs[c][0][:], cv[:], ALU.mult),
                         reads=[bb.xs[c][1], cvR], writes=[dst[1]])

        def tile_prep(b, j):
            bb = B[b]
            gt, gtR = bb.gt
            cs = slice(j * 128, (j + 1) * 128)
            kn, knR = bb.kn
            qn, qnR = bb.qn
            vs, vsR = bb.xs[2]
            T = Ctx()
            pt, ptR = C.psum.next()
            S.op("pe", lambda e: e.transpose(pt[:, 0:128], kn[:, cs], C.ident), reads=[knR, C.constR], writes=[ptR])
            S.op("pe", lambda e: e.transpose(pt[:, 128:256], vs[:, cs], C.ident), reads=[vsR, C.constR], writes=[ptR])
            T.kb = mats.next()
            T.kd = keep.next()
            T.vb = mats.next()
            S.op("act", lambda e: e.activation(out=T.kb[0][:], in_=pt[:, 0:128], func=AF.Identity,
                                               scale=gt[:, 20 + j:21 + j]), reads=[ptR, gtR], writes=[T.kb[1]])
            S.op("dve", lambda e: e.tensor_scalar(T.kd[0][:], pt[:, 0:128], gt[:, 16 + j:17 + j], None, ALU.mult),
                 reads=[ptR, gtR], writes=[T.kd[1]])
            S.op("dve", lambda e: e.tensor_scalar(T.vb[0][:], pt[:, 128:256], gt[:, j:j + 1], None, ALU.mult),
                 reads=[ptR, gtR], writes=[T.vb[1]])
            if PSUB < 1:
                return T
            pG, pGR = C.psum.next()
            S.op("pe", lambda e: e.matmul(pG[:, 0:128], kn[:, cs], kn[:, cs], start=True, stop=True),
                 reads=[knR], writes=[pGR])
            S.op("pe", lambda e: e.matmul(pG[:, 128:256], kn[:, cs], qn[:, cs], start=True, stop=True),
                 reads=[knR, qnR], writes=[pGR])
            if PSUB < 2:
                return T
            gU = mats.next()
            S.op("dve", lambda e: e.tensor_scalar(gU[0][:], C.U, gt[:, 4 + j:5 + j], None, ALU.mult),
                 reads=[gtR, C.constR], writes=[gU[1]])
            pR, pRR = C.psum.next()
            S.op("pe", lambda e: e.matmul(pR[:, 0:128], C.ones, gU[0][:], start=True, stop=True),
                 reads=[gU[1], C.constR], writes=[pRR])
            if PSUB < 3:
                return T
            t1 = mats.next()
            Dm = mats.next()
            t2 = mats.next()
            DmT = mats.next()
            E = mats.next()
            gc = gt[:, 8 + j:9 + j]
            S.op("dve", lambda e: e.tensor_scalar(t1[0][:], pR[:, 0:128], gc, 0.0, ALU.subtract, ALU.max),
                 reads=[pRR, gtR], writes=[t1[1]])
            S.op("act", lambda e: e.activation(out=Dm[0][:], in_=t1[0][:], func=AF.Exp, scale=-1.0),
                 reads=[t1[1]], writes=[Dm[1]])
            S.op("dve", lambda e: e.tensor_scalar(t2[0][:], pR[:, 0:128], gc, 0.0, ALU.subtract, ALU.min),
                 reads=[pRR, gtR], writes=[t2[1]])
            S.op("act", lambda e: e.activation(out=DmT[0][:], in_=t2[0][:], func=AF.Exp), reads=[t2[1]], writes=[DmT[1]])
            S.op("act", lambda e: e.activation(out=E[0][:], in_=pR[:, 0:128], func=AF.Exp), reads=[pRR], writes=[E[1]])
            if PSUB < 4:
                return T
            T.qd = keep.next()
            S.op("dve", lambda e: e.tensor_tensor(T.qd[0][:], qn[:, cs], E[0][:], ALU.mult),
                 reads=[qnR, E[1]], writes=[T.qd[1]])
            A = mats.next()
            S.op("dve", lambda e: e.tensor_tensor(t1[0][:], pG[:, 0:128], Dm[0][:], ALU.mult),
                 reads=[pGR, Dm[1]], writes=[t1[1]])
            S.op("dve", lambda e: e.scalar_tensor_tensor(A[0][:], t1[0][:], gt[:, j:j + 1], C.L, ALU.mult, ALU.mult),
                 reads=[t1[1], gtR, C.constR], writes=[A[1]])
            T.qk = keep.next()
            S.op("dve", lambda e: e.tensor_tensor(t2[0][:], pG[:, 128:256], DmT[0][:], ALU.mult),
                 reads=[pGR, DmT[1]], writes=[t2[1]])
            S.op("dve", lambda e: e.tensor_tensor(T.qk[0][:], t2[0][:], C.U, ALU.mult),
                 reads=[t2[1], C.constR], writes=[T.qk[1]])
            if PSUB < 5:
                return T
            pB, pBR = C.psum.next()
            S.op("pe", lambda e: e.transpose(pB[:, 0:128], A[0][:], C.ident), reads=[A[1], C.constR], writes=[pBR])
            Bm = mats.next()
            Y = mats.next()
            S.op("act", lambda e: e.activation(out=Bm[0][:], in_=pB[:, 0:128], func=AF.Copy), reads=[pBR], writes=[Bm[1]])
            S.op("dve", lambda e: e.tensor_tensor(Y[0][:], C.ident, pB[:, 0:128], ALU.subtract),
                 reads=[pBR, C.constR], writes=[Y[1]])
            if PSUB < 6:
                return T
            Ak, Bk = A, Bm
            for lvl in range(1, 7):
                pA, pAR = C.psum.next()
                S.op("pe", lambda e: e.matmul(pA[:, 0:128], Bk[0][:], Ak[0][:], start=True, stop=True),
                     reads=[Ak[1], Bk[1]], writes=[pAR])
                if lvl < 6:
                    S.op("pe", lambda e: e.matmul(pA[:, 128:256], Ak[0][:], Bk[0][:], start=True, stop=True),
                         reads=[Ak[1], Bk[1]], writes=[pAR])
                An = mats.next()
                S.op("act", lambda e: e.activation(out=An[0][:], in_=pA[:, 0:128], func=AF.Copy),
                     reads=[pAR], writes=[An[1]])
                if lvl < 6:
                    Bn = mats.next()
                    S.op("act", lambda e: e.activation(out=Bn[0][:], in_=pA[:, 128:256], func=AF.Copy),
                         reads=[pAR], writes=[Bn[1]])
                pY, pYR = C.psum.next()
                S.op("pe", lambda e: e.matmul(pY[:, 0:128], An[0][:], Y[0][:], start=True, stop=True),
                     reads=[An[1], Y[1]], writes=[pYR])
                Yn = mats.next()
                S.op("dve", lambda e: e.tensor_tensor(Yn[0][:], Y[0][:], pY[:, 0:128], ALU.add),
                     reads=[Y[1], pYR], writes=[Yn[1]])
                Y = Yn
                Ak = An
                if lvl < 6:
                    Bk = Bn
            if PSUB < 7:
                return T
            pu, puR = C.psum.next()
            S.op("pe", lambda e: e.matmul(pu[:, 0:128], Y[0][:], T.vb[0][:], start=True, stop=True),
                 reads=[Y[1], T.vb[1]], writes=[puR])
            S.op("pe", lambda e: e.matmul(pu[:, 128:256], T.kb[0][:], Y[0][:], start=True, stop=True),
                 reads=[Y[1], T.kb[1]], writes=[puR])
            T.u = keep.next()
            T.wT = keep.next()
            S.op("act", lambda e: e.activation(out=T.u[0][:], in_=pu[:, 0:128], func=AF.Copy), reads=[puR], writes=[T.u[1]])
            S.op("act", lambda e: e.activation(out=T.wT[0][:], in_=pu[:, 128:256], func=AF.Copy),
                 reads=[puR], writes=[T.wT[1]])
            return T

        def tile_rec(b, j, T, po):
            bb = B[b]
            gt, gtR = bb.gt
            St, SR = bb.S
            cs = slice(j * 128, (j + 1) * 128)
            pv, pvR = C.psum.next()
            S.op("pe", lambda e: e.matmul(pv[:, 0:128], T.wT[0][:], St[:], start=True, stop=True),
                 reads=[T.wT[1], SR], writes=[pvR])
            vn = mats.next()
            S.op("dve", lambda e: e.tensor_tensor(vn[0][:], T.u[0][:], pv[:, 0:128], ALU.subtract),
                 reads=[T.u[1], pvR], writes=[vn[1]])
            S.op("pe", lambda e: e.matmul(po[0][:, cs], St[:], T.qd[0][:], start=True, stop=False),
                 reads=[SR, T.qd[1]], writes=[po[1]])
            S.op("pe", lambda e: e.matmul(po[0][:, cs], vn[0][:], T.qk[0][:], start=False, stop=True),
                 reads=[vn[1], T.qk[1]], writes=[po[1]])
            pS, pSR = C.psum.next()
            S.op("pe", lambda e: e.matmul(pS[:, 0:128], T.kd[0][:], vn[0][:], start=True, stop=True),
                 reads=[T.kd[1], vn[1]], writes=[pSR])
            S.op("dve", lambda e: e.scalar_tensor_tensor(St[:], St[:], gt[:, 12 + j:13 + j], pS[:, 0:128],
                                                         ALU.mult, ALU.add),
                 reads=[SR, gtR, pSR], writes=[SR])

        def block_back(b, k, po):
            bb = B[b]
            cv, cvR = bb.cv
            ot, otR = bb.ot
            S.op("act", lambda e: e.activation(out=ot[:], in_=po[0][:, 0:512], func=AF.Copy), reads=[po[1]], writes=[otR])
            S.op("act", lambda e: e.activation(out=cv[:], in_=po[0][:, 0:512], func=AF.Square), reads=[po[1]], writes=[cvR])
            pn, pnR = C.psum.next()
            for q4 in range(4):
                S.op("pe", lambda e: e.matmul(pn[:, q4 * 128:(q4 + 1) * 128], C.ones, cv[:, q4 * 128:(q4 + 1) * 128],
                                              start=True, stop=True), reads=[cvR, C.constR], writes=[pnR])
            S.op("act", lambda e: e.activation(out=cv[:], in_=pn[:, 0:512], func=AF.Sqrt, bias=C.eps[:, 2:3],
                                               scale=1.0 / DH), reads=[pnR, C.constR], writes=[cvR])
            S.op("dve", lambda e: e.reciprocal(cv[:], cv[:]), reads=[cvR], writes=[cvR])
            S.op("dve", lambda e: e.tensor_tensor(ot[:], ot[:], cv[:], ALU.mult), reads=[otR, cvR], writes=[otR])
            og, ogR = bb.og.next()
            S.op("dve", lambda e: e.scalar_tensor_tensor(og[:], ot[:], hp[:, 14:15], bb.zs[0][:], ALU.mult, ALU.mult),
                 reads=[otR, hpR, bb.zs[1]], writes=[ogR])
            t0 = b * SEQ + k * 512
            S.dma("sp", og_dst[:, t0:t0 + 512], og[:], reads=[ogR])

        import os
        nblk = int(os.environ.get("P2_NBLK", SEQ // 512))
        stage = int(os.environ.get("P2_STAGE", 3))
        for k in range(nblk):
            pos = []
            for b in range(2):
                block_front(b, k)
                pos.append(opsum.next())
            for j in range(4):
                if stage >= 1:
                    Ts = [tile_prep(b, j) for b in range(2)]
                if stage >= 2:
                    for b in range(2):
                        tile_rec(b, j, Ts[b], pos[b])
            if stage >= 3:
                for b in range(2):
                    block_back(b, k, pos[b])
        S.barrier()


def _band_mats(first_segment):
    band = np.zeros((128, 4, 3, 128), np.float32)
    tp = np.arange(128)[:, None]
    t = np.arange(128)[None, :]
    for g, w in enumerate((2, 4, 8, 16)):
        own = ((t - tp >= 0) & (t - tp < w)).astype(np.float32) / w - (t == tp).astype(np.float32)
        prev = ((t + 128 - tp) < w).astype(np.float32) / w
        band[:, g, 0, :] = prev
        band[:, g, 2, :] = own
        if first_segment:
            cnt = np.minimum(t + 1, w).astype(np.float32)
            band[:, g, 1, :] = ((t - tp >= 0) & (t - tp < w)).astype(np.float32) / cnt - (t == tp).astype(np.float32)
        else:
            band[:, g, 1, :] = own
    return band


def _cmat():
    c = np.zeros((128, 4, 128), np.float32)
    i = np.arange(128)[:, None]
    j = np.arange(128)[None, :]
    c[:, 0, :] = (i == j)
    c[:, 1, :] = 1.0
    c[:, 2, :] = (i <= j)
    c[:, 3, :] = (i > j)
    return c


def _dram_in(nc, name, shape, dt=F32):
    return nc.dram_tensor(name, list(shape), dt, kind="ExternalInput")


W_NAMES = ("pool_w", "mlp_w10", "mlp_w20", "ple_gate_w0", "ple_proj0",
           "gdn_w_out", "mlp_w11", "mlp_w21", "ple_gate_w1", "ple_proj1")
W_SHAPES = {"pool_w": (4, 256, 256), "mlp_w10": (D, DFF), "mlp_w20": (DFF, D), "ple_gate_w0": (D, D),
            "ple_proj0": (256, D), "gdn_w_out": (D, D), "mlp_w11": (D, DFF), "mlp_w21": (DFF, D),
            "ple_gate_w1": (D, D), "ple_proj1": (256, D)}


def build(mode):
    nc = bass.Bass("TRN2", target_bir_lowering=False)
    dd = {}
    with ExitStack() as es:
        S = Sched(nc, es)
        vecs_h = _dram_in(nc, "vecs", (12, D))
        cmat_h = _dram_in(nc, "cmat", (128, 4, 128))
        C = setup_common(nc, es, S, vecs_h, cmat_h.ap(), tok=(mode != "p2"))
        if mode in ("p1", "fused"):
            for n in ("pool_w", "mlp_w10", "mlp_w20", "ple_gate_w0", "ple_proj0"):
                dd[n] = _dram_in(nc, n, W_SHAPES[n]).ap()
            dd["x"] = _dram_in(nc, "x", (128 + TOK, D)).ap()
            dd["p0"] = _dram_in(nc, "p0", (TOK, 256)).ap()
            dd["band"] = _dram_in(nc, "band", (128, 4, 3, 128)).ap()
        if mode in ("p2", "fused"):
            dd["win"] = _dram_in(nc, "win", (D, 514)).ap()
            dd["hp"] = _dram_in(nc, "hp", (128, 16)).ap()
        if mode in ("p3", "fused"):
            for n in ("gdn_w_out", "mlp_w11", "mlp_w21", "ple_gate_w1", "ple_proj1"):
                dd[n] = _dram_in(nc, n, W_SHAPES[n]).ap()
            dd["p1"] = _dram_in(nc, "p1", (TOK, 256)).ap()
        if mode == "p1":
            accout = nc.dram_tensor("accout", [TOK, D], F32, kind="ExternalOutput").ap()
            x1T = nc.dram_tensor("x1T", [8, 128, TOK], BF16, kind="ExternalOutput").ap()
            emit_phase1(C, dd, x1T)
            for i in range(NT):
                S.dma("sp", accout[i * 128:(i + 1) * 128, :], C.acc[:, i, :], reads=[C.accR[i]])
        elif mode == "p2":
            xg = _dram_in(nc, "xg", (8, 8, 128, TOK), BF16).ap()
            og = nc.dram_tensor("og", [128, 2 * SEQ], BF16, kind="ExternalOutput").ap()
            emit_phase2(C, dd, xg, og)
        elif mode == "fused":
            gidx_d = _dram_in(nc, "gidx", (128, 8), I32).ap()
            out = nc.dram_tensor("out", [TOK, D], F32, kind="ExternalOutput").ap()
            x1T_src = nc.dram_tensor("x1T_src", [8 * 128, TOK], BF16).ap()
            xg_all = nc.dram_tensor("xg_all", [8 * 8 * 128, TOK], BF16).ap()
            og_src = nc.dram_tensor("og_src", [128, 2 * SEQ], BF16).ap()
            og_all = nc.dram_tensor("og_all", [8 * 128, 2 * SEQ], BF16).ap()
            x1R, xgR, ogsR, ogaR = R("x1T_src"), R("xg_all"), R("og_src"), R("og_all")
            gi = es.enter_context(nc.sbuf_tensor("gidx_sb", [128, 8], I32))
            giR = R("gidx")
            S.dma("sp", gi[:], gidx_d, writes=[giR])
            C.dramR = {"x1T": x1R, "xg": xgR, "og": ogsR}
            emit_phase1(C, dd, x1T_src.rearrange("(f p) t -> f p t", p=128))
            S.collective(lambda e: e.collective_compute("AllGather", ALU.bypass, replica_groups=[list(range(NCORES))],
                                                        ins=[x1T_src.opt()], outs=[xg_all.opt()]),
                         reads=[x1R], writes=[xgR])
            emit_phase2(C, dd, xg_all.rearrange("(r f p) t -> r f p t", r=8, f=8), og_src)
            S.collective(lambda e: e.collective_compute("AllGather", ALU.bypass, replica_groups=[list(range(NCORES))],
                                                        ins=[og_src.opt()], outs=[og_all.opt()]),
                         reads=[ogsR], writes=[ogaR])

            def load_og(C):
                rows = og_all.rearrange("r (c t) -> (r c) t", t=TOK)
                for h in range(8):
                    S.dma("pool", None, None, reads=[ogaR, giR], writes=C.xTR,
                          fn=lambda e: e.indirect_dma_start(
                              out=C.xT[:, h, :], out_offset=None, in_=rows,
                              in_offset=bass.IndirectOffsetOnAxis(ap=gi[:, h:h + 1], axis=0)))
            emit_phase3(C, dd, load_og, out)
        elif mode == "p3":
            accin = _dram_in(nc, "accin", (TOK, D)).ap()
            ogin = _dram_in(nc, "ogin", (8, 128, TOK), BF16).ap()
            out = nc.dram_tensor("out", [TOK, D], F32, kind="ExternalOutput").ap()
            for i in range(NT):
                S.dma("sp", C.acc[:, i, :], accin[i * 128:(i + 1) * 128, :], writes=[C.accR[i]])

            def load_og(C):
                for h in range(8):
                    S.dma("sp", C.xT[:, h, :], ogin[h], writes=C.xTR)
            emit_phase3(C, dd, load_og, out)
        S.finish()
    return nc


def run_fused(x, p, ln_gain, ln_bias, pool_w, pool_b, pool_scale, gdn_w_in, gdn_conv, gdn_a_log, gdn_dt_bias,
              gdn_norm_w, gdn_w_out, mlp_w1, mlp_w2, ple_gate_w, ple_gate_b, ple_proj):
    com = _common_inputs(ln_gain, ln_bias, pool_b, pool_scale, ple_gate_b)
    shared = {"pool_w": np.ascontiguousarray(pool_w[0]), "mlp_w10": np.ascontiguousarray(mlp_w1[0]),
              "mlp_w20": np.ascontiguousarray(mlp_w2[0]), "ple_gate_w0": np.ascontiguousarray(ple_gate_w[0]),
              "ple_proj0": np.ascontiguousarray(ple_proj[0]), "gdn_w_out": np.ascontiguousarray(gdn_w_out[0]),
              "mlp_w11": np.ascontiguousarray(mlp_w1[1]), "mlp_w21": np.ascontiguousarray(mlp_w2[1]),
              "ple_gate_w1": np.ascontiguousarray(ple_gate_w[1]), "ple_proj1": np.ascontiguousarray(ple_proj[1])}
    ins = []
    for r in range(NCORES):
        b, s = r // 4, r % 4
        xs = np.zeros((128 + TOK, D), np.float32)
        xs[128:] = x[b, s * TOK:(s + 1) * TOK]
        if s > 0:
            xs[:128] = x[b, s * TOK - 128:s * TOK]
        m = dict(com)
        m.update(shared)
        m.update(_p2_inputs(r, gdn_w_in, gdn_conv, gdn_a_log, gdn_dt_bias, gdn_norm_w))
        gidx = ((np.arange(8)[None, :] * 128 + np.arange(128)[:, None]) * 8 + r).astype(np.int32)
        m.update({"x": xs, "p0": np.ascontiguousarray(p[0, b, s * TOK:(s + 1) * TOK]),
                  "p1": np.ascontiguousarray(p[1, b, s * TOK:(s + 1) * TOK]), "band": _band_mats(s == 0),
                  "gidx": gidx})
        ins.append(m)
    res = run_bass_kernel_spmd(_get_nc("fused"), ins, core_ids=list(range(NCORES)))
    out = np.zeros((2, SEQ, D), np.float32)
    for r in range(NCORES):
        b, s = r // 4, r % 4
        out[b, s * TOK:(s + 1) * TOK] = res.results[r]["out"]
    return out


_NC_CACHE = {}


def _get_nc(mode):
    if mode not in _NC_CACHE:
        _NC_CACHE[mode] = build(mode)
    return _NC_CACHE[mode]


def _common_inputs(ln_gain, ln_bias, pool_b, pool_scale, ple_gate_b):
    vecs = np.concatenate([
        np.asarray(ln_gain, np.float32).reshape(4, D), np.asarray(ln_bias, np.float32).reshape(4, D),
        np.asarray(pool_b, np.float32).reshape(1, D), np.asarray(pool_scale, np.float32).reshape(1, D),
        np.asarray(ple_gate_b, np.float32).reshape(2, D)], axis=0)
    return {"vecs": np.ascontiguousarray(vecs), "cmat": _cmat()}


def run_p1(x, p, ln_gain, ln_bias, pool_w, pool_b, pool_scale, mlp_w1, mlp_w2, ple_gate_w, ple_gate_b, ple_proj):
    com = _common_inputs(ln_gain, ln_bias, pool_b, pool_scale, ple_gate_b)
    ins = []
    for r in range(NCORES):
        b, s = r // 4, r % 4
        xs = np.zeros((128 + TOK, D), np.float32)
        xs[128:] = x[b, s * TOK:(s + 1) * TOK]
        if s > 0:
            xs[:128] = x[b, s * TOK - 128:s * TOK]
        m = dict(com)
        m.update({"x": xs, "p0": np.ascontiguousarray(p[0, b, s * TOK:(s + 1) * TOK]), "band": _band_mats(s == 0),
                  "pool_w": np.ascontiguousarray(pool_w[0]), "mlp_w10": np.ascontiguousarray(mlp_w1[0]),
                  "mlp_w20": np.ascontiguousarray(mlp_w2[0]), "ple_gate_w0": np.ascontiguousarray(ple_gate_w[0]),
                  "ple_proj0": np.ascontiguousarray(ple_proj[0])})
        ins.append(m)
    res = run_bass_kernel_spmd(_get_nc("p1"), ins, core_ids=list(range(NCORES)))
    return [r_["accout"] for r_ in res.results], [r_["x1T"] for r_ in res.results]


def _p2_inputs(r, gdn_w_in, gdn_conv, gdn_a_log, gdn_dt_bias, gdn_norm_w):
    w = gdn_w_in[0]
    cols = np.concatenate([np.arange(r * 128, (r + 1) * 128), 1024 + np.arange(r * 128, (r + 1) * 128),
                           2048 + np.arange(r * 128, (r + 1) * 128), 3072 + np.arange(r * 128, (r + 1) * 128),
                           np.array([4096 + r, 4104 + r])])
    win = np.ascontiguousarray(w[:, cols])
    hp = np.zeros((128, 16), np.float32)
    for c in range(3):
        hp[:, 4 * c:4 * c + 4] = gdn_conv[0][:, c * 1024 + r * 128:c * 1024 + (r + 1) * 128].T
    hp[:, 12] = gdn_a_log[0, r]
    hp[:, 13] = gdn_dt_bias[0, r]
    hp[:, 14] = gdn_norm_w[0]
    return {"win": win, "hp": hp}


def run_p2(x1T_list, com, gdn_w_in, gdn_conv, gdn_a_log, gdn_dt_bias, gdn_norm_w):
    xg = np.ascontiguousarray(np.stack(x1T_list, axis=0))
    ins = []
    for r in range(NCORES):
        m = dict(com)
        m.update(_p2_inputs(r, gdn_w_in, gdn_conv, gdn_a_log, gdn_dt_bias, gdn_norm_w))
        m["xg"] = xg
        ins.append(m)
    res = run_bass_kernel_spmd(_get_nc("p2"), ins, core_ids=list(range(NCORES)))
    return [r_["og"] for r_ in res.results]


def run_p3(acc_list, og_list, com, p, gdn_w_out, mlp_w1, mlp_w2, ple_gate_w, ple_proj):
    ins = []
    for r in range(NCORES):
        b, s = r // 4, r % 4
        t0 = b * SEQ + s * TOK
        ogin = np.ascontiguousarray(np.stack([og_list[h][:, t0:t0 + TOK] for h in range(8)], axis=0))
        m = dict(com)
        m.update({"accin": acc_list[r], "ogin": ogin, "p1": np.ascontiguousarray(p[1, b, s * TOK:(s + 1) * TOK]),
                  "gdn_w_out": np.ascontiguousarray(gdn_w_out[0]), "mlp_w11": np.ascontiguousarray(mlp_w1[1]),
                  "mlp_w21": np.ascontiguousarray(mlp_w2[1]), "ple_gate_w1": np.ascontiguousarray(ple_gate_w[1]),
                  "ple_proj1": np.ascontiguousarray(ple_proj[1])})
        ins.append(m)
    res = run_bass_kernel_spmd(_get_nc("p3"), ins, core_ids=list(range(NCORES)))
    return [r_["out"] for r_ in res.results]


def kernel(x, p, ln_gain, ln_bias, pool_w, pool_b, pool_scale, gdn_w_in, gdn_conv, gdn_a_log, gdn_dt_bias,
           gdn_norm_w, gdn_w_out, mlp_w1, mlp_w2, ple_gate_w, ple_gate_b, ple_proj):
    args = [np.asarray(a) for a in (x, p, ln_gain, ln_bias, pool_w, pool_b, pool_scale, gdn_w_in, gdn_conv,
                                    gdn_a_log, gdn_dt_bias, gdn_norm_w, gdn_w_out, mlp_w1, mlp_w2, ple_gate_w,
                                    ple_gate_b, ple_proj)]
    (x, p, ln_gain, ln_bias, pool_w, pool_b, pool_scale, gdn_w_in, gdn_conv, gdn_a_log, gdn_dt_bias,
     gdn_norm_w, gdn_w_out, mlp_w1, mlp_w2, ple_gate_w, ple_gate_b, ple_proj) = args
    if FUSED:
        return run_fused(x, p, ln_gain, ln_bias, pool_w, pool_b, pool_scale, gdn_w_in, gdn_conv, gdn_a_log,
                         gdn_dt_bias, gdn_norm_w, gdn_w_out, mlp_w1, mlp_w2, ple_gate_w, ple_gate_b, ple_proj)
    com = _common_inputs(ln_gain, ln_bias, pool_b, pool_scale, ple_gate_b)
    accs, x1Ts = run_p1(x, p, ln_gain, ln_bias, pool_w, pool_b, pool_scale, mlp_w1, mlp_w2, ple_gate_w, ple_gate_b,
                        ple_proj)
    ogs = run_p2(x1Ts, com, gdn_w_in, gdn_conv, gdn_a_log, gdn_dt_bias, gdn_norm_w)
    outs = run_p3(accs, ogs, com, p, gdn_w_out, mlp_w1, mlp_w2, ple_gate_w, ple_proj)
    out = np.zeros((2, SEQ, D), np.float32)
    for r in range(NCORES):
        b, s = r // 4, r % 4
        out[b, s * TOK:(s + 1) * TOK] = outs[r]
    return out
```

```python
import os
import numpy as np
from contextlib import ExitStack
import concourse.bass as bass
import concourse.mybir as mybir
from concourse.bass_utils import run_bass_kernel_spmd

F32 = mybir.dt.float32
BF16 = mybir.dt.bfloat16
I32 = mybir.dt.int32
AF = mybir.ActivationFunctionType
ALU = mybir.AluOpType

NCORES = 8
D = 1024
TOK = 2048
NT = 16
SEQ = 8192
DFF = 4096
ALPHA = (2.0 * 2) ** 0.25
LN_EPS = 1e-5
RMS_EPS = 1e-6
L2_EPS = 1e-6
DH = 128
FUSED = os.environ.get("K_FUSED", "1") == "1"


class R:
    __slots__ = ("name", "w", "r", "excl")

    def __init__(self, name, excl=False):
        self.name = name
        self.w = None
        self.r = {}
        self.excl = excl


class Sched:
    def __init__(self, nc, es, nd=24):
        self.nc = nc
        self._es = es
        self.eng = {"pe": nc.tensor, "act": nc.scalar, "dve": nc.vector, "pool": nc.gpsimd, "sp": nc.sync}
        self.sem = {k: es.enter_context(nc.semaphore("s_" + k)) for k in self.eng}
        self.cnt = {k: 0 for k in self.eng}
        self.ND = nd
        self.dsem = [es.enter_context(nc.semaphore(f"dq{i}")) for i in range(nd)]
        self.dcnt = [0] * nd
        self.dnext = 0
        self.waited = {}

    def _semof(self, src):
        if src == "cc":
            return self.csem
        return self.sem[src] if src in self.sem else self.dsem[int(src[1:])]

    def _wait(self, eng, deps):
        for src, val in deps.items():
            if src == eng and eng == "pe":
                continue
            if self.waited.get((eng, src), 0) < val:
                self.eng[eng].wait_ge(self._semof(src), val)
                self.waited[(eng, src)] = val

    @staticmethod
    def _deps(reads, writes, eng=None):
        deps = {}
        for r in reads:
            if r.w is not None:
                s, v = r.w
                if deps.get(s, 0) < v:
                    deps[s] = v
        for w in writes:
            if w.w is not None:
                s, v = w.w
                if deps.get(s, 0) < v:
                    deps[s] = v
            for s, v in w.r.items():
                if deps.get(s, 0) < v:
                    deps[s] = v
        return deps

    def op(self, eng, fn, reads=(), writes=()):
        ex = [r for r in reads if r.excl]
        if ex:
            writes = list(writes) + ex
        self._wait(eng, self._deps(reads, writes, eng))
        ins = fn(self.eng[eng])
        self.cnt[eng] += 1
        v = self.cnt[eng]
        ins.then_inc(self.sem[eng], 1)
        for r in reads:
            r.r[eng] = v
        for w in writes:
            w.w = (eng, v)
            w.r = {}
        return ins

    def dma(self, q, out, in_, reads=(), writes=(), fn=None):
        k = self.dnext
        self.dnext = (k + 1) % self.ND
        deps = self._deps(reads, writes)
        src = f"d{k}"
        if self.dcnt[k] > 0:
            deps[src] = max(deps.get(src, 0), 16 * self.dcnt[k])
        self._wait(q, deps)
        if fn is None:
            ins = self.eng[q].dma_start(out=out, in_=in_)
        else:
            ins = fn(self.eng[q])
        self.dcnt[k] += 1
        v = 16 * self.dcnt[k]
        ins.then_inc(self.dsem[k], 16)
        for r in reads:
            r.r[src] = v
        for w in writes:
            w.w = (src, v)
            w.r = {}
        self.last_dma = (src, v)
        return ins

    def collective(self, fn, reads=(), writes=(), extra=None):
        if not hasattr(self, "csem"):
            self.csem = self._es.enter_context(self.nc.semaphore("s_cc"))
            self.ccnt = 0
        deps = self._deps(reads, writes, "pool")
        for k_, v_ in (extra or {}).items():
            if deps.get(k_, 0) < v_:
                deps[k_] = v_
        self._wait("pool", deps)
        ins = fn(self.eng["pool"])
        self.ccnt += 1
        ins.then_inc(self.csem)
        for r in reads:
            r.r["cc"] = self.ccnt
        for w in writes:
            w.w = ("cc", self.ccnt)
            w.r = {}
        return ins

    def barrier(self, skip_cc=False):
        tgt = {k: v for k, v in self.cnt.items() if v > 0}
        for k in range(self.ND):
            if self.dcnt[k] > 0:
                tgt[f"d{k}"] = 16 * self.dcnt[k]
        if getattr(self, "ccnt", 0) > 0 and not skip_cc:
            tgt["cc"] = self.ccnt
        for e in self.eng:
            self._wait(e, tgt)

    def finish(self):
        tgt = {k: v for k, v in self.cnt.items() if v > 0}
        for k in range(self.ND):
            if self.dcnt[k] > 0:
                tgt[f"d{k}"] = 16 * self.dcnt[k]
        self._wait("sp", tgt)


class Ring:
    def __init__(self, es, nc, name, shape, dtype, n, psum=False):
        self.t = []
        for i in range(n):
            if psum:
                t = es.enter_context(nc.psum_tensor(f"{name}{i}", shape, dtype))
            else:
                t = es.enter_context(nc.sbuf_tensor(f"{name}{i}", shape, dtype))
            self.t.append((t, R(f"{name}{i}", excl=psum)))
        self.i = 0

    def next(self):
        x = self.t[self.i]
        self.i = (self.i + 1) % len(self.t)
        return x

    def acq(self):
        if not hasattr(self, "free"):
            self.free = list(self.t)
        while not self.free:
            yield
        return self.free.pop(0)

    def rel(self, x):
        self.free.append(x)


def bcast_ap(handle, off, n):
    return bass.AP(handle, off, [[0, 128], [1, n]])


class Ctx:
    pass


def emit_layernorm(C, i, gain, bias):
    S = C.S
    a = C.acc[:, i, :]
    aR = C.accR[i]
    st, stR = C.small.next()
    S.op("dve", lambda e: e.bn_stats(st[:, 0:6], C.acc[:, i, 0:512]), reads=[aR], writes=[stR])
    S.op("dve", lambda e: e.bn_stats(st[:, 6:12], C.acc[:, i, 512:1024]), reads=[aR], writes=[stR])
    S.op("dve", lambda e: e.bn_aggr(st[:, 12:14], st[:, 0:12]), reads=[stR], writes=[stR])
    S.op("act", lambda e: e.activation(out=st[:, 14:15], in_=st[:, 13:14], func=AF.Sqrt, bias=C.eps[:, 0:1]),
         reads=[stR, C.constR], writes=[stR])
    S.op("dve", lambda e: e.reciprocal(st[:, 14:15], st[:, 14:15]), reads=[stR], writes=[stR])
    S.op("dve", lambda e: e.scalar_tensor_tensor(st[:, 15:16], st[:, 12:13], -1.0, st[:, 14:15], ALU.mult, ALU.mult),
         reads=[stR], writes=[stR])
    S.op("act", lambda e: e.activation(out=a, in_=a, func=AF.Identity, bias=st[:, 15:16], scale=st[:, 14:15]),
         reads=[stR, aR], writes=[aR])
    S.op("dve", lambda e: e.tensor_tensor(a, a, gain[0][:], ALU.mult), reads=[aR, gain[1]], writes=[aR])
    S.op("dve", lambda e: e.tensor_tensor(a, a, bias[0][:], ALU.add), reads=[aR, bias[1]], writes=[aR])


def emit_transpose_to_xT(C, i, scale_after=None):
    S = C.S
    for half in range(2):
        ps, psR = C.psum.next()
        for q in range(4):
            fc = half * 4 + q
            S.op("pe", lambda e: e.transpose(ps[:, q * 128:(q + 1) * 128], C.acc[:, i, fc * 128:(fc + 1) * 128],
                                             C.ident[:]),
                 reads=[C.accR[i], C.constR], writes=[psR])
        S.op("act", lambda e: e.activation(
            out=C.xT[:, half * 4:(half + 1) * 4, i * 128:(i + 1) * 128],
            in_=ps[:, 0:512].rearrange("p (q t) -> p q t", q=4), func=AF.Copy),
            reads=[psR], writes=[C.xTR[i]])
    if scale_after is not None:
        S.op("dve", lambda e: e.tensor_scalar(C.acc[:, i, :], C.acc[:, i, :], float(scale_after), None, ALU.mult),
             reads=[C.accR[i]], writes=[C.accR[i]])


def alloc_tok(C, es, tag):
    C.xT = es.enter_context(C.nc.sbuf_tensor(f"xT{tag}", [128, 8, TOK], BF16))
    C.xTR = [R(f"xT{i}") for i in range(NT)]
    C.bc = Ring(es, C.nc, f"bc{tag}", [128, D], F32, 5)
    C.tmp = Ring(es, C.nc, f"tmp{tag}", [128, 512], F32, 3)


def load_bcast(C, row):
    t, r = C.bc.next()
    C.S.dma("sp", t[:], bcast_ap(C.vecs_h, row * D, D), writes=[r])
    return (t, r)


def emit_ple(C, es, layer, p_d, wg_d, wp_d, bg_row):
    S, nc = C.S, C.nc
    wg = es.enter_context(nc.sbuf_tensor(f"wg{layer}", [128, 8, D], BF16))
    wgR = R("wg")
    wp = es.enter_context(nc.sbuf_tensor(f"wp{layer}", [128, 2, D], BF16))
    wpR = R("wp")
    pring = Ring(es, nc, f"pt{layer}", [128, 4, 256], F32, 2)
    pTring = Ring(es, nc, f"pT{layer}", [128, 2, 128], BF16, 2)
    for kc2 in range(2):
        S.dma("pool", wg[:, kc2 * 4:(kc2 + 1) * 4, :],
              wg_d[kc2 * 512:(kc2 + 1) * 512, :].rearrange("(k p) c -> p k c", p=128), writes=[wgR])
    S.dma("pool", wp[:], wp_d.rearrange("(k p) c -> p k c", p=128), writes=[wpR])
    bg = load_bcast(C, bg_row)
    st = {"pt": None}

    def tile(i):
        if i % 4 == 0:
            st["pt"] = pring.next()
            S.dma("sp", st["pt"][0][:], p_d[i * 128:(i + 4) * 128, :].rearrange("(j t) f -> t j f", t=128),
                  writes=[st["pt"][1]])
        pt = st["pt"]
        ps, psR = C.psum.next()
        for k in range(2):
            S.op("pe", lambda e: e.transpose(ps[:, k * 128:(k + 1) * 128], pt[0][:, i % 4, k * 128:(k + 1) * 128],
                                             C.ident[:]), reads=[pt[1], C.constR], writes=[psR])
        pT, pTR = pTring.next()
        S.op("act", lambda e: e.activation(out=pT[:], in_=ps[:, 0:256].rearrange("p (k t) -> p k t", k=2),
                                           func=AF.Copy), reads=[psR], writes=[pTR])
        for half in range(2):
            cs = slice(half * 512, (half + 1) * 512)
            pg, pgR = C.psum.next()
            for kc in range(8):
                S.op("pe", lambda e: e.matmul(pg[:, 0:512], C.xT[:, kc, i * 128:(i + 1) * 128], wg[:, kc, cs],
                                              start=(kc == 0), stop=(kc == 7)),
                     reads=[C.xTR[i], wgR], writes=[pgR])
            pp, ppR = C.psum.next()
            for k in range(2):
                S.op("pe", lambda e: e.matmul(pp[:, 0:512], pT[:, k, :], wp[:, k, cs], start=(k == 0), stop=(k == 1)),
                     reads=[pTR, wpR], writes=[ppR])
            tm, tmR = C.tmp.next()
            S.op("dve", lambda e: e.tensor_tensor(tm[:, 0:512], pg[:, 0:512], bg[0][:, cs], ALU.add),
                 reads=[pgR, bg[1]], writes=[tmR])
            S.op("act", lambda e: e.activation(out=tm[:, 0:512], in_=tm[:, 0:512], func=AF.Sigmoid),
                 reads=[tmR], writes=[tmR])
            S.op("dve", lambda e: e.tensor_tensor(tm[:, 0:512], tm[:, 0:512], pp[:, 0:512], ALU.mult),
                 reads=[tmR, ppR], writes=[tmR])
            S.op("dve", lambda e: e.tensor_tensor(C.acc[:, i, cs], C.acc[:, i, cs], tm[:, 0:512], ALU.add),
                 reads=[tmR, C.accR[i]], writes=[C.accR[i]])

    return tile


def emit_mlp(C, es, layer, w1_d, w2_d, after_last=None):
    S, nc = C.S, C.nc
    w1ring = Ring(es, nc, f"w1g{layer}", [128, 8, 512], BF16, 2)
    w2ring = Ring(es, nc, f"w2g{layer}", [128, 4, D], BF16, 2)
    hring = Ring(es, nc, f"hT{layer}", [128, 4, TOK], BF16, 2)
    NG = DFF // 512

    def load(gi):
        w1g = w1ring.next()
        w2g = w2ring.next()
        for kc2 in range(2):
            S.dma("pool", w1g[0][:, kc2 * 4:(kc2 + 1) * 4, :],
                  w1_d[kc2 * 512:(kc2 + 1) * 512, gi * 512:(gi + 1) * 512].rearrange("(k p) c -> p k c", p=128),
                  writes=[w1g[1]])
        S.dma("pool", w2g[0][:], w2_d[gi * 512:(gi + 1) * 512, :].rearrange("(k p) c -> p k c", p=128),
              writes=[w2g[1]])
        return w1g, w2g

    nxt = load(0)
    for gi in range(NG):
        w1g, w2g = nxt
        if gi + 1 < NG:
            nxt = load(gi + 1)
        hT, hTR = hring.next()
        for tb in range(4):
            ts_ = slice(tb * 512, (tb + 1) * 512)
            for hcl in range(4):
                ps, psR = C.psum.next()
                for kc in range(8):
                    S.op("pe", lambda e: e.matmul(ps[:, 0:512], w1g[0][:, kc, hcl * 128:(hcl + 1) * 128],
                                                  C.xT[:, kc, ts_], start=(kc == 0), stop=(kc == 7)),
                         reads=[w1g[1]] + C.xTR[tb * 4:(tb + 1) * 4], writes=[psR])
                tm, tmR = C.tmp.next()
                S.op("act", lambda e: e.activation(out=tm[:, 0:512], in_=ps[:, 0:512], func=AF.Relu),
                     reads=[psR], writes=[tmR])
                S.op("dve", lambda e: e.tensor_tensor(hT[:, hcl, ts_], tm[:, 0:512], tm[:, 0:512], ALU.mult),
                     reads=[tmR], writes=[hTR])
        for i in range(NT):
            for half in range(2):
                cs = slice(half * 512, (half + 1) * 512)
                ps, psR = C.psum.next()
                for hcl in range(4):
                    S.op("pe", lambda e: e.matmul(ps[:, 0:512], hT[:, hcl, i * 128:(i + 1) * 128], w2g[0][:, hcl, cs],
                                                  start=(hcl == 0), stop=(hcl == 3)),
                         reads=[hTR, w2g[1]], writes=[psR])
                S.op("dve", lambda e: e.tensor_tensor(C.acc[:, i, cs], C.acc[:, i, cs], ps[:, 0:512], ALU.add),
                     reads=[psR, C.accR[i]], writes=[C.accR[i]])
            if gi == NG - 1 and after_last is not None:
                after_last(i)


def setup_common(nc, es, S, vecs_h, cmat_d, tok=True):
    C = Ctx()
    C.nc, C.S = nc, S
    C.vecs_h = vecs_h
    if tok:
        C.acc = es.enter_context(nc.sbuf_tensor("acc", [128, NT, D], F32))
        C.accR = [R(f"acc{i}") for i in range(NT)]
    C.cmat = es.enter_context(nc.sbuf_tensor("cmat_sb", [128, 6, 128], F32))
    C.constR = R("const")
    S.dma("sp", C.cmat[:], cmat_d, writes=[C.constR])
    C.eps = es.enter_context(nc.sbuf_tensor("eps_sb", [128, 4], F32))
    S.op("dve", lambda e: e.memset(C.eps[:, 0:1], LN_EPS), writes=[C.constR])
    S.op("dve", lambda e: e.memset(C.eps[:, 1:2], L2_EPS), writes=[C.constR])
    S.op("dve", lambda e: e.memset(C.eps[:, 2:3], RMS_EPS), writes=[C.constR])
    C.ident = C.cmat[:, 0, :]
    C.ones = C.cmat[:, 1, :]
    C.U = C.cmat[:, 2, :]
    C.L = C.cmat[:, 3, :]
    C.Lbd = C.cmat[:, 4, :]
    C.Loff = C.cmat[:, 5, :]
    C.psum = Ring(es, nc, "ps", [128, 512], F32, 6, psum=True)
    C.small = Ring(es, nc, "sm", [128, 16], F32, 4)
    return C


def emit_phase1(C, dd, x1T_dst):
    with ExitStack() as esx:
        alloc_tok(C, esx, "a")
        _emit_phase1(C, dd, x1T_dst)


def _emit_phase1(C, dd, x1T_dst):
    S, nc = C.S, C.nc
    with ExitStack() as es:
        xh = es.enter_context(nc.sbuf_tensor("xh", [128, D], F32))
        xhR = R("xh")
        band = es.enter_context(nc.sbuf_tensor("band_sb", [128, 4, 3, 128], F32))
        bandR = R("band")
        pw = es.enter_context(nc.sbuf_tensor("poolw", [128, 8, 256], BF16))
        pwR = R("pw")
        S.dma("sp", band[:], dd["band"], writes=[bandR])
        S.dma("pool", pw[:], dd["pool_w"].rearrange("g (k p) o -> p (g k) o", p=128), writes=[pwR])
        S.dma("sp", xh[:], dd["x"][0:128, :], writes=[xhR])
        for i in range(NT):
            S.dma("sp", C.acc[:, i, :], dd["x"][128 * (i + 1):128 * (i + 2), :], writes=[C.accR[i]])
        for i in range(NT):
            for gh in range(2):
                ps, psR = C.psum.next()
                for gq in range(4):
                    g8 = gh * 4 + gq
                    g, cc = g8 // 2, g8 % 2
                    c0 = 256 * g + 128 * cc
                    if i == 0:
                        prev_ap, prevR = xh[:, c0:c0 + 128], xhR
                    else:
                        prev_ap, prevR = C.acc[:, i - 1, c0:c0 + 128], C.accR[i - 1]
                    o = ps[:, gq * 128:(gq + 1) * 128]
                    S.op("pe", lambda e: e.matmul(o, prev_ap, band[:, g, 0, :], start=True, stop=False),
                         reads=[prevR, bandR], writes=[psR])
                    S.op("pe", lambda e: e.matmul(o, C.acc[:, i, c0:c0 + 128], band[:, g, 1 if i == 0 else 2, :],
                                                  start=False, stop=True),
                         reads=[C.accR[i], bandR], writes=[psR])
                S.op("act", lambda e: e.activation(
                    out=C.xT[:, gh * 4:(gh + 1) * 4, i * 128:(i + 1) * 128],
                    in_=ps[:, 0:512].rearrange("p (q t) -> p q t", q=4), func=AF.Copy),
                    reads=[psR], writes=[C.xTR[i]])
        pb = load_bcast(C, 8)
        psc = load_bcast(C, 9)
        g0 = load_bcast(C, 0)
        b0 = load_bcast(C, 4)
        ple_tile = emit_ple(C, es, 0, dd["p0"], dd["ple_gate_w0"], dd["ple_proj0"], 10)
        for i in range(NT):
            for half in range(2):
                ps, psR = C.psum.next()
                for gq in range(2):
                    g = half * 2 + gq
                    for cc in range(2):
                        S.op("pe", lambda e: e.matmul(ps[:, gq * 256:(gq + 1) * 256],
                                                      C.xT[:, 2 * g + cc, i * 128:(i + 1) * 128], pw[:, 2 * g + cc, :],
                                                      start=(cc == 0), stop=(cc == 1)),
                             reads=[C.xTR[i], pwR], writes=[psR])
                cs = slice(half * 512, (half + 1) * 512)
                tm, tmR = C.tmp.next()
                S.op("dve", lambda e: e.tensor_tensor(tm[:, 0:512], ps[:, 0:512], pb[0][:, cs], ALU.add),
                     reads=[psR, pb[1]], writes=[tmR])
                S.op("dve", lambda e: e.tensor_tensor(tm[:, 0:512], tm[:, 0:512], psc[0][:, cs], ALU.mult),
                     reads=[tmR, psc[1]], writes=[tmR])
                S.op("dve", lambda e: e.scalar_tensor_tensor(C.acc[:, i, cs], C.acc[:, i, cs], ALPHA, tm[:, 0:512],
                                                             ALU.mult, ALU.add),
                     reads=[tmR, C.accR[i]], writes=[C.accR[i]])
            emit_layernorm(C, i, g0, b0)
            if i >= 1:
                emit_transpose_to_xT(C, i - 1, scale_after=ALPHA)
            if i >= 2:
                ple_tile(i - 2)
        emit_transpose_to_xT(C, NT - 1, scale_after=ALPHA)
        ple_tile(NT - 2)
        ple_tile(NT - 1)
        S.barrier()
    with ExitStack() as es:
        g1 = load_bcast(C, 1)
        b1 = load_bcast(C, 5)

        def tr(t):
            emit_transpose_to_xT(C, t, scale_after=ALPHA)
            if t % 4 == 3:
                x1T_dst(t // 4)

        def after_last(i):
            emit_layernorm(C, i, g1, b1)
            if i >= 2:
                tr(i - 2)
        emit_mlp(C, es, 0, dd["mlp_w10"], dd["mlp_w20"], after_last)
        tr(NT - 2)
        tr(NT - 1)
        S.barrier(skip_cc=True)


def emit_phase3(C, dd, load_og, out_d):
    with ExitStack() as esx:
        alloc_tok(C, esx, "c")
        _emit_phase3(C, dd, load_og, out_d)


def _emit_phase3(C, dd, load_og, out_d):
    S, nc = C.S, C.nc
    with ExitStack() as es:
        wo = es.enter_context(nc.sbuf_tensor("wo", [128, 8, D], BF16))
        woR = R("wo")
        for kc2 in range(2):
            S.dma("pool", wo[:, kc2 * 4:(kc2 + 1) * 4, :],
                  dd["gdn_w_out"][kc2 * 512:(kc2 + 1) * 512, :].rearrange("(k p) c -> p k c", p=128), writes=[woR])
        load_og(C)
        g0 = load_bcast(C, 2)
        b0 = load_bcast(C, 6)
        ple_tile = emit_ple(C, es, 1, dd["p1"], dd["ple_gate_w1"], dd["ple_proj1"], 11)
        for i in range(NT):
            for half in range(2):
                cs = slice(half * 512, (half + 1) * 512)
                ps, psR = C.psum.next()
                for kc in range(8):
                    S.op("pe", lambda e: e.matmul(ps[:, 0:512], C.xT[:, kc, i * 128:(i + 1) * 128], wo[:, kc, cs],
                                                  start=(kc == 0), stop=(kc == 7)),
                         reads=[C.xTR[i], woR], writes=[psR])
                S.op("dve", lambda e: e.tensor_tensor(C.acc[:, i, cs], C.acc[:, i, cs], ps[:, 0:512], ALU.add),
                     reads=[psR, C.accR[i]], writes=[C.accR[i]])
            emit_layernorm(C, i, g0, b0)
            if i >= 1:
                emit_transpose_to_xT(C, i - 1, scale_after=ALPHA)
            if i >= 2:
                ple_tile(i - 2)
        emit_transpose_to_xT(C, NT - 1, scale_after=ALPHA)
        ple_tile(NT - 2)
        ple_tile(NT - 1)
        S.barrier()
    with ExitStack() as es:
        g1 = load_bcast(C, 3)
        b1 = load_bcast(C, 7)

        def after_last(i):
            emit_layernorm(C, i, g1, b1)
            S.dma("sp", out_d[i * 128:(i + 1) * 128, :], C.acc[:, i, :], reads=[C.accR[i]])
        emit_mlp(C, es, 1, dd["mlp_w11"], dd["mlp_w21"], after_last)
        S.barrier()


def emit_phase2_old(C, dd, xg, og_dst):
    S, nc = C.S, C.nc
    with ExitStack() as es:
        def sb(name, shape, dt=F32):
            return es.enter_context(nc.sbuf_tensor(name, shape, dt))

        win = sb("win_sb", [128, 8, 514], BF16)
        winR = R("win")
        S.dma("pool", win[:], dd["win"].rearrange("(k p) c -> p k c", p=128), writes=[winR])
        hp = sb("hp_sb", [128, 16])
        hpR = R("hp")
        S.dma("sp", hp[:], dd["hp"], writes=[hpR])
        S.op("act", lambda e: e.activation(out=hp[:, 15:16], in_=hp[:, 12:13], func=AF.Exp), reads=[hpR], writes=[hpR])
        S.op("dve", lambda e: e.tensor_scalar(hp[:, 15:16], hp[:, 15:16], -1.0, None, ALU.mult),
             reads=[hpR], writes=[hpR])
        mats = Ring(es, nc, "m", [128, 128], F32, 40)
        keep = Ring(es, nc, "kp", [128, 128], F32, 24)
        opsum = Ring(es, nc, "po", [128, 512], F32, 2, psum=True)
        B = []
        for b in range(2):
            bb = Ctx()
            bb.xblk = Ring(es, nc, f"xb{b}", [128, 8, 512], BF16, 2)
            bb.pre = [(sb(f"pre{b}{c}", [128, 515]), R("pre")) for c in range(3)]
            bb.xs = [(sb(f"xs{b}{c}", [128, 512]), R("xs")) for c in range(3)]
            bb.qn = (sb(f"qn{b}", [128, 512]), R("qn"))
            bb.kn = (sb(f"kn{b}", [128, 512]), R("kn"))
            bb.zs = (sb(f"zs{b}", [128, 512]), R("zs"))
            bb.cv = (sb(f"cv{b}", [128, 512]), R("cv"))
            bb.gt = (sb(f"gt{b}", [128, 64]), R("gt"))
            bb.S = (sb(f"S{b}", [128, 128]), R("S"))
            bb.og = Ring(es, nc, f"og{b}", [128, 512], BF16, 2)
            bb.ot = (sb(f"ot{b}", [128, 512]), R("ot"))
            S.op("dve", lambda e: e.memset(bb.S[0][:], 0.0), writes=[bb.S[1]])
            for c in range(3):
                S.op("dve", lambda e: e.memset(bb.pre[c][0][:, 0:3], 0.0), writes=[bb.pre[c][1]])
            B.append(bb)

        import os
        SUB = int(os.environ.get("P2_SUB", 9))
        PSUB = int(os.environ.get("P2_PSUB", 9))

        def block_front(b, k):
            bb = B[b]
            rank = 4 * b + k // 4
            off = (k % 4) * 512
            xb, xbR = bb.xblk.next()
            S.dma("sp", xb[:], xg[rank, :, :, off:off + 512].rearrange("f p t -> p f t"),
                  reads=([C.dramR["xg"]] if hasattr(C, "dramR") else []), writes=[xbR])
            for c in range(4):
                ps, psR = C.psum.next()
                for kc in range(8):
                    S.op("pe", lambda e: e.matmul(ps[:, 0:512], win[:, kc, c * 128:(c + 1) * 128], xb[:, kc, :],
                                                  start=(kc == 0), stop=(kc == 7)), reads=[winR, xbR], writes=[psR])
                if c < 3:
                    S.op("act", lambda e: e.activation(out=bb.pre[c][0][:, 3:515], in_=ps[:, 0:512], func=AF.Copy),
                         reads=[psR], writes=[bb.pre[c][1]])
                else:
                    S.op("act", lambda e: e.activation(out=bb.zs[0][:], in_=ps[:, 0:512], func=AF.Silu),
                         reads=[psR], writes=[bb.zs[1]])
            if SUB < 1:
                return
            pg, pgR = C.psum.next()
            for j in range(4):
                for kc in range(8):
                    S.op("pe", lambda e: e.matmul(pg[:, 2 * j:2 * j + 2], xb[:, kc, j * 128:(j + 1) * 128],
                                                  win[:, kc, 512:514], start=(kc == 0), stop=(kc == 7)),
                         reads=[winR, xbR], writes=[pgR])
            if SUB < 2:
                return
            gt, gtR = bb.gt
            S.op("act", lambda e: e.activation(out=gt[:, 0:4], in_=pg[:, 0:8].rearrange("p (j c) -> p j c", c=2)[:, :, 0],
                                               func=AF.Sigmoid), reads=[pgR], writes=[gtR])
            S.op("act", lambda e: e.activation(out=gt[:, 24:28],
                                               in_=pg[:, 0:8].rearrange("p (j c) -> p j c", c=2)[:, :, 1],
                                               func=AF.Exp, bias=hp[:, 13:14]), reads=[pgR, hpR], writes=[gtR])
            S.op("act", lambda e: e.activation(out=gt[:, 24:28], in_=gt[:, 24:28], func=AF.Ln, bias=1.0),
                 reads=[gtR], writes=[gtR])
            S.op("dve", lambda e: e.tensor_scalar(gt[:, 4:8], gt[:, 24:28], hp[:, 15:16], None, ALU.mult),
                 reads=[gtR, hpR], writes=[gtR])
            if SUB < 3:
                return
            pc, pcR = C.psum.next()
            S.op("pe", lambda e: e.matmul(pc[:, 0:4], C.U, gt[:, 4:8], start=True, stop=True),
                 reads=[gtR, C.constR], writes=[pcR])
            S.op("pe", lambda e: e.matmul(pc[:, 4:8], C.ones, gt[:, 4:8], start=True, stop=True),
                 reads=[gtR, C.constR], writes=[pcR])
            S.op("dve", lambda e: e.tensor_copy(gt[:, 8:12], pc[:, 0:4]), reads=[pcR], writes=[gtR])
            S.op("act", lambda e: e.activation(out=gt[:, 12:16], in_=pc[:, 4:8], func=AF.Exp), reads=[pcR], writes=[gtR])
            S.op("dve", lambda e: e.tensor_tensor(gt[:, 24:28], pc[:, 4:8], gt[:, 8:12], ALU.subtract),
                 reads=[pcR, gtR], writes=[gtR])
            S.op("act", lambda e: e.activation(out=gt[:, 16:20], in_=gt[:, 24:28], func=AF.Exp), reads=[gtR], writes=[gtR])
            S.op("act", lambda e: e.activation(out=gt[:, 20:24], in_=gt[:, 8:12], func=AF.Exp), reads=[gtR], writes=[gtR])
            S.op("dve", lambda e: e.tensor_tensor(gt[:, 20:24], gt[:, 20:24], gt[:, 0:4], ALU.mult),
                 reads=[gtR], writes=[gtR])
            if SUB < 4:
                return
            cv, cvR = bb.cv
            for c in range(3):
                pre, preR = bb.pre[c]
                S.op("dve", lambda e: e.tensor_scalar(cv[:], pre[:, 0:512], hp[:, 4 * c:4 * c + 1], None, ALU.mult),
                     reads=[preR, hpR], writes=[cvR])
                for j in range(1, 4):
                    S.op("dve", lambda e: e.scalar_tensor_tensor(cv[:], pre[:, j:j + 512], hp[:, 4 * c + j:4 * c + j + 1],
                                                                 cv[:], ALU.mult, ALU.add),
                         reads=[preR, hpR, cvR], writes=[cvR])
                S.op("act", lambda e: e.activation(out=bb.xs[c][0][:], in_=cv[:], func=AF.Silu),
                     reads=[cvR], writes=[bb.xs[c][1]])
                S.op("dve", lambda e: e.tensor_copy(pre[:, 0:3], pre[:, 512:515]), reads=[preR], writes=[preR])
            if SUB < 5:
                return
            for c, dst in ((0, bb.qn), (1, bb.kn)):
                S.op("act", lambda e: e.activation(out=cv[:], in_=bb.xs[c][0][:], func=AF.Square),
                     reads=[bb.xs[c][1]], writes=[cvR])
                pn, pnR = C.psum.next()
                for q4 in range(4):
                    S.op("pe", lambda e: e.matmul(pn[:, q4 * 128:(q4 + 1) * 128], C.ones, cv[:, q4 * 128:(q4 + 1) * 128],
                                                  start=True, stop=True), reads=[cvR, C.constR], writes=[pnR])
                S.op("act", lambda e: e.activation(out=cv[:], in_=pn[:, 0:512], func=AF.Sqrt, bias=C.eps[:, 1:2]),
                     reads=[pnR, C.constR], writes=[cvR])
                S.op("dve", lambda e: e.reciprocal(cv[:], cv[:]), reads=[cvR], writes=[cvR])
                if c == 0:
                    S.op("dve", lambda e: e.scalar_tensor_tensor(dst[0][:], bb.xs[c][0][:], DH ** -0.5, cv[:],
                                                                 ALU.mult, ALU.mult),
                         reads=[bb.xs[c][1], cvR], writes=[dst[1]])
                else:
                    S.op("dve", lambda e: e.tensor_tensor(dst[0][:], bb.xs[c][0][:], cv[:], ALU.mult),
                         reads=[bb.xs[c][1], cvR], writes=[dst[1]])

        def tile_prep(b, j):
            bb = B[b]
            gt, gtR = bb.gt
            cs = slice(j * 128, (j + 1) * 128)
            kn, knR = bb.kn
            qn, qnR = bb.qn
            vs, vsR = bb.xs[2]
            T = Ctx()
            pt, ptR = C.psum.next()
            S.op("pe", lambda e: e.transpose(pt[:, 0:128], kn[:, cs], C.ident), reads=[knR, C.constR], writes=[ptR])
            S.op("pe", lambda e: e.transpose(pt[:, 128:256], vs[:, cs], C.ident), reads=[vsR, C.constR], writes=[ptR])
            T.kb = mats.next()
            T.kd = keep.next()
            T.vb = mats.next()
            S.op("act", lambda e: e.activation(out=T.kb[0][:], in_=pt[:, 0:128], func=AF.Identity,
                                               scale=gt[:, 20 + j:21 + j]), reads=[ptR, gtR], writes=[T.kb[1]])
            S.op("dve", lambda e: e.tensor_scalar(T.kd[0][:], pt[:, 0:128], gt[:, 16 + j:17 + j], None, ALU.mult),
                 reads=[ptR, gtR], writes=[T.kd[1]])
            S.op("dve", lambda e: e.tensor_scalar(T.vb[0][:], pt[:, 128:256], gt[:, j:j + 1], None, ALU.mult),
                 reads=[ptR, gtR], writes=[T.vb[1]])
            if PSUB < 1:
                return T
            pG, pGR = C.psum.next()
            S.op("pe", lambda e: e.matmul(pG[:, 0:128], kn[:, cs], kn[:, cs], start=True, stop=True),
                 reads=[knR], writes=[pGR])
            S.op("pe", lambda e: e.matmul(pG[:, 128:256], kn[:, cs], qn[:, cs], start=True, stop=True),
                 reads=[knR, qnR], writes=[pGR])
            if PSUB < 2:
                return T
            gU = mats.next()
            S.op("dve", lambda e: e.tensor_scalar(gU[0][:], C.U, gt[:, 4 + j:5 + j], None, ALU.mult),
                 reads=[gtR, C.constR], writes=[gU[1]])
            pR, pRR = C.psum.next()
            S.op("pe", lambda e: e.matmul(pR[:, 0:128], C.ones, gU[0][:], start=True, stop=True),
                 reads=[gU[1], C.constR], writes=[pRR])
            if PSUB < 3:
                return T
            t1 = mats.next()
            Dm = mats.next()
            t2 = mats.next()
            DmT = mats.next()
            E = mats.next()
            gc = gt[:, 8 + j:9 + j]
            S.op("dve", lambda e: e.tensor_scalar(t1[0][:], pR[:, 0:128], gc, 0.0, ALU.subtract, ALU.max),
                 reads=[pRR, gtR], writes=[t1[1]])
            S.op("act", lambda e: e.activation(out=Dm[0][:], in_=t1[0][:], func=AF.Exp, scale=-1.0),
                 reads=[t1[1]], writes=[Dm[1]])
            S.op("dve", lambda e: e.tensor_scalar(t2[0][:], pR[:, 0:128], gc, 0.0, ALU.subtract, ALU.min),
                 reads=[pRR, gtR], writes=[t2[1]])
            S.op("act", lambda e: e.activation(out=DmT[0][:], in_=t2[0][:], func=AF.Exp), reads=[t2[1]], writes=[DmT[1]])
            S.op("act", lambda e: e.activation(out=E[0][:], in_=pR[:, 0:128], func=AF.Exp), reads=[pRR], writes=[E[1]])
            if PSUB < 4:
                return T
            T.qd = keep.next()
            S.op("dve", lambda e: e.tensor_tensor(T.qd[0][:], qn[:, cs], E[0][:], ALU.mult),
                 reads=[qnR, E[1]], writes=[T.qd[1]])
            A = mats.next()
            S.op("dve", lambda e: e.tensor_tensor(t1[0][:], pG[:, 0:128], Dm[0][:], ALU.mult),
                 reads=[pGR, Dm[1]], writes=[t1[1]])
            S.op("dve", lambda e: e.scalar_tensor_tensor(A[0][:], t1[0][:], gt[:, j:j + 1], C.L, ALU.mult, ALU.mult),
                 reads=[t1[1], gtR, C.constR], writes=[A[1]])
            T.qk = keep.next()
            S.op("dve", lambda e: e.tensor_tensor(t2[0][:], pG[:, 128:256], DmT[0][:], ALU.mult),
                 reads=[pGR, DmT[1]], writes=[t2[1]])
            S.op("dve", lambda e: e.tensor_tensor(T.qk[0][:], t2[0][:], C.U, ALU.mult),
                 reads=[t2[1], C.constR], writes=[T.qk[1]])
            if PSUB < 5:
                return T
            pB, pBR = C.psum.next()
            S.op("pe", lambda e: e.transpose(pB[:, 0:128], A[0][:], C.ident), reads=[A[1], C.constR], writes=[pBR])
            Bm = mats.next()
            Y = mats.next()
            S.op("act", lambda e: e.activation(out=Bm[0][:], in_=pB[:, 0:128], func=AF.Copy), reads=[pBR], writes=[Bm[1]])
            S.op("dve", lambda e: e.tensor_tensor(Y[0][:], C.ident, pB[:, 0:128], ALU.subtract),
                 reads=[pBR, C.constR], writes=[Y[1]])
            if PSUB < 6:
                return T
            Ak, Bk = A, Bm
            for lvl in range(1, 7):
                pA, pAR = C.psum.next()
                S.op("pe", lambda e: e.matmul(pA[:, 0:128], Bk[0][:], Ak[0][:], start=True, stop=True),
                     reads=[Ak[1], Bk[1]], writes=[pAR])
                if lvl < 6:
                    S.op("pe", lambda e: e.matmul(pA[:, 128:256], Ak[0][:], Bk[0][:], start=True, stop=True),
                         reads=[Ak[1], Bk[1]], writes=[pAR])
                An = mats.next()
                S.op("act", lambda e: e.activation(out=An[0][:], in_=pA[:, 0:128], func=AF.Copy),
                     reads=[pAR], writes=[An[1]])
                if lvl < 6:
                    Bn = mats.next()
                    S.op("act", lambda e: e.activation(out=Bn[0][:], in_=pA[:, 128:256], func=AF.Copy),
                         reads=[pAR], writes=[Bn[1]])
                pY, pYR = C.psum.next()
                S.op("pe", lambda e: e.matmul(pY[:, 0:128], An[0][:], Y[0][:], start=True, stop=True),
                     reads=[An[1], Y[1]], writes=[pYR])
                Yn = mats.next()
                S.op("dve", lambda e: e.tensor_tensor(Yn[0][:], Y[0][:], pY[:, 0:128], ALU.add),
                     reads=[Y[1], pYR], writes=[Yn[1]])
                Y = Yn
                Ak = An
                if lvl < 6:
                    Bk = Bn
            if PSUB < 7:
                return T
            pu, puR = C.psum.next()
            S.op("pe", lambda e: e.matmul(pu[:, 0:128], Y[0][:], T.vb[0][:], start=True, stop=True),
                 reads=[Y[1], T.vb[1]], writes=[puR])
            S.op("pe", lambda e: e.matmul(pu[:, 128:256], T.kb[0][:], Y[0][:], start=True, stop=True),
                 reads=[Y[1], T.kb[1]], writes=[puR])
            T.u = keep.next()
            T.wT = keep.next()
            S.op("act", lambda e: e.activation(out=T.u[0][:], in_=pu[:, 0:128], func=AF.Copy), reads=[puR], writes=[T.u[1]])
            S.op("act", lambda e: e.activation(out=T.wT[0][:], in_=pu[:, 128:256], func=AF.Copy),
                 reads=[puR], writes=[T.wT[1]])
            return T

        def tile_rec(b, j, T, po):
            bb = B[b]
            gt, gtR = bb.gt
            St, SR = bb.S
            cs = slice(j * 128, (j + 1) * 128)
            pv, pvR = C.psum.next()
            S.op("pe", lambda e: e.matmul(pv[:, 0:128], T.wT[0][:], St[:], start=True, stop=True),
                 reads=[T.wT[1], SR], writes=[pvR])
            vn = mats.next()
            S.op("dve", lambda e: e.tensor_tensor(vn[0][:], T.u[0][:], pv[:, 0:128], ALU.subtract),
                 reads=[T.u[1], pvR], writes=[vn[1]])
            S.op("pe", lambda e: e.matmul(po[0][:, cs], St[:], T.qd[0][:], start=True, stop=False),
                 reads=[SR, T.qd[1]], writes=[po[1]])
            S.op("pe", lambda e: e.matmul(po[0][:, cs], vn[0][:], T.qk[0][:], start=False, stop=True),
                 reads=[vn[1], T.qk[1]], writes=[po[1]])
            pS, pSR = C.psum.next()
            S.op("pe", lambda e: e.matmul(pS[:, 0:128], T.kd[0][:], vn[0][:], start=True, stop=True),
                 reads=[T.kd[1], vn[1]], writes=[pSR])
            S.op("dve", lambda e: e.scalar_tensor_tensor(St[:], St[:], gt[:, 12 + j:13 + j], pS[:, 0:128],
                                                         ALU.mult, ALU.add),
                 reads=[SR, gtR, pSR], writes=[SR])

        def block_back(b, k, po):
            bb = B[b]
            cv, cvR = bb.cv
            ot, otR = bb.ot
            S.op("act", lambda e: e.activation(out=ot[:], in_=po[0][:, 0:512], func=AF.Copy), reads=[po[1]], writes=[otR])
            S.op("act", lambda e: e.activation(out=cv[:], in_=po[0][:, 0:512], func=AF.Square), reads=[po[1]], writes=[cvR])
            pn, pnR = C.psum.next()
            for q4 in range(4):
                S.op("pe", lambda e: e.matmul(pn[:, q4 * 128:(q4 + 1) * 128], C.ones, cv[:, q4 * 128:(q4 + 1) * 128],
                                              start=True, stop=True), reads=[cvR, C.constR], writes=[pnR])
            S.op("act", lambda e: e.activation(out=cv[:], in_=pn[:, 0:512], func=AF.Sqrt, bias=C.eps[:, 2:3],
                                               scale=1.0 / DH), reads=[pnR, C.constR], writes=[cvR])
            S.op("dve", lambda e: e.reciprocal(cv[:], cv[:]), reads=[cvR], writes=[cvR])
            S.op("dve", lambda e: e.tensor_tensor(ot[:], ot[:], cv[:], ALU.mult), reads=[otR, cvR], writes=[otR])
            og, ogR = bb.og.next()
            S.op("dve", lambda e: e.scalar_tensor_tensor(og[:], ot[:], hp[:, 14:15], bb.zs[0][:], ALU.mult, ALU.mult),
                 reads=[otR, hpR, bb.zs[1]], writes=[ogR])
            t0 = b * SEQ + k * 512
            S.dma("sp", og_dst[:, t0:t0 + 512], og[:], reads=[ogR])

        import os
        nblk = int(os.environ.get("P2_NBLK", SEQ // 512))
        stage = int(os.environ.get("P2_STAGE", 3))
        for k in range(nblk):
            pos = []
            for b in range(2):
                block_front(b, k)
                pos.append(opsum.next())
            for j in range(4):
                if stage >= 1:
                    Ts = [tile_prep(b, j) for b in range(2)]
                if stage >= 2:
                    for b in range(2):
                        tile_rec(b, j, Ts[b], pos[b])
            if stage >= 3:
                for b in range(2):
                    block_back(b, k, pos[b])
        S.barrier()


def _run_tasks(gens):
    gens = list(gens)
    while gens:
        for g in list(gens):
            try:
                next(g)
            except StopIteration:
                gens.remove(g)


def emit_phase2(C, dd, xg, og_dst):
    S, nc = C.S, C.nc
    NB = int(os.environ.get("P2_NBLK", SEQ // 512))
    with ExitStack() as es:
        def sb(name, shape, dt=F32):
            return es.enter_context(nc.sbuf_tensor(name, shape, dt))

        win = sb("win_sb", [128, 8, 514], BF16)
        winR = R("win")
        S.dma("pool", win[:], dd["win"].rearrange("(k p) c -> p k c", p=128), writes=[winR])
        hp = sb("hp_sb", [128, 16])
        hpR = R("hp")
        S.dma("sp", hp[:], dd["hp"], writes=[hpR])
        S.op("act", lambda e: e.activation(out=hp[:, 15:16], in_=hp[:, 12:13], func=AF.Exp), reads=[hpR], writes=[hpR])
        S.op("dve", lambda e: e.tensor_scalar(hp[:, 15:16], hp[:, 15:16], -1.0, None, ALU.mult),
             reads=[hpR], writes=[hpR])
        cb = sb("cb16", [128, 2, 128], BF16)
        cbR = R("cb16")
        S.op("dve", lambda e: e.tensor_copy(cb[:, 0, :], C.ident), reads=[C.constR], writes=[cbR])
        S.op("dve", lambda e: e.tensor_copy(cb[:, 1, :], C.ones), reads=[C.constR], writes=[cbR])
        identb, onesb = cb[:, 0, :], cb[:, 1, :]
        dg = sb("dg16", [128, 12, 128], BF16)
        dgR = R("dg16")
        for t in range(12):
            S.op("dve", lambda e: e.tensor_scalar(dg[:, t, :], C.ident, hp[:, t:t + 1], None, ALU.mult),
                 reads=[C.constR, hpR], writes=[dgR])
        m16 = Ring(es, nc, "m16_", [128, 2, 128], BF16, 22)
        ab16 = Ring(es, nc, "ab16_", [128, 2, 2, 128], BF16, 8)
        kT = Ring(es, nc, "kT_", [128, 2, 128], BF16, 28)
        ks = Ring(es, nc, "ks_", [128, 2, 128], BF16, 8)
        m32 = Ring(es, nc, "m32_", [128, 2, 128], F32, 8)
        u32 = Ring(es, nc, "u32_", [128, 2, 128], F32, 8)
        vn16 = Ring(es, nc, "vn16_", [128, 128], BF16, 8)
        cm2 = sb("cm2", [128, 4, 2, 128])
        cm2R = R("cm2")
        for mi, src in enumerate((C.Lbd, C.Loff, C.U, C.ident)):
            for b in range(2):
                S.op("dve", lambda e: e.tensor_copy(cm2[:, mi, b, :], src), reads=[C.constR], writes=[cm2R])
        pbf = Ring(es, nc, "pb", [128, 1024], BF16, 2, psum=True)
        B = []
        for b in range(2):
            bb = Ctx()
            bb.xb = (sb(f"xb{b}", [128, 8, 512], BF16), R("xb"))
            bb.pre = [(sb(f"pre{b}{c}", [128, 515], BF16), R("pre")) for c in range(3)]
            bb.xq = (sb(f"xq{b}", [128, 512]), R("xq"))
            bb.xk = (sb(f"xk{b}", [128, 512]), R("xk"))
            bb.cv = (sb(f"cv{b}", [128, 512]), R("cv"))
            bb.sq = (sb(f"sq{b}", [128, 512], BF16), R("sq"))
            bb.ot = (sb(f"ot{b}", [128, 512]), R("ot"))
            bb.S32 = (sb(f"S32{b}", [128, 128]), R("S32"))
            bb.S16 = (sb(f"S16{b}", [128, 128], BF16), R("S16"))
            bb.og = Ring(es, nc, f"og{b}", [128, 512], BF16, 2)
            bb.F = []
            for g in range(3):
                f = Ctx()
                f.zs = (sb(f"zs{b}{g}", [128, 512], BF16), R("zs"))
                f.gt = (sb(f"gt{b}{g}", [128, 32]), R("gt"))
                bb.F.append(f)
            bb.G = []
            for g in range(2):
                f = Ctx()
                f.qn = (sb(f"qn{b}{g}", [128, 512], BF16), R("qn"))
                f.kn = (sb(f"kn{b}{g}", [128, 512], BF16), R("kn"))
                f.vs = (sb(f"vs{b}{g}", [128, 512], BF16), R("vs"))
                bb.G.append(f)
            S.op("dve", lambda e: e.memset(bb.S32[0][:], 0.0), writes=[bb.S32[1]])
            S.op("dve", lambda e: e.memset(bb.S16[0][:], 0.0), writes=[bb.S16[1]])
            for c in range(3):
                S.op("dve", lambda e: e.memset(bb.pre[c][0][:, 0:3], 0.0), writes=[bb.pre[c][1]])
            B.append(bb)

        def load_x(b, k):
            rank = 4 * b + k // 4
            off = (k % 4) * 512
            src_ap, src_reads = xg(rank, k % 4)
            S.dma("sp", B[b].xb[0][:], src_ap.rearrange("f p t -> p f t"), reads=src_reads, writes=[B[b].xb[1]])

        def front(b, k):
            bb = B[b]
            xb, xbR = bb.xb
            F = bb.F[k % 3]
            G = bb.G[k % 2]
            gt, gtR = F.gt
            for c in range(4):
                ps, psR = yield from C.psum.acq()
                for kc in range(8):
                    S.op("pe", lambda e: e.matmul(ps[:, 0:512], win[:, kc, c * 128:(c + 1) * 128], xb[:, kc, :],
                                                  start=(kc == 0), stop=(kc == 7)), reads=[winR, xbR], writes=[psR])
                if c < 3:
                    S.op("act", lambda e: e.activation(out=bb.pre[c][0][:, 3:515], in_=ps[:, 0:512], func=AF.Copy),
                         reads=[psR], writes=[bb.pre[c][1]])
                else:
                    cvz, cvzR = bb.cv
                    S.op("act", lambda e: e.activation(out=cvz[:], in_=ps[:, 0:512], func=AF.Exp, scale=-1.0),
                         reads=[psR], writes=[cvzR])
                    yield
                    S.op("act", lambda e: e.activation(out=cvz[:], in_=cvz[:], func=AF.Ln, bias=1.0),
                         reads=[cvzR], writes=[cvzR])
                    yield
                    S.op("act", lambda e: e.activation(out=cvz[:], in_=cvz[:], func=AF.Exp, scale=-1.0),
                         reads=[cvzR], writes=[cvzR])
                    yield
                    S.op("dve", lambda e: e.tensor_tensor(F.zs[0][:], ps[:, 0:512], cvz[:], ALU.mult),
                         reads=[psR, cvzR], writes=[F.zs[1]])
                C.psum.rel((ps, psR))
                yield
            pg, pgR = yield from C.psum.acq()
            for j in range(4):
                for kc in range(8):
                    S.op("pe", lambda e: e.matmul(pg[:, 2 * j:2 * j + 2], xb[:, kc, j * 128:(j + 1) * 128],
                                                  win[:, kc, 512:514], start=(kc == 0), stop=(kc == 7)),
                         reads=[winR, xbR], writes=[pgR])
            if k + 1 < NB:
                load_x(b, k + 1)
            yield
            S.op("act", lambda e: e.activation(out=gt[:, 0:4], in_=pg[:, 0:8].rearrange("p (j c) -> p j c", c=2)[:, :, 0],
                                               func=AF.Exp, scale=-1.0), reads=[pgR], writes=[gtR])
            S.op("act", lambda e: e.activation(out=gt[:, 24:28],
                                               in_=pg[:, 0:8].rearrange("p (j c) -> p j c", c=2)[:, :, 1],
                                               func=AF.Exp, bias=hp[:, 13:14]), reads=[pgR, hpR], writes=[gtR])
            C.psum.rel((pg, pgR))
            yield
            S.op("act", lambda e: e.activation(out=gt[:, 24:28], in_=gt[:, 24:28], func=AF.Ln, bias=1.0),
                 reads=[gtR], writes=[gtR])
            S.op("dve", lambda e: e.tensor_scalar(gt[:, 0:4], gt[:, 0:4], 1.0, None, ALU.add), reads=[gtR], writes=[gtR])
            S.op("dve", lambda e: e.reciprocal(gt[:, 0:4], gt[:, 0:4]), reads=[gtR], writes=[gtR])
            yield
            S.op("dve", lambda e: e.tensor_scalar(gt[:, 4:8], gt[:, 24:28], hp[:, 15:16], None, ALU.mult),
                 reads=[gtR, hpR], writes=[gtR])
            yield
            pc, pcR = yield from C.psum.acq()
            S.op("pe", lambda e: e.matmul(pc[:, 0:4], C.U, gt[:, 4:8], start=True, stop=True),
                 reads=[gtR, C.constR], writes=[pcR])
            S.op("pe", lambda e: e.matmul(pc[:, 4:8], C.ones, gt[:, 4:8], start=True, stop=True),
                 reads=[gtR, C.constR], writes=[pcR])
            yield
            S.op("dve", lambda e: e.tensor_copy(gt[:, 8:12], pc[:, 0:4]), reads=[pcR], writes=[gtR])
            S.op("dve", lambda e: e.tensor_tensor(gt[:, 24:28], pc[:, 4:8], gt[:, 8:12], ALU.subtract),
                 reads=[pcR, gtR], writes=[gtR])
            S.op("act", lambda e: e.activation(out=gt[:, 12:16], in_=pc[:, 4:8], func=AF.Exp), reads=[pcR], writes=[gtR])
            S.op("act", lambda e: e.activation(out=gt[:, 20:24], in_=pc[:, 0:4], func=AF.Exp), reads=[pcR], writes=[gtR])
            C.psum.rel((pc, pcR))
            yield
            S.op("act", lambda e: e.activation(out=gt[:, 16:20], in_=gt[:, 24:28], func=AF.Exp), reads=[gtR], writes=[gtR])
            S.op("dve", lambda e: e.tensor_tensor(gt[:, 20:24], gt[:, 20:24], gt[:, 0:4], ALU.mult),
                 reads=[gtR], writes=[gtR])
            yield
            dsts = (bb.xq, bb.xk, G.vs)
            scr = (bb.xq, bb.xk, bb.xq)
            for c in (2, 0, 1):
                pre, preR = bb.pre[c]
                pcv, pcvR = yield from C.psum.acq()
                for j in range(4):
                    S.op("pe", lambda e: e.matmul(pcv[:, 0:512], dg[:, 4 * c + j, :], pre[:, j:j + 512],
                                                  start=(j == 0), stop=(j == 3)), reads=[dgR, preR], writes=[pcvR])
                S.op("dve", lambda e: e.tensor_copy(pre[:, 0:3], pre[:, 512:515]), reads=[preR], writes=[preR])
                yield
                sg, sgR = scr[c]
                S.op("act", lambda e: e.activation(out=sg[:], in_=pcv[:, 0:512], func=AF.Exp, scale=-1.0),
                     reads=[pcvR], writes=[sgR])
                yield
                S.op("act", lambda e: e.activation(out=sg[:], in_=sg[:], func=AF.Ln, bias=1.0), reads=[sgR], writes=[sgR])
                yield
                S.op("act", lambda e: e.activation(out=sg[:], in_=sg[:], func=AF.Exp, scale=-1.0), reads=[sgR], writes=[sgR])
                yield
                S.op("dve", lambda e: e.tensor_tensor(dsts[c][0][:], pcv[:, 0:512], sg[:], ALU.mult),
                     reads=[pcvR, sgR], writes=[dsts[c][1]])
                C.psum.rel((pcv, pcvR))
                yield
            cv, cvR = bb.cv
            sq, sqR = bb.sq
            for c, src, dst in ((0, bb.xq, G.qn), (1, bb.xk, G.kn)):
                S.op("act", lambda e: e.activation(out=sq[:], in_=src[0][:], func=AF.Square), reads=[src[1]], writes=[sqR])
                yield
                pn, pnR = yield from C.psum.acq()
                S.op("pe", lambda e: e.matmul(pn[:, 0:512], onesb, sq[:], start=True, stop=True),
                     reads=[sqR, cbR], writes=[pnR])
                yield
                S.op("act", lambda e: e.activation(out=cv[:], in_=pn[:, 0:512], func=AF.Ln, bias=C.eps[:, 1:2]),
                     reads=[pnR, C.constR], writes=[cvR])
                C.psum.rel((pn, pnR))
                yield
                S.op("act", lambda e: e.activation(out=cv[:], in_=cv[:], func=AF.Exp, scale=-0.5), reads=[cvR], writes=[cvR])
                yield
                if c == 0:
                    S.op("dve", lambda e: e.scalar_tensor_tensor(dst[0][:], src[0][:], DH ** -0.5, cv[:],
                                                                 ALU.mult, ALU.mult),
                         reads=[src[1], cvR], writes=[dst[1]])
                else:
                    S.op("dve", lambda e: e.tensor_tensor(dst[0][:], src[0][:], cv[:], ALU.mult),
                         reads=[src[1], cvR], writes=[dst[1]])
                yield

        Tres = {}

        def prep_pair(k, j):
            cs = slice(j * 128, (j + 1) * 128)
            Fs = [B[b].F[k % 3] for b in range(2)]
            Gs = [B[b].G[k % 2] for b in range(2)]
            gts = [f.gt for f in Fs]
            Ts = []
            for b in range(2):
                T = Ctx()
                Tres[(b, k, j)] = T
                Ts.append(T)
            ptb, ptR = yield from pbf.acq()
            for b in range(2):
                S.op("pe", lambda e: e.transpose(ptb[:, b * 128:(b + 1) * 128], Gs[b].kn[0][:, cs], identb),
                     reads=[Gs[b].kn[1], cbR], writes=[ptR])
                S.op("pe", lambda e: e.transpose(ptb[:, 256 + b * 128:256 + (b + 1) * 128], Gs[b].vs[0][:, cs], identb),
                     reads=[Gs[b].vs[1], cbR], writes=[ptR])
            gU = m32.next()
            for b in range(2):
                S.op("dve", lambda e: e.tensor_scalar(gU[0][:, b, :], C.U, gts[b][0][:, 4 + j:5 + j], None, ALU.mult),
                     reads=[gts[b][1], C.constR], writes=[gU[1]])
            yield
            kb = ks.next()
            kd = kT.next()
            vb = ks.next()
            for b in range(2):
                gt, gtR = gts[b]
                S.op("act", lambda e: e.activation(out=kb[0][:, b, :], in_=ptb[:, b * 128:(b + 1) * 128], func=AF.Identity,
                                                   scale=gt[:, 20 + j:21 + j]), reads=[ptR, gtR], writes=[kb[1]])
                S.op("act", lambda e: e.activation(out=kd[0][:, b, :], in_=ptb[:, b * 128:(b + 1) * 128], func=AF.Identity,
                                                   scale=gt[:, 16 + j:17 + j]), reads=[ptR, gtR], writes=[kd[1]])
                S.op("dve", lambda e: e.tensor_scalar(vb[0][:, b, :], ptb[:, 256 + b * 128:256 + (b + 1) * 128],
                                                      gt[:, j:j + 1], None, ALU.mult),
                     reads=[ptR, gtR], writes=[vb[1]])
            pbf.rel((ptb, ptR))
            pR, pRR = yield from C.psum.acq()
            for b in range(2):
                S.op("pe", lambda e: e.matmul(pR[:, b * 128:(b + 1) * 128], C.ones, gU[0][:, b, :], start=True, stop=True),
                     reads=[gU[1], C.constR], writes=[pRR])
            yield
            t1 = m32.next()
            t2 = m32.next()
            E = m16.next()
            for b in range(2):
                gt, gtR = gts[b]
                gc = gt[:, 8 + j:9 + j]
                S.op("dve", lambda e: e.tensor_scalar(t1[0][:, b, :], pR[:, b * 128:(b + 1) * 128], gc, 0.0,
                                                      ALU.subtract, ALU.max), reads=[pRR, gtR], writes=[t1[1]])
                S.op("dve", lambda e: e.tensor_scalar(t2[0][:, b, :], pR[:, b * 128:(b + 1) * 128], gc, 0.0,
                                                      ALU.subtract, ALU.min), reads=[pRR, gtR], writes=[t2[1]])
            S.op("act", lambda e: e.activation(out=E[0][:].rearrange("p b c -> p (b c)"), in_=pR[:, 0:256], func=AF.Exp),
                 reads=[pRR], writes=[E[1]])
            C.psum.rel((pR, pRR))
            yield
            Dm = m16.next()
            DmT = m16.next()
            S.op("act", lambda e: e.activation(out=Dm[0][:], in_=t1[0][:], func=AF.Exp, scale=-1.0),
                 reads=[t1[1]], writes=[Dm[1]])
            S.op("act", lambda e: e.activation(out=DmT[0][:], in_=t2[0][:], func=AF.Exp), reads=[t2[1]], writes=[DmT[1]])
            qd = kT.next()
            for b in range(2):
                S.op("dve", lambda e: e.tensor_tensor(qd[0][:, b, :], Gs[b].qn[0][:, cs], E[0][:, b, :], ALU.mult),
                     reads=[Gs[b].qn[1], E[1]], writes=[qd[1]])
            pG, pGR = yield from C.psum.acq()
            for b in range(2):
                kn, knR = Gs[b].kn
                qn, qnR = Gs[b].qn
                S.op("pe", lambda e: e.matmul(pG[:, b * 256:b * 256 + 128], kn[:, cs], kn[:, cs], start=True, stop=True),
                     reads=[knR], writes=[pGR])
                S.op("pe", lambda e: e.matmul(pG[:, b * 256 + 128:b * 256 + 256], kn[:, cs], qn[:, cs],
                                              start=True, stop=True), reads=[knR, qnR], writes=[pGR])
            yield
            a0 = m32.next()
            for b in range(2):
                gt, gtR = gts[b]
                S.op("dve", lambda e: e.scalar_tensor_tensor(a0[0][:, b, :], pG[:, b * 256:b * 256 + 128], gt[:, j:j + 1],
                                                             Dm[0][:, b, :], ALU.mult, ALU.mult),
                     reads=[pGR, gtR, Dm[1]], writes=[a0[1]])
            q0 = m16.next()
            S.op("dve", lambda e: e.tensor_tensor(q0[0][:], pG[:, 0:512].rearrange("p (b h c) -> p b h c", b=2, h=2)[:, :, 1, :],
                                                  DmT[0][:], ALU.mult), reads=[pGR, DmT[1]], writes=[q0[1]])
            C.psum.rel((pG, pGR))
            yield
            A = m16.next()
            Aoff = ks.next()
            qk = kT.next()
            S.op("dve", lambda e: e.tensor_tensor(A[0][:], a0[0][:], cm2[:, 0], ALU.mult),
                 reads=[a0[1], cm2R], writes=[A[1]])
            S.op("dve", lambda e: e.tensor_tensor(Aoff[0][:], a0[0][:], cm2[:, 1], ALU.mult),
                 reads=[a0[1], cm2R], writes=[Aoff[1]])
            S.op("dve", lambda e: e.tensor_tensor(qk[0][:], q0[0][:], cm2[:, 2], ALU.mult),
                 reads=[q0[1], cm2R], writes=[qk[1]])
            yield
            pBb, pBR = yield from pbf.acq()
            for b in range(2):
                S.op("pe", lambda e: e.transpose(pBb[:, b * 128:(b + 1) * 128], A[0][:, b, :], identb),
                     reads=[A[1], cbR], writes=[pBR])
            yield
            Bm = m16.next()
            Y = m16.next()
            S.op("act", lambda e: e.activation(out=Bm[0][:].rearrange("p b c -> p (b c)"), in_=pBb[:, 0:256], func=AF.Copy),
                 reads=[pBR], writes=[Bm[1]])
            S.op("dve", lambda e: e.tensor_tensor(Y[0][:].rearrange("p b c -> p (b c)"),
                                                  cm2[:, 3].rearrange("p b c -> p (b c)"), pBb[:, 0:256], ALU.subtract),
                 reads=[pBR, cm2R], writes=[Y[1]])
            pbf.rel((pBb, pBR))
            yield
            NL = 5

            def squares(pA, pAR, Ak, Bk, ABR, with_b):
                for b in range(2):
                    S.op("pe", lambda e: e.matmul(pA[:, b * 256:b * 256 + 128], Bk(b), Ak(b), start=True, stop=True),
                         reads=ABR, writes=[pAR])
                    if with_b:
                        S.op("pe", lambda e: e.matmul(pA[:, b * 256 + 128:b * 256 + 256], Ak(b), Bk(b),
                                                      start=True, stop=True), reads=ABR, writes=[pAR])

            def evac(pA, pAR, with_b):
                AB = ab16.next()
                if with_b:
                    S.op("act", lambda e: e.activation(out=AB[0][:].rearrange("p b h c -> p (b h c)"), in_=pA[:, 0:512],
                                                       func=AF.Copy), reads=[pAR], writes=[AB[1]])
                else:
                    S.op("act", lambda e: e.activation(
                        out=AB[0][:, :, 0, :], in_=pA[:, 0:512].rearrange("p (b h c) -> p b h c", b=2, h=2)[:, :, 0, :],
                        func=AF.Copy), reads=[pAR], writes=[AB[1]])
                C.psum.rel((pA, pAR))
                return AB

            pA, pAR = yield from C.psum.acq()
            squares(pA, pAR, lambda b: A[0][:, b, :], lambda b: Bm[0][:, b, :], [A[1], Bm[1]], NL > 1)
            yield
            AB = evac(pA, pAR, NL > 1)
            yield
            for lvl in range(1, NL + 1):
                An = (lambda AB_: (lambda b: AB_[0][:, b, 0, :]))(AB)
                Bn = (lambda AB_: (lambda b: AB_[0][:, b, 1, :]))(AB)
                pY, pYR = yield from C.psum.acq()
                for b in range(2):
                    S.op("pe", lambda e: e.matmul(pY[:, b * 128:(b + 1) * 128], An(b), Y[0][:, b, :], start=True, stop=True),
                         reads=[AB[1], Y[1]], writes=[pYR])
                if lvl < NL:
                    pA, pAR = yield from C.psum.acq()
                    squares(pA, pAR, An, Bn, [AB[1]], lvl + 1 < NL)
                yield
                Yn = m16.next()
                S.op("dve", lambda e: e.tensor_tensor(Yn[0][:].rearrange("p b c -> p (b c)"),
                                                      Y[0][:].rearrange("p b c -> p (b c)"), pY[:, 0:256], ALU.add),
                     reads=[Y[1], pYR], writes=[Yn[1]])
                C.psum.rel((pY, pYR))
                Y = Yn
                if lvl < NL:
                    AB = evac(pA, pAR, lvl + 1 < NL)
                yield
            pZb, pZR = yield from pbf.acq()
            pT, pTR = yield from C.psum.acq()
            for b in range(2):
                S.op("pe", lambda e: e.transpose(pZb[:, b * 128:(b + 1) * 128], Y[0][:, b, :], identb),
                     reads=[Y[1], cbR], writes=[pZR])
                S.op("pe", lambda e: e.matmul(pT[:, b * 128:(b + 1) * 128], Aoff[0][:, b, :], Y[0][:, b, :],
                                              start=True, stop=True), reads=[Aoff[1], Y[1]], writes=[pTR])
            yield
            Zd = m16.next()
            T1 = m16.next()
            S.op("act", lambda e: e.activation(out=Zd[0][:].rearrange("p b c -> p (b c)"), in_=pZb[:, 0:256], func=AF.Copy),
                 reads=[pZR], writes=[Zd[1]])
            S.op("act", lambda e: e.activation(out=T1[0][:].rearrange("p b c -> p (b c)"), in_=pT[:, 0:256], func=AF.Copy),
                 reads=[pTR], writes=[T1[1]])
            pbf.rel((pZb, pZR))
            C.psum.rel((pT, pTR))
            yield
            pY, pYR = yield from C.psum.acq()
            for b in range(2):
                S.op("pe", lambda e: e.matmul(pY[:, b * 128:(b + 1) * 128], Zd[0][:, b, :], T1[0][:, b, :],
                                              start=True, stop=True), reads=[Zd[1], T1[1]], writes=[pYR])
            yield
            Yf = m16.next()
            S.op("dve", lambda e: e.tensor_tensor(Yf[0][:].rearrange("p b c -> p (b c)"),
                                                  Y[0][:].rearrange("p b c -> p (b c)"), pY[:, 0:256], ALU.subtract),
                 reads=[Y[1], pYR], writes=[Yf[1]])
            C.psum.rel((pY, pYR))
            yield
            pu, puR = yield from C.psum.acq()
            for b in range(2):
                S.op("pe", lambda e: e.matmul(pu[:, b * 256:b * 256 + 128], Yf[0][:, b, :], vb[0][:, b, :],
                                              start=True, stop=True), reads=[Yf[1], vb[1]], writes=[puR])
                S.op("pe", lambda e: e.matmul(pu[:, b * 256 + 128:b * 256 + 256], kb[0][:, b, :], Yf[0][:, b, :],
                                              start=True, stop=True), reads=[Yf[1], kb[1]], writes=[puR])
            yield
            u = u32.next()
            wT = kT.next()
            puv = pu[:, 0:512].rearrange("p (b h c) -> p b h c", b=2, h=2)
            S.op("act", lambda e: e.activation(out=u[0][:], in_=puv[:, :, 0, :], func=AF.Copy), reads=[puR], writes=[u[1]])
            S.op("act", lambda e: e.activation(out=wT[0][:], in_=puv[:, :, 1, :], func=AF.Copy), reads=[puR], writes=[wT[1]])
            C.psum.rel((pu, puR))
            for b in range(2):
                T = Ts[b]
                T.kd = (kd[0][:, b, :], kd[1])
                T.qd = (qd[0][:, b, :], qd[1])
                T.qk = (qk[0][:, b, :], qk[1])
                T.u = (u[0][:, b, :], u[1])
                T.wT = (wT[0][:, b, :], wT[1])
            yield

        def rec_chain(b, k):
            bb = B[b]
            F = bb.F[k % 3]
            gt, gtR = F.gt
            S32, S32R = bb.S32
            S16, S16R = bb.S16
            ot, otR = bb.ot
            for j in range(4):
                T = Tres.pop((b, k, j))
                cs = slice(j * 128, (j + 1) * 128)
                pv, pvR = yield from C.psum.acq()
                S.op("pe", lambda e: e.matmul(pv[:, 0:128], T.wT[0], S16[:], start=True, stop=True),
                     reads=[T.wT[1], S16R], writes=[pvR])
                yield
                vn = vn16.next()
                S.op("dve", lambda e: e.tensor_tensor(vn[0][:], T.u[0], pv[:, 0:128], ALU.subtract),
                     reads=[T.u[1], pvR], writes=[vn[1]])
                C.psum.rel((pv, pvR))
                yield
                pS, pSR = yield from C.psum.acq()
                S.op("pe", lambda e: e.matmul(pS[:, 0:128], T.kd[0], vn[0][:], start=True, stop=True),
                     reads=[T.kd[1], vn[1]], writes=[pSR])
                S.op("pe", lambda e: e.matmul(pS[:, 128:256], S16[:], T.qd[0], start=True, stop=False),
                     reads=[S16R, T.qd[1]], writes=[pSR])
                S.op("pe", lambda e: e.matmul(pS[:, 128:256], vn[0][:], T.qk[0], start=False, stop=True),
                     reads=[vn[1], T.qk[1]], writes=[pSR])
                yield
                gl = gt[:, 12 + j:13 + j]
                S.op("dve", lambda e: e.scalar_tensor_tensor(S16[:], S32[:], gl, pS[:, 0:128], ALU.mult, ALU.add),
                     reads=[S32R, gtR, pSR], writes=[S16R])
                S.op("dve", lambda e: e.scalar_tensor_tensor(S32[:], S32[:], gl, pS[:, 0:128], ALU.mult, ALU.add),
                     reads=[S32R, gtR, pSR], writes=[S32R])
                S.op("dve", lambda e: e.tensor_copy(ot[:, cs], pS[:, 128:256]), reads=[pSR], writes=[otR])
                C.psum.rel((pS, pSR))
                yield
            sq, sqR = bb.sq
            cv, cvR = bb.cv
            S.op("act", lambda e: e.activation(out=sq[:], in_=ot[:], func=AF.Square), reads=[otR], writes=[sqR])
            yield
            pn, pnR = yield from C.psum.acq()
            S.op("pe", lambda e: e.matmul(pn[:, 0:512], onesb, sq[:], start=True, stop=True),
                 reads=[sqR, cbR], writes=[pnR])
            yield
            S.op("act", lambda e: e.activation(out=cv[:], in_=pn[:, 0:512], func=AF.Ln, bias=C.eps[:, 2:3],
                                               scale=1.0 / DH), reads=[pnR, C.constR], writes=[cvR])
            C.psum.rel((pn, pnR))
            yield
            S.op("act", lambda e: e.activation(out=cv[:], in_=cv[:], func=AF.Exp, scale=-0.5), reads=[cvR], writes=[cvR])
            yield
            S.op("dve", lambda e: e.tensor_tensor(ot[:], ot[:], cv[:], ALU.mult), reads=[otR, cvR], writes=[otR])
            yield
            og, ogR = bb.og.next()
            S.op("dve", lambda e: e.scalar_tensor_tensor(og[:], ot[:], hp[:, 14:15], F.zs[0][:], ALU.mult, ALU.mult),
                 reads=[otR, hpR, F.zs[1]], writes=[ogR])
            S.dma("sp", og_dst.dst(b, k), og[:], reads=[ogR])
            og_dst.done(b, k, S.last_dma)
            yield

        for b in range(2):
            load_x(b, 0)
        for it in range(NB + 2):
            tasks = []
            if 0 <= it - 2 < NB:
                tasks += [rec_chain(b, it - 2) for b in range(2)]
            if 0 <= it - 1 < NB:
                tasks += [prep_pair(it - 1, j) for j in range(2)]
            _run_tasks(tasks)
            if 0 <= it - 2 < NB and (it - 2) % 4 == 3:
                og_dst.group_done((it - 2) // 4)
            tasks = []
            if it < NB:
                tasks += [front(b, it) for b in range(2)]
            if 0 <= it - 1 < NB:
                tasks += [prep_pair(it - 1, j) for j in range(2, 4)]
            _run_tasks(tasks)
        S.barrier(skip_cc=True)


def _band_mats(first_segment):
    band = np.zeros((128, 4, 3, 128), np.float32)
    tp = np.arange(128)[:, None]
    t = np.arange(128)[None, :]
    for g, w in enumerate((2, 4, 8, 16)):
        own = ((t - tp >= 0) & (t - tp < w)).astype(np.float32) / w - (t == tp).astype(np.float32)
        prev = ((t + 128 - tp) < w).astype(np.float32) / w
        band[:, g, 0, :] = prev
        band[:, g, 2, :] = own
        if first_segment:
            cnt = np.minimum(t + 1, w).astype(np.float32)
            band[:, g, 1, :] = ((t - tp >= 0) & (t - tp < w)).astype(np.float32) / cnt - (t == tp).astype(np.float32)
        else:
            band[:, g, 1, :] = own
    return band


def _cmat():
    c = np.zeros((128, 6, 128), np.float32)
    i = np.arange(128)[:, None]
    j = np.arange(128)[None, :]
    c[:, 0, :] = (i == j)
    c[:, 1, :] = 1.0
    c[:, 2, :] = (i <= j)
    c[:, 3, :] = (i > j)
    c[:, 4, :] = (i > j) & ((i // 64) == (j // 64))
    c[:, 5, :] = (i >= 64) & (j < 64)
    return c


def _dram_in(nc, name, shape, dt=F32):
    return nc.dram_tensor(name, list(shape), dt, kind="ExternalInput")


W_NAMES = ("pool_w", "mlp_w10", "mlp_w20", "ple_gate_w0", "ple_proj0",
           "gdn_w_out", "mlp_w11", "mlp_w21", "ple_gate_w1", "ple_proj1")
W_SHAPES = {"pool_w": (4, 256, 256), "mlp_w10": (D, DFF), "mlp_w20": (DFF, D), "ple_gate_w0": (D, D),
            "ple_proj0": (256, D), "gdn_w_out": (D, D), "mlp_w11": (D, DFF), "mlp_w21": (DFF, D),
            "ple_gate_w1": (D, D), "ple_proj1": (256, D)}


def build(mode):
    nc = bass.Bass("TRN2", target_bir_lowering=False)
    dd = {}
    with ExitStack() as es:
        S = Sched(nc, es)
        vecs_h = _dram_in(nc, "vecs", (12, D))
        cmat_h = _dram_in(nc, "cmat", (128, 6, 128))
        C = setup_common(nc, es, S, vecs_h, cmat_h.ap(), tok=(mode != "p2"))
        if mode in ("p1", "fused"):
            for n in ("pool_w", "mlp_w10", "mlp_w20", "ple_gate_w0", "ple_proj0"):
                dd[n] = _dram_in(nc, n, W_SHAPES[n]).ap()
            dd["x"] = _dram_in(nc, "x", (128 + TOK, D)).ap()
            dd["p0"] = _dram_in(nc, "p0", (TOK, 256)).ap()
            dd["band"] = _dram_in(nc, "band", (128, 4, 3, 128)).ap()
        if mode in ("p2", "fused"):
            dd["win"] = _dram_in(nc, "win", (D, 514)).ap()
            dd["hp"] = _dram_in(nc, "hp", (128, 16)).ap()
        if mode in ("p3", "fused"):
            for n in ("gdn_w_out", "mlp_w11", "mlp_w21", "ple_gate_w1", "ple_proj1"):
                dd[n] = _dram_in(nc, n, W_SHAPES[n]).ap()
            dd["p1"] = _dram_in(nc, "p1", (TOK, 256)).ap()
        if mode == "p1":
            accout = nc.dram_tensor("accout", [TOK, D], F32, kind="ExternalOutput").ap()
            x1T = nc.dram_tensor("x1T", [8, 128, TOK], BF16, kind="ExternalOutput").ap()

            def export_q(q):
                S.dma("sp", x1T[:, :, q * 512:(q + 1) * 512].rearrange("f p t -> p f t"),
                      C.xT[:, :, q * 512:(q + 1) * 512], reads=C.xTR[4 * q:4 * q + 4])
            emit_phase1(C, dd, export_q)
            for i in range(NT):
                S.dma("sp", accout[i * 128:(i + 1) * 128, :], C.acc[:, i, :], reads=[C.accR[i]])
        elif mode == "p2":
            xg = _dram_in(nc, "xg", (8, 8, 128, TOK), BF16).ap()
            og = nc.dram_tensor("og", [128, 2 * SEQ], BF16, kind="ExternalOutput").ap()

            class OGp2:
                def dst(self, b, k):
                    t0 = b * SEQ + k * 512
                    return og[:, t0:t0 + 512]

                def done(self, b, k, dep):
                    pass

                def group_done(self, g):
                    pass
            emit_phase2(C, dd, lambda rank, q: (xg[rank, :, :, q * 512:(q + 1) * 512], []), OGp2())
        elif mode == "fused":
            gidx_d = _dram_in(nc, "gidx", (128, 32), I32).ap()
            out = nc.dram_tensor("out", [TOK, D], F32, kind="ExternalOutput").ap()
            RG = [list(range(NCORES))]
            x1s = [nc.dram_tensor(f"x1T_src{q}", [8 * 128, 512], BF16).ap() for q in range(4)]
            xgs = [nc.dram_tensor(f"xg_all{q}", [8 * 8 * 128, 512], BF16, addr_space="Shared").ap() for q in range(4)]
            ogs = [nc.dram_tensor(f"og_src{g}", [128, 2 * TOK], BF16).ap() for g in range(4)]
            ogas = [nc.dram_tensor(f"og_all{g}", [8 * 128, 2 * TOK], BF16, addr_space="Shared").ap() for g in range(4)]
            x1R = [R(f"x1T_src{q}") for q in range(4)]
            xgR = [R(f"xg_all{q}") for q in range(4)]
            ogaR = [R(f"og_all{g}") for g in range(4)]
            gi = es.enter_context(nc.sbuf_tensor("gidx_sb", [128, 32], I32))
            giR = R("gidx")
            S.dma("sp", gi[:], gidx_d, writes=[giR])

            def export_q(q):
                S.dma("sp", x1s[q].rearrange("(f p) t -> p f t", p=128), C.xT[:, :, q * 512:(q + 1) * 512],
                      reads=C.xTR[4 * q:4 * q + 4], writes=[x1R[q]])
                S.collective(lambda e: e.collective_compute("AllGather", ALU.bypass, replica_groups=RG,
                                                            ins=[x1s[q].opt()], outs=[xgs[q].opt()]),
                             reads=[x1R[q]], writes=[xgR[q]])
            emit_phase1(C, dd, export_q)

            class OGf:
                def __init__(self):
                    self.deps = [dict() for _ in range(4)]

                def dst(self, b, k):
                    c0 = b * TOK + (k % 4) * 512
                    return ogs[k // 4][:, c0:c0 + 512]

                def done(self, b, k, dep):
                    src, v = dep
                    d = self.deps[k // 4]
                    d[src] = max(d.get(src, 0), v)

                def group_done(self, g):
                    S.collective(lambda e: e.collective_compute("AllGather", ALU.bypass, replica_groups=RG,
                                                                ins=[ogs[g].opt()], outs=[ogas[g].opt()]),
                                 writes=[ogaR[g]], extra=self.deps[g])

            def xg_blk(rank, q):
                return xgs[q].rearrange("(r f p) t -> r f p t", r=8, f=8)[rank], [xgR[q]]
            emit_phase2(C, dd, xg_blk, OGf())

            def load_og(C):
                for g in range(4):
                    rows = ogas[g].rearrange("r (c t) -> (r c) t", t=TOK)
                    for h in range(8):
                        S.dma("pool", None, None, reads=[ogaR[g], giR], writes=C.xTR,
                              fn=lambda e: e.indirect_dma_start(
                                  out=C.xT[:, h, :], out_offset=None, in_=rows,
                                  in_offset=bass.IndirectOffsetOnAxis(ap=gi[:, g * 8 + h:g * 8 + h + 1], axis=0),
                                  bounds_check=8 * 128 * 2 - 1, oob_is_err=False))
            emit_phase3(C, dd, load_og, out)
        elif mode == "p3":
            accin = _dram_in(nc, "accin", (TOK, D)).ap()
            ogin = _dram_in(nc, "ogin", (8, 128, TOK), BF16).ap()
            out = nc.dram_tensor("out", [TOK, D], F32, kind="ExternalOutput").ap()
            for i in range(NT):
                S.dma("sp", C.acc[:, i, :], accin[i * 128:(i + 1) * 128, :], writes=[C.accR[i]])

            def load_og(C):
                for h in range(8):
                    S.dma("sp", C.xT[:, h, :], ogin[h], writes=C.xTR)
            emit_phase3(C, dd, load_og, out)
        S.finish()
    return nc


def run_fused(x, p, ln_gain, ln_bias, pool_w, pool_b, pool_scale, gdn_w_in, gdn_conv, gdn_a_log, gdn_dt_bias,
              gdn_norm_w, gdn_w_out, mlp_w1, mlp_w2, ple_gate_w, ple_gate_b, ple_proj):
    com = _common_inputs(ln_gain, ln_bias, pool_b, pool_scale, ple_gate_b)
    shared = {"pool_w": np.ascontiguousarray(pool_w[0]), "mlp_w10": np.ascontiguousarray(mlp_w1[0]),
              "mlp_w20": np.ascontiguousarray(mlp_w2[0]), "ple_gate_w0": np.ascontiguousarray(ple_gate_w[0]),
              "ple_proj0": np.ascontiguousarray(ple_proj[0]), "gdn_w_out": np.ascontiguousarray(gdn_w_out[0]),
              "mlp_w11": np.ascontiguousarray(mlp_w1[1]), "mlp_w21": np.ascontiguousarray(mlp_w2[1]),
              "ple_gate_w1": np.ascontiguousarray(ple_gate_w[1]), "ple_proj1": np.ascontiguousarray(ple_proj[1])}
    ins = []
    for r in range(NCORES):
        b, s = r // 4, r % 4
        xs = np.zeros((128 + TOK, D), np.float32)
        xs[128:] = x[b, s * TOK:(s + 1) * TOK]
        if s > 0:
            xs[:128] = x[b, s * TOK - 128:s * TOK]
        m = dict(com)
        m.update(shared)
        m.update(_p2_inputs(r, gdn_w_in, gdn_conv, gdn_a_log, gdn_dt_bias, gdn_norm_w))
        gidx = np.full((128, 32), 1 << 24, np.int32)
        gidx[:, s * 8:(s + 1) * 8] = (np.arange(8)[None, :] * 128 + np.arange(128)[:, None]) * 2 + b
        m.update({"x": xs, "p0": np.ascontiguousarray(p[0, b, s * TOK:(s + 1) * TOK]),
                  "p1": np.ascontiguousarray(p[1, b, s * TOK:(s + 1) * TOK]), "band": _band_mats(s == 0),
                  "gidx": gidx})
        ins.append(m)
    res = run_bass_kernel_spmd(_get_nc("fused"), ins, core_ids=list(range(NCORES)))
    out = np.zeros((2, SEQ, D), np.float32)
    for r in range(NCORES):
        b, s = r // 4, r % 4
        out[b, s * TOK:(s + 1) * TOK] = res.results[r]["out"]
    return out


_NC_CACHE = {}


def _get_nc(mode):
    if mode not in _NC_CACHE:
        _NC_CACHE[mode] = build(mode)
    return _NC_CACHE[mode]


def _common_inputs(ln_gain, ln_bias, pool_b, pool_scale, ple_gate_b):
    vecs = np.concatenate([
        np.asarray(ln_gain, np.float32).reshape(4, D), np.asarray(ln_bias, np.float32).reshape(4, D),
        np.asarray(pool_b, np.float32).reshape(1, D), np.asarray(pool_scale, np.float32).reshape(1, D),
        np.asarray(ple_gate_b, np.float32).reshape(2, D)], axis=0)
    return {"vecs": np.ascontiguousarray(vecs), "cmat": _cmat()}


def run_p1(x, p, ln_gain, ln_bias, pool_w, pool_b, pool_scale, mlp_w1, mlp_w2, ple_gate_w, ple_gate_b, ple_proj):
    com = _common_inputs(ln_gain, ln_bias, pool_b, pool_scale, ple_gate_b)
    ins = []
    for r in range(NCORES):
        b, s = r // 4, r % 4
        xs = np.zeros((128 + TOK, D), np.float32)
        xs[128:] = x[b, s * TOK:(s + 1) * TOK]
        if s > 0:
            xs[:128] = x[b, s * TOK - 128:s * TOK]
        m = dict(com)
        m.update({"x": xs, "p0": np.ascontiguousarray(p[0, b, s * TOK:(s + 1) * TOK]), "band": _band_mats(s == 0),
                  "pool_w": np.ascontiguousarray(pool_w[0]), "mlp_w10": np.ascontiguousarray(mlp_w1[0]),
                  "mlp_w20": np.ascontiguousarray(mlp_w2[0]), "ple_gate_w0": np.ascontiguousarray(ple_gate_w[0]),
                  "ple_proj0": np.ascontiguousarray(ple_proj[0])})
        ins.append(m)
    res = run_bass_kernel_spmd(_get_nc("p1"), ins, core_ids=list(range(NCORES)))
    return [r_["accout"] for r_ in res.results], [r_["x1T"] for r_ in res.results]


def _p2_inputs(r, gdn_w_in, gdn_conv, gdn_a_log, gdn_dt_bias, gdn_norm_w):
    w = gdn_w_in[0]
    cols = np.concatenate([np.arange(r * 128, (r + 1) * 128), 1024 + np.arange(r * 128, (r + 1) * 128),
                           2048 + np.arange(r * 128, (r + 1) * 128), 3072 + np.arange(r * 128, (r + 1) * 128),
                           np.array([4096 + r, 4104 + r])])
    win = np.ascontiguousarray(w[:, cols])
    hp = np.zeros((128, 16), np.float32)
    for c in range(3):
        hp[:, 4 * c:4 * c + 4] = gdn_conv[0][:, c * 1024 + r * 128:c * 1024 + (r + 1) * 128].T
    hp[:, 12] = gdn_a_log[0, r]
    hp[:, 13] = gdn_dt_bias[0, r]
    hp[:, 14] = gdn_norm_w[0]
    return {"win": win, "hp": hp}


def run_p2(x1T_list, com, gdn_w_in, gdn_conv, gdn_a_log, gdn_dt_bias, gdn_norm_w):
    xg = np.ascontiguousarray(np.stack(x1T_list, axis=0))
    ins = []
    for r in range(NCORES):
        m = dict(com)
        m.update(_p2_inputs(r, gdn_w_in, gdn_conv, gdn_a_log, gdn_dt_bias, gdn_norm_w))
        m["xg"] = xg
        ins.append(m)
    res = run_bass_kernel_spmd(_get_nc("p2"), ins, core_ids=list(range(NCORES)))
    return [r_["og"] for r_ in res.results]


def run_p3(acc_list, og_list, com, p, gdn_w_out, mlp_w1, mlp_w2, ple_gate_w, ple_proj):
    ins = []
    for r in range(NCORES):
        b, s = r // 4, r % 4
        t0 = b * SEQ + s * TOK
        ogin = np.ascontiguousarray(np.stack([og_list[h][:, t0:t0 + TOK] for h in range(8)], axis=0))
        m = dict(com)
        m.update({"accin": acc_list[r], "ogin": ogin, "p1": np.ascontiguousarray(p[1, b, s * TOK:(s + 1) * TOK]),
                  "gdn_w_out": np.ascontiguousarray(gdn_w_out[0]), "mlp_w11": np.ascontiguousarray(mlp_w1[1]),
                  "mlp_w21": np.ascontiguousarray(mlp_w2[1]), "ple_gate_w1": np.ascontiguousarray(ple_gate_w[1]),
                  "ple_proj1": np.ascontiguousarray(ple_proj[1])})
        ins.append(m)
    res = run_bass_kernel_spmd(_get_nc("p3"), ins, core_ids=list(range(NCORES)))
    return [r_["out"] for r_ in res.results]


def kernel(x, p, ln_gain, ln_bias, pool_w, pool_b, pool_scale, gdn_w_in, gdn_conv, gdn_a_log, gdn_dt_bias,
           gdn_norm_w, gdn_w_out, mlp_w1, mlp_w2, ple_gate_w, ple_gate_b, ple_proj):
    args = [np.asarray(a) for a in (x, p, ln_gain, ln_bias, pool_w, pool_b, pool_scale, gdn_w_in, gdn_conv,
                                    gdn_a_log, gdn_dt_bias, gdn_norm_w, gdn_w_out, mlp_w1, mlp_w2, ple_gate_w,
                                    ple_gate_b, ple_proj)]
    (x, p, ln_gain, ln_bias, pool_w, pool_b, pool_scale, gdn_w_in, gdn_conv, gdn_a_log, gdn_dt_bias,
     gdn_norm_w, gdn_w_out, mlp_w1, mlp_w2, ple_gate_w, ple_gate_b, ple_proj) = args
    if FUSED:
        return run_fused(x, p, ln_gain, ln_bias, pool_w, pool_b, pool_scale, gdn_w_in, gdn_conv, gdn_a_log,
                         gdn_dt_bias, gdn_norm_w, gdn_w_out, mlp_w1, mlp_w2, ple_gate_w, ple_gate_b, ple_proj)
    com = _common_inputs(ln_gain, ln_bias, pool_b, pool_scale, ple_gate_b)
    accs, x1Ts = run_p1(x, p, ln_gain, ln_bias, pool_w, pool_b, pool_scale, mlp_w1, mlp_w2, ple_gate_w, ple_gate_b,
                        ple_proj)
    ogs = run_p2(x1Ts, com, gdn_w_in, gdn_conv, gdn_a_log, gdn_dt_bias, gdn_norm_w)
    outs = run_p3(accs, ogs, com, p, gdn_w_out, mlp_w1, mlp_w2, ple_gate_w, ple_proj)
    out = np.zeros((2, SEQ, D), np.float32)
    for r in range(NCORES):
        b, s = r // 4, r % 4
        out[b, s * TOK:(s + 1) * TOK] = outs[r]
    return out
```

```python
import os
import numpy as np
from contextlib import ExitStack
import concourse.bass as bass
import concourse.mybir as mybir
from concourse.bass_utils import run_bass_kernel_spmd

F32 = mybir.dt.float32
BF16 = mybir.dt.bfloat16
I32 = mybir.dt.int32
AF = mybir.ActivationFunctionType
ALU = mybir.AluOpType

NCORES = 8
D = 1024
TOK = 2048
NT = 16
SEQ = 8192
DFF = 4096
ALPHA = (2.0 * 2) ** 0.25
LN_EPS = 1e-5
RMS_EPS = 1e-6
L2_EPS = 1e-6
DH = 128
FUSED = os.environ.get("K_FUSED", "1") == "1"


class R:
    __slots__ = ("name", "w", "r", "excl")

    def __init__(self, name, excl=False):
        self.name = name
        self.w = None
        self.r = {}
        self.excl = excl


class Sched:
    def __init__(self, nc, es, nd=24):
        self.nc = nc
        self._es = es
        self.eng = {"pe": nc.tensor, "act": nc.scalar, "dve": nc.vector, "pool": nc.gpsimd, "sp": nc.sync}
        self.sem = {k: es.enter_context(nc.semaphore("s_" + k)) for k in self.eng}
        self.cnt = {k: 0 for k in self.eng}
        self.ND = nd
        self.dsem = [es.enter_context(nc.semaphore(f"dq{i}")) for i in range(nd)]
        self.dcnt = [0] * nd
        self.dnext = 0
        self.waited = {}

    def _semof(self, src):
        if src == "cc":
            return self.csem
        return self.sem[src] if src in self.sem else self.dsem[int(src[1:])]

    def _wait(self, eng, deps):
        for src, val in deps.items():
            if src == eng and eng == "pe":
                continue
            if self.waited.get((eng, src), 0) < val:
                self.eng[eng].wait_ge(self._semof(src), val)
                self.waited[(eng, src)] = val

    @staticmethod
    def _deps(reads, writes, eng=None):
        deps = {}
        for r in reads:
            if r.w is not None:
                s, v = r.w
                if deps.get(s, 0) < v:
                    deps[s] = v
        for w in writes:
            if w.w is not None:
                s, v = w.w
                if deps.get(s, 0) < v:
                    deps[s] = v
            for s, v in w.r.items():
                if deps.get(s, 0) < v:
                    deps[s] = v
        return deps

    def op(self, eng, fn, reads=(), writes=()):
        ex = [r for r in reads if r.excl]
        if ex:
            writes = list(writes) + ex
        self._wait(eng, self._deps(reads, writes, eng))
        ins = fn(self.eng[eng])
        self.cnt[eng] += 1
        v = self.cnt[eng]
        ins.then_inc(self.sem[eng], 1)
        for r in reads:
            r.r[eng] = v
        for w in writes:
            w.w = (eng, v)
            w.r = {}
        return ins

    def dma(self, q, out, in_, reads=(), writes=(), fn=None):
        k = self.dnext
        self.dnext = (k + 1) % self.ND
        deps = self._deps(reads, writes)
        src = f"d{k}"
        if self.dcnt[k] > 0:
            deps[src] = max(deps.get(src, 0), 16 * self.dcnt[k])
        self._wait(q, deps)
        if fn is None:
            ins = self.eng[q].dma_start(out=out, in_=in_)
        else:
            ins = fn(self.eng[q])
        self.dcnt[k] += 1
        v = 16 * self.dcnt[k]
        ins.then_inc(self.dsem[k], 16)
        for r in reads:
            r.r[src] = v
        for w in writes:
            w.w = (src, v)
            w.r = {}
        self.last_dma = (src, v)
        return ins

    def collective(self, fn, reads=(), writes=(), extra=None):
        if not hasattr(self, "csem"):
            self.csem = self._es.enter_context(self.nc.semaphore("s_cc"))
            self.ccnt = 0
        deps = self._deps(reads, writes, "pool")
        for k_, v_ in (extra or {}).items():
            if deps.get(k_, 0) < v_:
                deps[k_] = v_
        self._wait("pool", deps)
        ins = fn(self.eng["pool"])
        self.ccnt += 1
        ins.then_inc(self.csem)
        for r in reads:
            r.r["cc"] = self.ccnt
        for w in writes:
            w.w = ("cc", self.ccnt)
            w.r = {}
        return ins

    def barrier(self, skip_cc=False):
        tgt = {k: v for k, v in self.cnt.items() if v > 0}
        for k in range(self.ND):
            if self.dcnt[k] > 0:
                tgt[f"d{k}"] = 16 * self.dcnt[k]
        if getattr(self, "ccnt", 0) > 0 and not skip_cc:
            tgt["cc"] = self.ccnt
        for e in self.eng:
            self._wait(e, tgt)

    def finish(self):
        tgt = {k: v for k, v in self.cnt.items() if v > 0}
        for k in range(self.ND):
            if self.dcnt[k] > 0:
                tgt[f"d{k}"] = 16 * self.dcnt[k]
        self._wait("sp", tgt)


class Ring:
    def __init__(self, es, nc, name, shape, dtype, n, psum=False):
        self.t = []
        for i in range(n):
            if psum:
                t = es.enter_context(nc.psum_tensor(f"{name}{i}", shape, dtype))
            else:
                t = es.enter_context(nc.sbuf_tensor(f"{name}{i}", shape, dtype))
            self.t.append((t, R(f"{name}{i}", excl=psum)))
        self.i = 0

    def next(self):
        x = self.t[self.i]
        self.i = (self.i + 1) % len(self.t)
        return x

    def acq(self):
        if not hasattr(self, "free"):
            self.free = list(self.t)
        while not self.free:
            yield
        return self.free.pop(0)

    def rel(self, x):
        self.free.append(x)


def bcast_ap(handle, off, n):
    return bass.AP(handle, off, [[0, 128], [1, n]])


class Ctx:
    pass


def emit_layernorm(C, i, gain, bias):
    S = C.S
    a = C.acc[:, i, :]
    aR = C.accR[i]
    st, stR = C.small.next()
    S.op("dve", lambda e: e.bn_stats(st[:, 0:6], C.acc[:, i, 0:512]), reads=[aR], writes=[stR])
    S.op("dve", lambda e: e.bn_stats(st[:, 6:12], C.acc[:, i, 512:1024]), reads=[aR], writes=[stR])
    S.op("dve", lambda e: e.bn_aggr(st[:, 12:14], st[:, 0:12]), reads=[stR], writes=[stR])
    S.op("act", lambda e: e.activation(out=st[:, 14:15], in_=st[:, 13:14], func=AF.Sqrt, bias=C.eps[:, 0:1]),
         reads=[stR, C.constR], writes=[stR])
    S.op("dve", lambda e: e.reciprocal(st[:, 14:15], st[:, 14:15]), reads=[stR], writes=[stR])
    S.op("dve", lambda e: e.scalar_tensor_tensor(st[:, 15:16], st[:, 12:13], -1.0, st[:, 14:15], ALU.mult, ALU.mult),
         reads=[stR], writes=[stR])
    S.op("act", lambda e: e.activation(out=a, in_=a, func=AF.Identity, bias=st[:, 15:16], scale=st[:, 14:15]),
         reads=[stR, aR], writes=[aR])
    S.op("dve", lambda e: e.tensor_tensor(a, a, gain[0][:], ALU.mult), reads=[aR, gain[1]], writes=[aR])
    S.op("dve", lambda e: e.tensor_tensor(a, a, bias[0][:], ALU.add), reads=[aR, bias[1]], writes=[aR])


def emit_transpose_to_xT(C, i, scale_after=None, evac_scale=None):
    S = C.S
    for half in range(2):
        ps, psR = C.psum.next()
        for q in range(4):
            fc = half * 4 + q
            S.op("pe", lambda e: e.transpose(ps[:, q * 128:(q + 1) * 128], C.acc[:, i, fc * 128:(fc + 1) * 128],
                                             C.ident[:]),
                 reads=[C.accR[i], C.constR], writes=[psR])
        S.op("act", lambda e: e.activation(
            out=C.xT[:, half * 4:(half + 1) * 4, i * 128:(i + 1) * 128],
            in_=ps[:, 0:512].rearrange("p (q t) -> p q t", q=4), func=AF.Copy,
            **({} if evac_scale is None else {"scale": float(evac_scale)})),
            reads=[psR], writes=[C.xTR[i]])
    if scale_after is not None:
        S.op("dve", lambda e: e.tensor_scalar(C.acc[:, i, :], C.acc[:, i, :], float(scale_after), None, ALU.mult),
             reads=[C.accR[i]], writes=[C.accR[i]])


def alloc_tok(C, es, tag):
    C.xT = es.enter_context(C.nc.sbuf_tensor(f"xT{tag}", [128, 8, TOK], BF16))
    C.xTR = [R(f"xT{i}") for i in range(NT)]
    C.bc = Ring(es, C.nc, f"bc{tag}", [128, D], F32, 5)
    C.tmp = Ring(es, C.nc, f"tmp{tag}", [128, 512], F32, 3)


def load_bcast(C, row):
    t, r = C.bc.next()
    C.S.dma("sp", t[:], bcast_ap(C.vecs_h, row * D, D), writes=[r])
    return (t, r)


def load_bcast_scaled(C, row, scale):
    t, r = load_bcast(C, row)
    C.S.op("dve", lambda e: e.tensor_scalar(t[:], t[:], float(scale), None, ALU.mult), reads=[r], writes=[r])
    return (t, r)


def load_rows16(C, es, tag, rows):
    nc, S = C.nc, C.S
    t = es.enter_context(nc.sbuf_tensor(f"rows16{tag}", [1, len(rows) + 1, D], BF16))
    tR = R("rows16")
    for n, row in enumerate(rows):
        S.dma("pool", t[0:1, n, :], bass.AP(C.vecs_h, row * D, [[0, 1], [1, D]]), writes=[tR])
    S.op("dve", lambda e: e.memset(t[0:1, len(rows), :], 1.0), writes=[tR])
    return t, tR


def emit_ple(C, es, layer, p_d, wg_d, wp_d, bg_row, rows16=None, bg_idx=0):
    S, nc = C.S, C.nc
    wg = es.enter_context(nc.sbuf_tensor(f"wg{layer}", [128, 8, D], BF16))
    wgR = R("wg")
    wp = es.enter_context(nc.sbuf_tensor(f"wp{layer}", [128, 2, D], BF16))
    wpR = R("wp")
    pring = Ring(es, nc, f"pt{layer}", [128, 4, 256], F32, 2)
    pTring = Ring(es, nc, f"pT{layer}", [128, 2, 128], BF16, 2)
    for kc2 in range(2):
        S.dma("pool", wg[:, kc2 * 4:(kc2 + 1) * 4, :],
              wg_d[kc2 * 512:(kc2 + 1) * 512, :].rearrange("(k p) c -> p k c", p=128), writes=[wgR])
    S.dma("pool", wp[:], wp_d.rearrange("(k p) c -> p k c", p=128), writes=[wpR])
    r16, r16R = rows16
    n16 = r16.shape[1] - 1
    st = {"pt": None}

    def tile(i):
        if i % 4 == 0:
            st["pt"] = pring.next()
            S.dma("sp", st["pt"][0][:], p_d[i * 128:(i + 4) * 128, :].rearrange("(j t) f -> t j f", t=128),
                  writes=[st["pt"][1]])
        pt = st["pt"]
        ps, psR = C.psum.next()
        for k in range(2):
            S.op("pe", lambda e: e.transpose(ps[:, k * 128:(k + 1) * 128], pt[0][:, i % 4, k * 128:(k + 1) * 128],
                                             C.ident[:]), reads=[pt[1], C.constR], writes=[psR])
        pT, pTR = pTring.next()
        S.op("act", lambda e: e.activation(out=pT[:], in_=ps[:, 0:256].rearrange("p (k t) -> p k t", k=2),
                                           func=AF.Copy), reads=[psR], writes=[pTR])
        for half in range(2):
            cs = slice(half * 512, (half + 1) * 512)
            pg, pgR = C.psum.next()
            for kc in range(8):
                S.op("pe", lambda e: e.matmul(pg[:, 0:512], C.xT[:, kc, i * 128:(i + 1) * 128], wg[:, kc, cs],
                                              start=(kc == 0), stop=False),
                     reads=[C.xTR[i], wgR], writes=[pgR])
            S.op("pe", lambda e: e.matmul(pg[:, 0:512], r16[0:1, n16, 0:128], r16[0:1, bg_idx, cs],
                                          start=False, stop=True), reads=[r16R], writes=[pgR])
            pp, ppR = C.psum.next()
            for k in range(2):
                S.op("pe", lambda e: e.matmul(pp[:, 0:512], pT[:, k, :], wp[:, k, cs], start=(k == 0), stop=(k == 1)),
                     reads=[pTR, wpR], writes=[ppR])
            tm, tmR = C.tmp.next()
            S.op("act", lambda e: e.activation(out=tm[:, 0:512], in_=pg[:, 0:512], func=AF.Sigmoid),
                 reads=[pgR], writes=[tmR])
            S.op("dve", lambda e: e.tensor_tensor(tm[:, 0:512], tm[:, 0:512], pp[:, 0:512], ALU.mult),
                 reads=[tmR, ppR], writes=[tmR])
            S.op("dve", lambda e: e.tensor_tensor(C.acc[:, i, cs], C.acc[:, i, cs], tm[:, 0:512], ALU.add),
                 reads=[tmR, C.accR[i]], writes=[C.accR[i]])

    return tile


def emit_mlp(C, es, layer, w1_d, w2_d, after_last=None):
    S, nc = C.S, C.nc
    w1ring = Ring(es, nc, f"w1g{layer}", [128, 8, 512], BF16, 2)
    w2ring = Ring(es, nc, f"w2g{layer}", [128, 4, D], BF16, 2)
    hring = Ring(es, nc, f"hT{layer}", [128, 4, TOK], BF16, 2)
    NG = DFF // 512

    def load(gi):
        w1g = w1ring.next()
        w2g = w2ring.next()
        for kc2 in range(2):
            S.dma("pool", w1g[0][:, kc2 * 4:(kc2 + 1) * 4, :],
                  w1_d[kc2 * 512:(kc2 + 1) * 512, gi * 512:(gi + 1) * 512].rearrange("(k p) c -> p k c", p=128),
                  writes=[w1g[1]])
        S.dma("pool", w2g[0][:], w2_d[gi * 512:(gi + 1) * 512, :].rearrange("(k p) c -> p k c", p=128),
              writes=[w2g[1]])
        return w1g, w2g

    nxt = load(0)
    for gi in range(NG):
        w1g, w2g = nxt
        if gi + 1 < NG:
            nxt = load(gi + 1)
        hT, hTR = hring.next()
        for tb in range(4):
            ts_ = slice(tb * 512, (tb + 1) * 512)
            for hcl in range(4):
                ps, psR = C.psum.next()
                for kc in range(8):
                    S.op("pe", lambda e: e.matmul(ps[:, 0:512], w1g[0][:, kc, hcl * 128:(hcl + 1) * 128],
                                                  C.xT[:, kc, ts_], start=(kc == 0), stop=(kc == 7)),
                         reads=[w1g[1]] + C.xTR[tb * 4:(tb + 1) * 4], writes=[psR])
                tm, tmR = C.tmp.next()
                S.op("act", lambda e: e.activation(out=tm[:, 0:512], in_=ps[:, 0:512], func=AF.Relu),
                     reads=[psR], writes=[tmR])
                S.op("dve", lambda e: e.tensor_tensor(hT[:, hcl, ts_], tm[:, 0:512], tm[:, 0:512], ALU.mult),
                     reads=[tmR], writes=[hTR])
        for i in range(NT):
            for half in range(2):
                cs = slice(half * 512, (half + 1) * 512)
                ps, psR = C.psum.next()
                for hcl in range(4):
                    S.op("pe", lambda e: e.matmul(ps[:, 0:512], hT[:, hcl, i * 128:(i + 1) * 128], w2g[0][:, hcl, cs],
                                                  start=(hcl == 0), stop=(hcl == 3)),
                         reads=[hTR, w2g[1]], writes=[psR])
                S.op("dve", lambda e: e.tensor_tensor(C.acc[:, i, cs], C.acc[:, i, cs], ps[:, 0:512], ALU.add),
                     reads=[psR, C.accR[i]], writes=[C.accR[i]])
            if gi == NG - 1 and after_last is not None:
                after_last(i)


def setup_common(nc, es, S, vecs_h, cmat_d, tok=True):
    C = Ctx()
    C.nc, C.S = nc, S
    C.vecs_h = vecs_h
    if tok:
        C.acc = es.enter_context(nc.sbuf_tensor("acc", [128, NT, D], F32))
        C.accR = [R(f"acc{i}") for i in range(NT)]
    C.cmat = es.enter_context(nc.sbuf_tensor("cmat_sb", [128, 6, 128], F32))
    C.constR = R("const")
    S.dma("sp", C.cmat[:], cmat_d, writes=[C.constR])
    C.eps = es.enter_context(nc.sbuf_tensor("eps_sb", [128, 4], F32))
    S.op("dve", lambda e: e.memset(C.eps[:, 0:1], LN_EPS), writes=[C.constR])
    S.op("dve", lambda e: e.memset(C.eps[:, 1:2], L2_EPS), writes=[C.constR])
    S.op("dve", lambda e: e.memset(C.eps[:, 2:3], RMS_EPS), writes=[C.constR])
    C.ident = C.cmat[:, 0, :]
    C.ones = C.cmat[:, 1, :]
    C.U = C.cmat[:, 2, :]
    C.L = C.cmat[:, 3, :]
    C.Lbd = C.cmat[:, 4, :]
    C.Loff = C.cmat[:, 5, :]
    C.psum = Ring(es, nc, "ps", [128, 512], F32, 6, psum=True)
    C.small = Ring(es, nc, "sm", [128, 16], F32, 4)
    return C


def emit_phase1(C, dd, x1T_dst):
    with ExitStack() as esx:
        alloc_tok(C, esx, "a")
        _emit_phase1(C, dd, x1T_dst)


def _emit_phase1(C, dd, x1T_dst):
    S, nc = C.S, C.nc
    with ExitStack() as es:
        xh = es.enter_context(nc.sbuf_tensor("xh", [128, D], F32))
        xhR = R("xh")
        band = es.enter_context(nc.sbuf_tensor("band_sb", [128, 4, 3, 128], F32))
        bandR = R("band")
        pw = es.enter_context(nc.sbuf_tensor("poolw", [128, 8, 256], BF16))
        pwR = R("pw")
        S.dma("sp", band[:], dd["band"], writes=[bandR])
        S.dma("pool", pw[:], dd["pool_w"].rearrange("g (k p) o -> p (g k) o", p=128), writes=[pwR])
        S.dma("sp", xh[:], dd["x"][0:128, :], writes=[xhR])
        for i in range(NT):
            S.dma("sp", C.acc[:, i, :], dd["x"][128 * (i + 1):128 * (i + 2), :], writes=[C.accR[i]])
        for i in range(NT):
            for gh in range(2):
                ps, psR = C.psum.next()
                for gq in range(4):
                    g8 = gh * 4 + gq
                    g, cc = g8 // 2, g8 % 2
                    c0 = 256 * g + 128 * cc
                    if i == 0:
                        prev_ap, prevR = xh[:, c0:c0 + 128], xhR
                    else:
                        prev_ap, prevR = C.acc[:, i - 1, c0:c0 + 128], C.accR[i - 1]
                    o = ps[:, gq * 128:(gq + 1) * 128]
                    S.op("pe", lambda e: e.matmul(o, prev_ap, band[:, g, 0, :], start=True, stop=False),
                         reads=[prevR, bandR], writes=[psR])
                    S.op("pe", lambda e: e.matmul(o, C.acc[:, i, c0:c0 + 128], band[:, g, 1 if i == 0 else 2, :],
                                                  start=False, stop=True),
                         reads=[C.accR[i], bandR], writes=[psR])
                S.op("act", lambda e: e.activation(
                    out=C.xT[:, gh * 4:(gh + 1) * 4, i * 128:(i + 1) * 128],
                    in_=ps[:, 0:512].rearrange("p (q t) -> p q t", q=4), func=AF.Copy),
                    reads=[psR], writes=[C.xTR[i]])
        pb = load_bcast(C, 8)
        psc = load_bcast(C, 9)
        g0 = load_bcast_scaled(C, 0, ALPHA)
        b0 = load_bcast_scaled(C, 4, ALPHA)
        for g8 in range(8):
            S.op("dve", lambda e: e.tensor_tensor(pw[:, g8, :], pw[:, g8, :], psc[0][:, (g8 // 2) * 256:(g8 // 2 + 1) * 256],
                                                  ALU.mult), reads=[pwR, psc[1]], writes=[pwR])
        rows16 = load_rows16(C, es, "a", (10,))
        r16, r16R = rows16
        S.op("dve", lambda e: e.tensor_tensor(pb[0][0:1, :], pb[0][0:1, :], psc[0][0:1, :], ALU.mult),
             reads=[pb[1], psc[1]], writes=[pb[1]])
        pbs = es.enter_context(nc.sbuf_tensor("pbs16", [1, D], BF16))
        pbsR = R("pbs16")
        S.op("dve", lambda e: e.tensor_copy(pbs[0:1, :], pb[0][0:1, :]), reads=[pb[1]], writes=[pbsR])
        ple_tile = emit_ple(C, es, 0, dd["p0"], dd["ple_gate_w0"], dd["ple_proj0"], 10, rows16, 0)
        for i in range(NT):
            for half in range(2):
                ps, psR = C.psum.next()
                for gq in range(2):
                    g = half * 2 + gq
                    for cc in range(2):
                        S.op("pe", lambda e: e.matmul(ps[:, gq * 256:(gq + 1) * 256],
                                                      C.xT[:, 2 * g + cc, i * 128:(i + 1) * 128], pw[:, 2 * g + cc, :],
                                                      start=(cc == 0), stop=False),
                             reads=[C.xTR[i], pwR], writes=[psR])
                    S.op("pe", lambda e: e.matmul(ps[:, gq * 256:(gq + 1) * 256], r16[0:1, 1, 0:128],
                                                  pbs[0:1, g * 256:(g + 1) * 256], start=False, stop=True),
                         reads=[r16R, pbsR], writes=[psR])
                cs = slice(half * 512, (half + 1) * 512)
                S.op("dve", lambda e: e.scalar_tensor_tensor(C.acc[:, i, cs], C.acc[:, i, cs], ALPHA, ps[:, 0:512],
                                                             ALU.mult, ALU.add),
                     reads=[psR, C.accR[i]], writes=[C.accR[i]])
            emit_layernorm(C, i, g0, b0)
            if i >= 1:
                emit_transpose_to_xT(C, i - 1, evac_scale=1.0 / ALPHA)
            if i >= 2:
                ple_tile(i - 2)
        emit_transpose_to_xT(C, NT - 1, evac_scale=1.0 / ALPHA)
        ple_tile(NT - 2)
        ple_tile(NT - 1)
        S.barrier()
    with ExitStack() as es:
        g1 = load_bcast_scaled(C, 1, ALPHA)
        b1 = load_bcast_scaled(C, 5, ALPHA)

        def tr(t):
            emit_transpose_to_xT(C, t, evac_scale=1.0 / ALPHA)
            if t % 4 == 3:
                x1T_dst(t // 4)

        def after_last(i):
            emit_layernorm(C, i, g1, b1)
            if i >= 2:
                tr(i - 2)
        emit_mlp(C, es, 0, dd["mlp_w10"], dd["mlp_w20"], after_last)
        tr(NT - 2)
        tr(NT - 1)
        S.barrier(skip_cc=True)


def emit_phase3(C, dd, load_og, out_d):
    with ExitStack() as esx:
        alloc_tok(C, esx, "c")
        _emit_phase3(C, dd, load_og, out_d)


def _emit_phase3(C, dd, load_og, out_d):
    S, nc = C.S, C.nc
    with ExitStack() as es:
        wo = es.enter_context(nc.sbuf_tensor("wo", [128, 8, D], BF16))
        woR = R("wo")
        for kc2 in range(2):
            S.dma("pool", wo[:, kc2 * 4:(kc2 + 1) * 4, :],
                  dd["gdn_w_out"][kc2 * 512:(kc2 + 1) * 512, :].rearrange("(k p) c -> p k c", p=128), writes=[woR])
        load_og(C)
        g0 = load_bcast_scaled(C, 2, ALPHA)
        b0 = load_bcast_scaled(C, 6, ALPHA)
        rows16 = load_rows16(C, es, "c", (11,))
        ple_tile = emit_ple(C, es, 1, dd["p1"], dd["ple_gate_w1"], dd["ple_proj1"], 11, rows16, 0)
        for i in range(NT):
            for half in range(2):
                cs = slice(half * 512, (half + 1) * 512)
                ps, psR = C.psum.next()
                for kc in range(8):
                    S.op("pe", lambda e: e.matmul(ps[:, 0:512], C.xT[:, kc, i * 128:(i + 1) * 128], wo[:, kc, cs],
                                                  start=(kc == 0), stop=(kc == 7)),
                         reads=[C.xTR[i], woR], writes=[psR])
                S.op("dve", lambda e: e.tensor_tensor(C.acc[:, i, cs], C.acc[:, i, cs], ps[:, 0:512], ALU.add),
                     reads=[psR, C.accR[i]], writes=[C.accR[i]])
            emit_layernorm(C, i, g0, b0)
            if i >= 1:
                emit_transpose_to_xT(C, i - 1, evac_scale=1.0 / ALPHA)
            if i >= 2:
                ple_tile(i - 2)
        emit_transpose_to_xT(C, NT - 1, evac_scale=1.0 / ALPHA)
        ple_tile(NT - 2)
        ple_tile(NT - 1)
        S.barrier()
    with ExitStack() as es:
        g1 = load_bcast(C, 3)
        b1 = load_bcast(C, 7)

        def after_last(i):
            emit_layernorm(C, i, g1, b1)
            S.dma("sp", out_d[i * 128:(i + 1) * 128, :], C.acc[:, i, :], reads=[C.accR[i]])
        emit_mlp(C, es, 1, dd["mlp_w11"], dd["mlp_w21"], after_last)
        S.barrier()


def emit_phase2_old(C, dd, xg, og_dst):
    S, nc = C.S, C.nc
    with ExitStack() as es:
        def sb(name, shape, dt=F32):
            return es.enter_context(nc.sbuf_tensor(name, shape, dt))

        win = sb("win_sb", [128, 8, 514], BF16)
        winR = R("win")
        S.dma("pool", win[:], dd["win"].rearrange("(k p) c -> p k c", p=128), writes=[winR])
        hp = sb("hp_sb", [128, 16])
        hpR = R("hp")
        S.dma("sp", hp[:], dd["hp"], writes=[hpR])
        S.op("act", lambda e: e.activation(out=hp[:, 15:16], in_=hp[:, 12:13], func=AF.Exp), reads=[hpR], writes=[hpR])
        S.op("dve", lambda e: e.tensor_scalar(hp[:, 15:16], hp[:, 15:16], -1.0, None, ALU.mult),
             reads=[hpR], writes=[hpR])
        mats = Ring(es, nc, "m", [128, 128], F32, 40)
        keep = Ring(es, nc, "kp", [128, 128], F32, 24)
        opsum = Ring(es, nc, "po", [128, 512], F32, 2, psum=True)
        B = []
        for b in range(2):
            bb = Ctx()
            bb.xblk = Ring(es, nc, f"xb{b}", [128, 8, 512], BF16, 2)
            bb.pre = [(sb(f"pre{b}{c}", [128, 515]), R("pre")) for c in range(3)]
            bb.xs = [(sb(f"xs{b}{c}", [128, 512]), R("xs")) for c in range(3)]
            bb.qn = (sb(f"qn{b}", [128, 512]), R("qn"))
            bb.kn = (sb(f"kn{b}", [128, 512]), R("kn"))
            bb.zs = (sb(f"zs{b}", [128, 512]), R("zs"))
            bb.cv = (sb(f"cv{b}", [128, 512]), R("cv"))
            bb.gt = (sb(f"gt{b}", [128, 64]), R("gt"))
            bb.S = (sb(f"S{b}", [128, 128]), R("S"))
            bb.og = Ring(es, nc, f"og{b}", [128, 512], BF16, 2)
            bb.ot = (sb(f"ot{b}", [128, 512]), R("ot"))
            S.op("dve", lambda e: e.memset(bb.S[0][:], 0.0), writes=[bb.S[1]])
            for c in range(3):
                S.op("dve", lambda e: e.memset(bb.pre[c][0][:, 0:3], 0.0), writes=[bb.pre[c][1]])
            B.append(bb)

        import os
        SUB = int(os.environ.get("P2_SUB", 9))
        PSUB = int(os.environ.get("P2_PSUB", 9))

        def block_front(b, k):
            bb = B[b]
            rank = 4 * b + k // 4
            off = (k % 4) * 512
            xb, xbR = bb.xblk.next()
            S.dma("sp", xb[:], xg[rank, :, :, off:off + 512].rearrange("f p t -> p f t"),
                  reads=([C.dramR["xg"]] if hasattr(C, "dramR") else []), writes=[xbR])
            for c in range(4):
                ps, psR = C.psum.next()
                for kc in range(8):
                    S.op("pe", lambda e: e.matmul(ps[:, 0:512], win[:, kc, c * 128:(c + 1) * 128], xb[:, kc, :],
                                                  start=(kc == 0), stop=(kc == 7)), reads=[winR, xbR], writes=[psR])
                if c < 3:
                    S.op("act", lambda e: e.activation(out=bb.pre[c][0][:, 3:515], in_=ps[:, 0:512], func=AF.Copy),
                         reads=[psR], writes=[bb.pre[c][1]])
                else:
                    S.op("act", lambda e: e.activation(out=bb.zs[0][:], in_=ps[:, 0:512], func=AF.Silu),
                         reads=[psR], writes=[bb.zs[1]])
            if SUB < 1:
                return
            pg, pgR = C.psum.next()
            for j in range(4):
                for kc in range(8):
                    S.op("pe", lambda e: e.matmul(pg[:, 2 * j:2 * j + 2], xb[:, kc, j * 128:(j + 1) * 128],
                                                  win[:, kc, 512:514], start=(kc == 0), stop=(kc == 7)),
                         reads=[winR, xbR], writes=[pgR])
            if SUB < 2:
                return
            gt, gtR = bb.gt
            S.op("act", lambda e: e.activation(out=gt[:, 0:4], in_=pg[:, 0:8].rearrange("p (j c) -> p j c", c=2)[:, :, 0],
                                               func=AF.Sigmoid), reads=[pgR], writes=[gtR])
            S.op("act", lambda e: e.activation(out=gt[:, 24:28],
                                               in_=pg[:, 0:8].rearrange("p (j c) -> p j c", c=2)[:, :, 1],
                                               func=AF.Exp, bias=hp[:, 13:14]), reads=[pgR, hpR], writes=[gtR])
            S.op("act", lambda e: e.activation(out=gt[:, 24:28], in_=gt[:, 24:28], func=AF.Ln, bias=1.0),
                 reads=[gtR], writes=[gtR])
            S.op("dve", lambda e: e.tensor_scalar(gt[:, 4:8], gt[:, 24:28], hp[:, 15:16], None, ALU.mult),
                 reads=[gtR, hpR], writes=[gtR])
            if SUB < 3:
                return
            pc, pcR = C.psum.next()
            S.op("pe", lambda e: e.matmul(pc[:, 0:4], C.U, gt[:, 4:8], start=True, stop=True),
                 reads=[gtR, C.constR], writes=[pcR])
            S.op("pe", lambda e: e.matmul(pc[:, 4:8], C.ones, gt[:, 4:8], start=True, stop=True),
                 reads=[gtR, C.constR], writes=[pcR])
            S.op("dve", lambda e: e.tensor_copy(gt[:, 8:12], pc[:, 0:4]), reads=[pcR], writes=[gtR])
            S.op("act", lambda e: e.activation(out=gt[:, 12:16], in_=pc[:, 4:8], func=AF.Exp), reads=[pcR], writes=[gtR])
            S.op("dve", lambda e: e.tensor_tensor(gt[:, 24:28], pc[:, 4:8], gt[:, 8:12], ALU.subtract),
                 reads=[pcR, gtR], writes=[gtR])
            S.op("act", lambda e: e.activation(out=gt[:, 16:20], in_=gt[:, 24:28], func=AF.Exp), reads=[gtR], writes=[gtR])
            S.op("act", lambda e: e.activation(out=gt[:, 20:24], in_=gt[:, 8:12], func=AF.Exp), reads=[gtR], writes=[gtR])
            S.op("dve", lambda e: e.tensor_tensor(gt[:, 20:24], gt[:, 20:24], gt[:, 0:4], ALU.mult),
                 reads=[gtR], writes=[gtR])
            if SUB < 4:
                return
            cv, cvR = bb.cv
            for c in range(3):
                pre, preR = bb.pre[c]
                S.op("dve", lambda e: e.tensor_scalar(cv[:], pre[:, 0:512], hp[:, 4 * c:4 * c + 1], None, ALU.mult),
                     reads=[preR, hpR], writes=[cvR])
                for j in range(1, 4):
                    S.op("dve", lambda e: e.scalar_tensor_tensor(cv[:], pre[:, j:j + 512], hp[:, 4 * c + j:4 * c + j + 1],
                                                                 cv[:], ALU.mult, ALU.add),
                         reads=[preR, hpR, cvR], writes=[cvR])
                S.op("act", lambda e: e.activation(out=bb.xs[c][0][:], in_=cv[:], func=AF.Silu),
                     reads=[cvR], writes=[bb.xs[c][1]])
                S.op("dve", lambda e: e.tensor_copy(pre[:, 0:3], pre[:, 512:515]), reads=[preR], writes=[preR])
            if SUB < 5:
                return
            for c, dst in ((0, bb.qn), (1, bb.kn)):
                S.op("act", lambda e: e.activation(out=cv[:], in_=bb.xs[c][0][:], func=AF.Square),
                     reads=[bb.xs[c][1]], writes=[cvR])
                pn, pnR = C.psum.next()
                for q4 in range(4):
                    S.op("pe", lambda e: e.matmul(pn[:, q4 * 128:(q4 + 1) * 128], C.ones, cv[:, q4 * 128:(q4 + 1) * 128],
                                                  start=True, stop=True), reads=[cvR, C.constR], writes=[pnR])
                S.op("act", lambda e: e.activation(out=cv[:], in_=pn[:, 0:512], func=AF.Sqrt, bias=C.eps[:, 1:2]),
                     reads=[pnR, C.constR], writes=[cvR])
                S.op("dve", lambda e: e.reciprocal(cv[:], cv[:]), reads=[cvR], writes=[cvR])
                if c == 0:
                    S.op("dve", lambda e: e.scalar_tensor_tensor(dst[0][:], bb.xs[c][0][:], DH ** -0.5, cv[:],
                                                                 ALU.mult, ALU.mult),
                         reads=[bb.xs[c][1], cvR], writes=[dst[1]])
                else:
                    S.op("dve", lambda e: e.tensor_tensor(dst[0][:], bb.xs[c][0][:], cv[:], ALU.mult),
                         reads=[bb.xs[c][1], cvR], writes=[dst[1]])

        def tile_prep(b, j):
            bb = B[b]
            gt, gtR = bb.gt
            cs = slice(j * 128, (j + 1) * 128)
            kn, knR = bb.kn
            qn, qnR = bb.qn
            vs, vsR = bb.xs[2]
            T = Ctx()
            pt, ptR = C.psum.next()
            S.op("pe", lambda e: e.transpose(pt[:, 0:128], kn[:, cs], C.ident), reads=[knR, C.constR], writes=[ptR])
            S.op("pe", lambda e: e.transpose(pt[:, 128:256], vs[:, cs], C.ident), reads=[vsR, C.constR], writes=[ptR])
            T.kb = mats.next()
            T.kd = keep.next()
            T.vb = mats.next()
            S.op("act", lambda e: e.activation(out=T.kb[0][:], in_=pt[:, 0:128], func=AF.Identity,
                                               scale=gt[:, 20 + j:21 + j]), reads=[ptR, gtR], writes=[T.kb[1]])
            S.op("dve", lambda e: e.tensor_scalar(T.kd[0][:], pt[:, 0:128], gt[:, 16 + j:17 + j], None, ALU.mult),
                 reads=[ptR, gtR], writes=[T.kd[1]])
            S.op("dve", lambda e: e.tensor_scalar(T.vb[0][:], pt[:, 128:256], gt[:, j:j + 1], None, ALU.mult),
                 reads=[ptR, gtR], writes=[T.vb[1]])
            if PSUB < 1:
                return T
            pG, pGR = C.psum.next()
            S.op("pe", lambda e: e.matmul(pG[:, 0:128], kn[:, cs], kn[:, cs], start=True, stop=True),
                 reads=[knR], writes=[pGR])
            S.op("pe", lambda e: e.matmul(pG[:, 128:256], kn[:, cs], qn[:, cs], start=True, stop=True),
                 reads=[knR, qnR], writes=[pGR])
            if PSUB < 2:
                return T
            gU = mats.next()
            S.op("dve", lambda e: e.tensor_scalar(gU[0][:], C.U, gt[:, 4 + j:5 + j], None, ALU.mult),
                 reads=[gtR, C.constR], writes=[gU[1]])
            pR, pRR = C.psum.next()
            S.op("pe", lambda e: e.matmul(pR[:, 0:128], C.ones, gU[0][:], start=True, stop=True),
                 reads=[gU[1], C.constR], writes=[pRR])
            if PSUB < 3:
                return T
            t1 = mats.next()
            Dm = mats.next()
            t2 = mats.next()
            DmT = mats.next()
            E = mats.next()
            gc = gt[:, 8 + j:9 + j]
            S.op("dve", lambda e: e.tensor_scalar(t1[0][:], pR[:, 0:128], gc, 0.0, ALU.subtract, ALU.max),
                 reads=[pRR, gtR], writes=[t1[1]])
            S.op("act", lambda e: e.activation(out=Dm[0][:], in_=t1[0][:], func=AF.Exp, scale=-1.0),
                 reads=[t1[1]], writes=[Dm[1]])
            S.op("dve", lambda e: e.tensor_scalar(t2[0][:], pR[:, 0:128], gc, 0.0, ALU.subtract, ALU.min),
                 reads=[pRR, gtR], writes=[t2[1]])
            S.op("act", lambda e: e.activation(out=DmT[0][:], in_=t2[0][:], func=AF.Exp), reads=[t2[1]], writes=[DmT[1]])
            S.op("act", lambda e: e.activation(out=E[0][:], in_=pR[:, 0:128], func=AF.Exp), reads=[pRR], writes=[E[1]])
            if PSUB < 4:
                return T
            T.qd = keep.next()
            S.op("dve", lambda e: e.tensor_tensor(T.qd[0][:], qn[:, cs], E[0][:], ALU.mult),
                 reads=[qnR, E[1]], writes=[T.qd[1]])
            A = mats.next()
            S.op("dve", lambda e: e.tensor_tensor(t1[0][:], pG[:, 0:128], Dm[0][:], ALU.mult),
                 reads=[pGR, Dm[1]], writes=[t1[1]])
            S.op("dve", lambda e: e.scalar_tensor_tensor(A[0][:], t1[0][:], gt[:, j:j + 1], C.L, ALU.mult, ALU.mult),
                 reads=[t1[1], gtR, C.constR], writes=[A[1]])
            T.qk = keep.next()
            S.op("dve", lambda e: e.tensor_tensor(t2[0][:], pG[:, 128:256], DmT[0][:], ALU.mult),
                 reads=[pGR, DmT[1]], writes=[t2[1]])
            S.op("dve", lambda e: e.tensor_tensor(T.qk[0][:], t2[0][:], C.U, ALU.mult),
                 reads=[t2[1], C.constR], writes=[T.qk[1]])
            if PSUB < 5:
                return T
            pB, pBR = C.psum.next()
            S.op("pe", lambda e: e.transpose(pB[:, 0:128], A[0][:], C.ident), reads=[A[1], C.constR], writes=[pBR])
            Bm = mats.next()
            Y = mats.next()
            S.op("act", lambda e: e.activation(out=Bm[0][:], in_=pB[:, 0:128], func=AF.Copy), reads=[pBR], writes=[Bm[1]])
            S.op("dve", lambda e: e.tensor_tensor(Y[0][:], C.ident, pB[:, 0:128], ALU.subtract),
                 reads=[pBR, C.constR], writes=[Y[1]])
            if PSUB < 6:
                return T
            Ak, Bk = A, Bm
            for lvl in range(1, 7):
                pA, pAR = C.psum.next()
                S.op("pe", lambda e: e.matmul(pA[:, 0:128], Bk[0][:], Ak[0][:], start=True, stop=True),
                     reads=[Ak[1], Bk[1]], writes=[pAR])
                if lvl < 6:
                    S.op("pe", lambda e: e.matmul(pA[:, 128:256], Ak[0][:], Bk[0][:], start=True, stop=True),
                         reads=[Ak[1], Bk[1]], writes=[pAR])
                An = mats.next()
                S.op("act", lambda e: e.activation(out=An[0][:], in_=pA[:, 0:128], func=AF.Copy),
                     reads=[pAR], writes=[An[1]])
                if lvl < 6:
                    Bn = mats.next()
                    S.op("act", lambda e: e.activation(out=Bn[0][:], in_=pA[:, 128:256], func=AF.Copy),
                         reads=[pAR], writes=[Bn[1]])
                pY, pYR = C.psum.next()
                S.op("pe", lambda e: e.matmul(pY[:, 0:128], An[0][:], Y[0][:], start=True, stop=True),
                     reads=[An[1], Y[1]], writes=[pYR])
                Yn = mats.next()
                S.op("dve", lambda e: e.tensor_tensor(Yn[0][:], Y[0][:], pY[:, 0:128], ALU.add),
                     reads=[Y[1], pYR], writes=[Yn[1]])
                Y = Yn
                Ak = An
                if lvl < 6:
                    Bk = Bn
            if PSUB < 7:
                return T
            pu, puR = C.psum.next()
            S.op("pe", lambda e: e.matmul(pu[:, 0:128], Y[0][:], T.vb[0][:], start=True, stop=True),
                 reads=[Y[1], T.vb[1]], writes=[puR])
            S.op("pe", lambda e: e.matmul(pu[:, 128:256], T.kb[0][:], Y[0][:], start=True, stop=True),
                 reads=[Y[1], T.kb[1]], writes=[puR])
            T.u = keep.next()
            T.wT = keep.next()
            S.op("act", lambda e: e.activation(out=T.u[0][:], in_=pu[:, 0:128], func=AF.Copy), reads=[puR], writes=[T.u[1]])
            S.op("act", lambda e: e.activation(out=T.wT[0][:], in_=pu[:, 128:256], func=AF.Copy),
                 reads=[puR], writes=[T.wT[1]])
            return T

        def tile_rec(b, j, T, po):
            bb = B[b]
            gt, gtR = bb.gt
            St, SR = bb.S
            cs = slice(j * 128, (j + 1) * 128)
            pv, pvR = C.psum.next()
            S.op("pe", lambda e: e.matmul(pv[:, 0:128], T.wT[0][:], St[:], start=True, stop=True),
                 reads=[T.wT[1], SR], writes=[pvR])
            vn = mats.next()
            S.op("dve", lambda e: e.tensor_tensor(vn[0][:], T.u[0][:], pv[:, 0:128], ALU.subtract),
                 reads=[T.u[1], pvR], writes=[vn[1]])
            S.op("pe", lambda e: e.matmul(po[0][:, cs], St[:], T.qd[0][:], start=True, stop=False),
                 reads=[SR, T.qd[1]], writes=[po[1]])
            S.op("pe", lambda e: e.matmul(po[0][:, cs], vn[0][:], T.qk[0][:], start=False, stop=True),
                 reads=[vn[1], T.qk[1]], writes=[po[1]])
            pS, pSR = C.psum.next()
            S.op("pe", lambda e: e.matmul(pS[:, 0:128], T.kd[0][:], vn[0][:], start=True, stop=True),
                 reads=[T.kd[1], vn[1]], writes=[pSR])
            S.op("dve", lambda e: e.scalar_tensor_tensor(St[:], St[:], gt[:, 12 + j:13 + j], pS[:, 0:128],
                                                         ALU.mult, ALU.add),
                 reads=[SR, gtR, pSR], writes=[SR])

        def block_back(b, k, po):
            bb = B[b]
            cv, cvR = bb.cv
            ot, otR = bb.ot
            S.op("act", lambda e: e.activation(out=ot[:], in_=po[0][:, 0:512], func=AF.Copy), reads=[po[1]], writes=[otR])
            S.op("act", lambda e: e.activation(out=cv[:], in_=po[0][:, 0:512], func=AF.Square), reads=[po[1]], writes=[cvR])
            pn, pnR = C.psum.next()
            for q4 in range(4):
                S.op("pe", lambda e: e.matmul(pn[:, q4 * 128:(q4 + 1) * 128], C.ones, cv[:, q4 * 128:(q4 + 1) * 128],
                                              start=True, stop=True), reads=[cvR, C.constR], writes=[pnR])
            S.op("act", lambda e: e.activation(out=cv[:], in_=pn[:, 0:512], func=AF.Sqrt, bias=C.eps[:, 2:3],
                                               scale=1.0 / DH), reads=[pnR, C.constR], writes=[cvR])
            S.op("dve", lambda e: e.reciprocal(cv[:], cv[:]), reads=[cvR], writes=[cvR])
            S.op("dve", lambda e: e.tensor_tensor(ot[:], ot[:], cv[:], ALU.mult), reads=[otR, cvR], writes=[otR])
            og, ogR = bb.og.next()
            S.op("dve", lambda e: e.scalar_tensor_tensor(og[:], ot[:], hp[:, 14:15], bb.zs[0][:], ALU.mult, ALU.mult),
                 reads=[otR, hpR, bb.zs[1]], writes=[ogR])
            t0 = b * SEQ + k * 512
            S.dma("sp", og_dst[:, t0:t0 + 512], og[:], reads=[ogR])

        import os
        nblk = int(os.environ.get("P2_NBLK", SEQ // 512))
        stage = int(os.environ.get("P2_STAGE", 3))
        for k in range(nblk):
            pos = []
            for b in range(2):
                block_front(b, k)
                pos.append(opsum.next())
            for j in range(4):
                if stage >= 1:
                    Ts = [tile_prep(b, j) for b in range(2)]
                if stage >= 2:
                    for b in range(2):
                        tile_rec(b, j, Ts[b], pos[b])
            if stage >= 3:
                for b in range(2):
                    block_back(b, k, pos[b])
        S.barrier()


def _run_tasks(gens):
    gens = list(gens)
    while gens:
        for g in list(gens):
            try:
                next(g)
            except StopIteration:
                gens.remove(g)


def emit_phase2(C, dd, xg, og_dst):
    S, nc = C.S, C.nc
    NB = int(os.environ.get("P2_NBLK", SEQ // 512))
    with ExitStack() as es:
        def sb(name, shape, dt=F32):
            return es.enter_context(nc.sbuf_tensor(name, shape, dt))

        win = sb("win_sb", [128, 8, 514], BF16)
        winR = R("win")
        S.dma("pool", win[:], dd["win"].rearrange("(k p) c -> p k c", p=128), writes=[winR])
        hp = sb("hp_sb", [128, 16])
        hpR = R("hp")
        S.dma("sp", hp[:], dd["hp"], writes=[hpR])
        S.op("act", lambda e: e.activation(out=hp[:, 15:16], in_=hp[:, 12:13], func=AF.Exp), reads=[hpR], writes=[hpR])
        S.op("dve", lambda e: e.tensor_scalar(hp[:, 15:16], hp[:, 15:16], -1.0, None, ALU.mult),
             reads=[hpR], writes=[hpR])
        cb = sb("cb16", [128, 2, 128], BF16)
        cbR = R("cb16")
        S.op("dve", lambda e: e.tensor_copy(cb[:, 0, :], C.ident), reads=[C.constR], writes=[cbR])
        S.op("dve", lambda e: e.tensor_copy(cb[:, 1, :], C.ones), reads=[C.constR], writes=[cbR])
        identb, onesb = cb[:, 0, :], cb[:, 1, :]
        dg = sb("dg16", [128, 12, 128], BF16)
        dgR = R("dg16")
        for t in range(12):
            S.op("dve", lambda e: e.tensor_scalar(dg[:, t, :], C.ident, hp[:, t:t + 1], None, ALU.mult),
                 reads=[C.constR, hpR], writes=[dgR])
        m16 = Ring(es, nc, "m16_", [128, 2, 128], BF16, 22)
        ab16 = Ring(es, nc, "ab16_", [128, 2, 2, 128], BF16, 8)
        kT = Ring(es, nc, "kT_", [128, 2, 128], BF16, 28)
        ks = Ring(es, nc, "ks_", [128, 2, 128], BF16, 8)
        m32 = Ring(es, nc, "m32_", [128, 2, 128], F32, 8)
        u32 = Ring(es, nc, "u32_", [128, 2, 128], F32, 8)
        vn16 = Ring(es, nc, "vn16_", [128, 128], BF16, 8)
        cm2 = sb("cm2", [128, 4, 2, 128])
        cm2R = R("cm2")
        for mi, src in enumerate((C.Lbd, C.Loff, C.U, C.ident)):
            for b in range(2):
                S.op("dve", lambda e: e.tensor_copy(cm2[:, mi, b, :], src), reads=[C.constR], writes=[cm2R])
        pbf = Ring(es, nc, "pb", [128, 1024], BF16, 2, psum=True)
        B = []
        for b in range(2):
            bb = Ctx()
            bb.xb = (sb(f"xb{b}", [128, 8, 512], BF16), R("xb"))
            bb.pre = [(sb(f"pre{b}{c}", [128, 515], BF16), R("pre")) for c in range(3)]
            bb.xq = (sb(f"xq{b}", [128, 512]), R("xq"))
            bb.xk = (sb(f"xk{b}", [128, 512]), R("xk"))
            bb.cv = (sb(f"cv{b}", [128, 512]), R("cv"))
            bb.sq = (sb(f"sq{b}", [128, 512], BF16), R("sq"))
            bb.ot = (sb(f"ot{b}", [128, 512]), R("ot"))
            bb.S32 = (sb(f"S32{b}", [128, 128]), R("S32"))
            bb.S16 = (sb(f"S16{b}", [128, 128], BF16), R("S16"))
            bb.og = Ring(es, nc, f"og{b}", [128, 512], BF16, 2)
            bb.F = []
            for g in range(3):
                f = Ctx()
                f.zs = (sb(f"zs{b}{g}", [128, 512], BF16), R("zs"))
                f.gt = (sb(f"gt{b}{g}", [128, 32]), R("gt"))
                bb.F.append(f)
            bb.G = []
            for g in range(2):
                f = Ctx()
                f.qn = (sb(f"qn{b}{g}", [128, 512], BF16), R("qn"))
                f.kn = (sb(f"kn{b}{g}", [128, 512], BF16), R("kn"))
                f.vs = (sb(f"vs{b}{g}", [128, 512], BF16), R("vs"))
                bb.G.append(f)
            S.op("dve", lambda e: e.memset(bb.S32[0][:], 0.0), writes=[bb.S32[1]])
            S.op("dve", lambda e: e.memset(bb.S16[0][:], 0.0), writes=[bb.S16[1]])
            for c in range(3):
                S.op("dve", lambda e: e.memset(bb.pre[c][0][:, 0:3], 0.0), writes=[bb.pre[c][1]])
            B.append(bb)

        def load_x(b, k):
            rank = 4 * b + k // 4
            off = (k % 4) * 512
            src_ap, src_reads = xg(rank, k % 4)
            S.dma("sp", B[b].xb[0][:], src_ap.rearrange("f p t -> p f t"), reads=src_reads, writes=[B[b].xb[1]])

        def front(b, k):
            bb = B[b]
            xb, xbR = bb.xb
            F = bb.F[k % 3]
            G = bb.G[k % 2]
            gt, gtR = F.gt
            for c in range(4):
                ps, psR = yield from C.psum.acq()
                for kc in range(8):
                    S.op("pe", lambda e: e.matmul(ps[:, 0:512], win[:, kc, c * 128:(c + 1) * 128], xb[:, kc, :],
                                                  start=(kc == 0), stop=(kc == 7)), reads=[winR, xbR], writes=[psR])
                if c < 3:
                    S.op("act", lambda e: e.activation(out=bb.pre[c][0][:, 3:515], in_=ps[:, 0:512], func=AF.Copy),
                         reads=[psR], writes=[bb.pre[c][1]])
                else:
                    cvz, cvzR = bb.cv
                    S.op("act", lambda e: e.activation(out=cvz[:], in_=ps[:, 0:512], func=AF.Exp, scale=-1.0),
                         reads=[psR], writes=[cvzR])
                    yield
                    S.op("act", lambda e: e.activation(out=cvz[:], in_=cvz[:], func=AF.Ln, bias=1.0),
                         reads=[cvzR], writes=[cvzR])
                    yield
                    S.op("act", lambda e: e.activation(out=cvz[:], in_=cvz[:], func=AF.Exp, scale=-1.0),
                         reads=[cvzR], writes=[cvzR])
                    yield
                    S.op("dve", lambda e: e.tensor_tensor(F.zs[0][:], ps[:, 0:512], cvz[:], ALU.mult),
                         reads=[psR, cvzR], writes=[F.zs[1]])
                C.psum.rel((ps, psR))
                yield
            pg, pgR = yield from C.psum.acq()
            for j in range(4):
                for kc in range(8):
                    S.op("pe", lambda e: e.matmul(pg[:, 2 * j:2 * j + 2], xb[:, kc, j * 128:(j + 1) * 128],
                                                  win[:, kc, 512:514], start=(kc == 0), stop=(kc == 7)),
                         reads=[winR, xbR], writes=[pgR])
            if k + 1 < NB:
                load_x(b, k + 1)
            yield
            S.op("act", lambda e: e.activation(out=gt[:, 0:4], in_=pg[:, 0:8].rearrange("p (j c) -> p j c", c=2)[:, :, 0],
                                               func=AF.Exp, scale=-1.0), reads=[pgR], writes=[gtR])
            S.op("act", lambda e: e.activation(out=gt[:, 24:28],
                                               in_=pg[:, 0:8].rearrange("p (j c) -> p j c", c=2)[:, :, 1],
                                               func=AF.Exp, bias=hp[:, 13:14]), reads=[pgR, hpR], writes=[gtR])
            C.psum.rel((pg, pgR))
            yield
            S.op("act", lambda e: e.activation(out=gt[:, 24:28], in_=gt[:, 24:28], func=AF.Ln, bias=1.0),
                 reads=[gtR], writes=[gtR])
            S.op("dve", lambda e: e.tensor_scalar(gt[:, 0:4], gt[:, 0:4], 1.0, None, ALU.add), reads=[gtR], writes=[gtR])
            S.op("dve", lambda e: e.reciprocal(gt[:, 0:4], gt[:, 0:4]), reads=[gtR], writes=[gtR])
            yield
            S.op("dve", lambda e: e.tensor_scalar(gt[:, 4:8], gt[:, 24:28], hp[:, 15:16], None, ALU.mult),
                 reads=[gtR, hpR], writes=[gtR])
            yield
            pc, pcR = yield from C.psum.acq()
            S.op("pe", lambda e: e.matmul(pc[:, 0:4], C.U, gt[:, 4:8], start=True, stop=True),
                 reads=[gtR, C.constR], writes=[pcR])
            S.op("pe", lambda e: e.matmul(pc[:, 4:8], C.ones, gt[:, 4:8], start=True, stop=True),
                 reads=[gtR, C.constR], writes=[pcR])
            yield
            S.op("dve", lambda e: e.tensor_copy(gt[:, 8:12], pc[:, 0:4]), reads=[pcR], writes=[gtR])
            S.op("dve", lambda e: e.tensor_tensor(gt[:, 24:28], pc[:, 4:8], gt[:, 8:12], ALU.subtract),
                 reads=[pcR, gtR], writes=[gtR])
            S.op("act", lambda e: e.activation(out=gt[:, 12:16], in_=pc[:, 4:8], func=AF.Exp), reads=[pcR], writes=[gtR])
            S.op("act", lambda e: e.activation(out=gt[:, 20:24], in_=pc[:, 0:4], func=AF.Exp), reads=[pcR], writes=[gtR])
            C.psum.rel((pc, pcR))
            yield
            S.op("act", lambda e: e.activation(out=gt[:, 16:20], in_=gt[:, 24:28], func=AF.Exp), reads=[gtR], writes=[gtR])
            S.op("dve", lambda e: e.tensor_tensor(gt[:, 20:24], gt[:, 20:24], gt[:, 0:4], ALU.mult),
                 reads=[gtR], writes=[gtR])
            yield
            dsts = (bb.xq, bb.xk, G.vs)
            scr = (bb.xq, bb.xk, bb.xq)
            for c in (2, 0, 1):
                pre, preR = bb.pre[c]
                pcv, pcvR = yield from C.psum.acq()
                for j in range(4):
                    S.op("pe", lambda e: e.matmul(pcv[:, 0:512], dg[:, 4 * c + j, :], pre[:, j:j + 512],
                                                  start=(j == 0), stop=(j == 3)), reads=[dgR, preR], writes=[pcvR])
                S.op("dve", lambda e: e.tensor_copy(pre[:, 0:3], pre[:, 512:515]), reads=[preR], writes=[preR])
                yield
                sg, sgR = scr[c]
                S.op("act", lambda e: e.activation(out=sg[:], in_=pcv[:, 0:512], func=AF.Exp, scale=-1.0),
                     reads=[pcvR], writes=[sgR])
                yield
                S.op("act", lambda e: e.activation(out=sg[:], in_=sg[:], func=AF.Ln, bias=1.0), reads=[sgR], writes=[sgR])
                yield
                S.op("act", lambda e: e.activation(out=sg[:], in_=sg[:], func=AF.Exp, scale=-1.0), reads=[sgR], writes=[sgR])
                yield
                S.op("dve", lambda e: e.tensor_tensor(dsts[c][0][:], pcv[:, 0:512], sg[:], ALU.mult),
                     reads=[pcvR, sgR], writes=[dsts[c][1]])
                C.psum.rel((pcv, pcvR))
                yield
            cv, cvR = bb.cv
            sq, sqR = bb.sq
            for c, src, dst in ((0, bb.xq, G.qn), (1, bb.xk, G.kn)):
                S.op("act", lambda e: e.activation(out=sq[:], in_=src[0][:], func=AF.Square), reads=[src[1]], writes=[sqR])
                yield
                pn, pnR = yield from C.psum.acq()
                S.op("pe", lambda e: e.matmul(pn[:, 0:512], onesb, sq[:], start=True, stop=True),
                     reads=[sqR, cbR], writes=[pnR])
                yield
                S.op("act", lambda e: e.activation(out=cv[:], in_=pn[:, 0:512], func=AF.Ln, bias=C.eps[:, 1:2]),
                     reads=[pnR, C.constR], writes=[cvR])
                C.psum.rel((pn, pnR))
                yield
                S.op("act", lambda e: e.activation(out=cv[:], in_=cv[:], func=AF.Exp, scale=-0.5), reads=[cvR], writes=[cvR])
                yield
                if c == 0:
                    S.op("dve", lambda e: e.scalar_tensor_tensor(dst[0][:], src[0][:], DH ** -0.5, cv[:],
                                                                 ALU.mult, ALU.mult),
                         reads=[src[1], cvR], writes=[dst[1]])
                else:
                    S.op("dve", lambda e: e.tensor_tensor(dst[0][:], src[0][:], cv[:], ALU.mult),
                         reads=[src[1], cvR], writes=[dst[1]])
                yield

        Tres = {}

        def prep_pair(k, j):
            cs = slice(j * 128, (j + 1) * 128)
            Fs = [B[b].F[k % 3] for b in range(2)]
            Gs = [B[b].G[k % 2] for b in range(2)]
            gts = [f.gt for f in Fs]
            Ts = []
            for b in range(2):
                T = Ctx()
                Tres[(b, k, j)] = T
                Ts.append(T)
            ptb, ptR = yield from pbf.acq()
            for b in range(2):
                S.op("pe", lambda e: e.transpose(ptb[:, b * 128:(b + 1) * 128], Gs[b].kn[0][:, cs], identb),
                     reads=[Gs[b].kn[1], cbR], writes=[ptR])
                S.op("pe", lambda e: e.transpose(ptb[:, 256 + b * 128:256 + (b + 1) * 128], Gs[b].vs[0][:, cs], identb),
                     reads=[Gs[b].vs[1], cbR], writes=[ptR])
            gU = m32.next()
            for b in range(2):
                S.op("dve", lambda e: e.tensor_scalar(gU[0][:, b, :], C.U, gts[b][0][:, 4 + j:5 + j], None, ALU.mult),
                     reads=[gts[b][1], C.constR], writes=[gU[1]])
            yield
            kb = ks.next()
            kd = kT.next()
            vb = ks.next()
            for b in range(2):
                gt, gtR = gts[b]
                S.op("act", lambda e: e.activation(out=kb[0][:, b, :], in_=ptb[:, b * 128:(b + 1) * 128], func=AF.Identity,
                                                   scale=gt[:, 20 + j:21 + j]), reads=[ptR, gtR], writes=[kb[1]])
                S.op("act", lambda e: e.activation(out=kd[0][:, b, :], in_=ptb[:, b * 128:(b + 1) * 128], func=AF.Identity,
                                                   scale=gt[:, 16 + j:17 + j]), reads=[ptR, gtR], writes=[kd[1]])
                S.op("dve", lambda e: e.tensor_scalar(vb[0][:, b, :], ptb[:, 256 + b * 128:256 + (b + 1) * 128],
                                                      gt[:, j:j + 1], None, ALU.mult),
                     reads=[ptR, gtR], writes=[vb[1]])
            pbf.rel((ptb, ptR))
            pR, pRR = yield from C.psum.acq()
            for b in range(2):
                S.op("pe", lambda e: e.matmul(pR[:, b * 128:(b + 1) * 128], C.ones, gU[0][:, b, :], start=True, stop=True),
                     reads=[gU[1], C.constR], writes=[pRR])
            yield
            t1 = m32.next()
            t2 = m32.next()
            E = m16.next()
            for b in range(2):
                gt, gtR = gts[b]
                gc = gt[:, 8 + j:9 + j]
                S.op("dve", lambda e: e.tensor_scalar(t1[0][:, b, :], pR[:, b * 128:(b + 1) * 128], gc, 0.0,
                                                      ALU.subtract, ALU.max), reads=[pRR, gtR], writes=[t1[1]])
                S.op("dve", lambda e: e.tensor_scalar(t2[0][:, b, :], pR[:, b * 128:(b + 1) * 128], gc, 0.0,
                                                      ALU.subtract, ALU.min), reads=[pRR, gtR], writes=[t2[1]])
            S.op("act", lambda e: e.activation(out=E[0][:].rearrange("p b c -> p (b c)"), in_=pR[:, 0:256], func=AF.Exp),
                 reads=[pRR], writes=[E[1]])
            C.psum.rel((pR, pRR))
            yield
            Dm = m16.next()
            DmT = m16.next()
            S.op("act", lambda e: e.activation(out=Dm[0][:], in_=t1[0][:], func=AF.Exp, scale=-1.0),
                 reads=[t1[1]], writes=[Dm[1]])
            S.op("act", lambda e: e.activation(out=DmT[0][:], in_=t2[0][:], func=AF.Exp), reads=[t2[1]], writes=[DmT[1]])
            qd = kT.next()
            for b in range(2):
                S.op("dve", lambda e: e.tensor_tensor(qd[0][:, b, :], Gs[b].qn[0][:, cs], E[0][:, b, :], ALU.mult),
                     reads=[Gs[b].qn[1], E[1]], writes=[qd[1]])
            pG, pGR = yield from C.psum.acq()
            for b in range(2):
                kn, knR = Gs[b].kn
                qn, qnR = Gs[b].qn
                S.op("pe", lambda e: e.matmul(pG[:, b * 256:b * 256 + 128], kn[:, cs], kn[:, cs], start=True, stop=True),
                     reads=[knR], writes=[pGR])
                S.op("pe", lambda e: e.matmul(pG[:, b * 256 + 128:b * 256 + 256], kn[:, cs], qn[:, cs],
                                              start=True, stop=True), reads=[knR, qnR], writes=[pGR])
            yield
            a0 = m32.next()
            for b in range(2):
                gt, gtR = gts[b]
                S.op("dve", lambda e: e.scalar_tensor_tensor(a0[0][:, b, :], pG[:, b * 256:b * 256 + 128], gt[:, j:j + 1],
                                                             Dm[0][:, b, :], ALU.mult, ALU.mult),
                     reads=[pGR, gtR, Dm[1]], writes=[a0[1]])
            q0 = m16.next()
            S.op("dve", lambda e: e.tensor_tensor(q0[0][:], pG[:, 0:512].rearrange("p (b h c) -> p b h c", b=2, h=2)[:, :, 1, :],
                                                  DmT[0][:], ALU.mult), reads=[pGR, DmT[1]], writes=[q0[1]])
            C.psum.rel((pG, pGR))
            yield
            A = m16.next()
            Aoff = ks.next()
            qk = kT.next()
            S.op("dve", lambda e: e.tensor_tensor(A[0][:], a0[0][:], cm2[:, 0], ALU.mult),
                 reads=[a0[1], cm2R], writes=[A[1]])
            S.op("dve", lambda e: e.tensor_tensor(Aoff[0][:], a0[0][:], cm2[:, 1], ALU.mult),
                 reads=[a0[1], cm2R], writes=[Aoff[1]])
            S.op("dve", lambda e: e.tensor_tensor(qk[0][:], q0[0][:], cm2[:, 2], ALU.mult),
                 reads=[q0[1], cm2R], writes=[qk[1]])
            yield
            pBb, pBR = yield from pbf.acq()
            for b in range(2):
                S.op("pe", lambda e: e.transpose(pBb[:, b * 128:(b + 1) * 128], A[0][:, b, :], identb),
                     reads=[A[1], cbR], writes=[pBR])
            yield
            Bm = m16.next()
            Y = m16.next()
            S.op("act", lambda e: e.activation(out=Bm[0][:].rearrange("p b c -> p (b c)"), in_=pBb[:, 0:256], func=AF.Copy),
                 reads=[pBR], writes=[Bm[1]])
            S.op("dve", lambda e: e.tensor_tensor(Y[0][:].rearrange("p b c -> p (b c)"),
                                                  cm2[:, 3].rearrange("p b c -> p (b c)"), pBb[:, 0:256], ALU.subtract),
                 reads=[pBR, cm2R], writes=[Y[1]])
            pbf.rel((pBb, pBR))
            yield
            NL = 5

            def squares(pA, pAR, Ak, Bk, ABR, with_b):
                for b in range(2):
                    S.op("pe", lambda e: e.matmul(pA[:, b * 256:b * 256 + 128], Bk(b), Ak(b), start=True, stop=True),
                         reads=ABR, writes=[pAR])
                    if with_b:
                        S.op("pe", lambda e: e.matmul(pA[:, b * 256 + 128:b * 256 + 256], Ak(b), Bk(b),
                                                      start=True, stop=True), reads=ABR, writes=[pAR])

            def evac(pA, pAR, with_b):
                AB = ab16.next()
                if with_b:
                    S.op("act", lambda e: e.activation(out=AB[0][:].rearrange("p b h c -> p (b h c)"), in_=pA[:, 0:512],
                                                       func=AF.Copy), reads=[pAR], writes=[AB[1]])
                else:
                    S.op("act", lambda e: e.activation(
                        out=AB[0][:, :, 0, :], in_=pA[:, 0:512].rearrange("p (b h c) -> p b h c", b=2, h=2)[:, :, 0, :],
                        func=AF.Copy), reads=[pAR], writes=[AB[1]])
                C.psum.rel((pA, pAR))
                return AB

            pA, pAR = yield from C.psum.acq()
            squares(pA, pAR, lambda b: A[0][:, b, :], lambda b: Bm[0][:, b, :], [A[1], Bm[1]], NL > 1)
            yield
            AB = evac(pA, pAR, NL > 1)
            yield
            for lvl in range(1, NL + 1):
                An = (lambda AB_: (lambda b: AB_[0][:, b, 0, :]))(AB)
                Bn = (lambda AB_: (lambda b: AB_[0][:, b, 1, :]))(AB)
                pY, pYR = yield from C.psum.acq()
                for b in range(2):
                    S.op("pe", lambda e: e.matmul(pY[:, b * 128:(b + 1) * 128], An(b), Y[0][:, b, :], start=True, stop=True),
                         reads=[AB[1], Y[1]], writes=[pYR])
                if lvl < NL:
                    pA, pAR = yield from C.psum.acq()
                    squares(pA, pAR, An, Bn, [AB[1]], lvl + 1 < NL)
                yield
                Yn = m16.next()
                S.op("dve", lambda e: e.tensor_tensor(Yn[0][:].rearrange("p b c -> p (b c)"),
                                                      Y[0][:].rearrange("p b c -> p (b c)"), pY[:, 0:256], ALU.add),
                     reads=[Y[1], pYR], writes=[Yn[1]])
                C.psum.rel((pY, pYR))
                Y = Yn
                if lvl < NL:
                    AB = evac(pA, pAR, lvl + 1 < NL)
                yield
            pZb, pZR = yield from pbf.acq()
            pT, pTR = yield from C.psum.acq()
            for b in range(2):
                S.op("pe", lambda e: e.transpose(pZb[:, b * 128:(b + 1) * 128], Y[0][:, b, :], identb),
                     reads=[Y[1], cbR], writes=[pZR])
                S.op("pe", lambda e: e.matmul(pT[:, b * 128:(b + 1) * 128], Aoff[0][:, b, :], Y[0][:, b, :],
                                              start=True, stop=True), reads=[Aoff[1], Y[1]], writes=[pTR])
            yield
            Zd = m16.next()
            T1 = m16.next()
            S.op("act", lambda e: e.activation(out=Zd[0][:].rearrange("p b c -> p (b c)"), in_=pZb[:, 0:256], func=AF.Copy),
                 reads=[pZR], writes=[Zd[1]])
            S.op("act", lambda e: e.activation(out=T1[0][:].rearrange("p b c -> p (b c)"), in_=pT[:, 0:256], func=AF.Copy),
                 reads=[pTR], writes=[T1[1]])
            pbf.rel((pZb, pZR))
            C.psum.rel((pT, pTR))
            yield
            pY, pYR = yield from C.psum.acq()
            for b in range(2):
                S.op("pe", lambda e: e.matmul(pY[:, b * 128:(b + 1) * 128], Zd[0][:, b, :], T1[0][:, b, :],
                                              start=True, stop=True), reads=[Zd[1], T1[1]], writes=[pYR])
            yield
            Yf = m16.next()
            S.op("dve", lambda e: e.tensor_tensor(Yf[0][:].rearrange("p b c -> p (b c)"),
                                                  Y[0][:].rearrange("p b c -> p (b c)"), pY[:, 0:256], ALU.subtract),
                 reads=[Y[1], pYR], writes=[Yf[1]])
            C.psum.rel((pY, pYR))
            yield
            pu, puR = yield from C.psum.acq()
            for b in range(2):
                S.op("pe", lambda e: e.matmul(pu[:, b * 256:b * 256 + 128], Yf[0][:, b, :], vb[0][:, b, :],
                                              start=True, stop=True), reads=[Yf[1], vb[1]], writes=[puR])
                S.op("pe", lambda e: e.matmul(pu[:, b * 256 + 128:b * 256 + 256], kb[0][:, b, :], Yf[0][:, b, :],
                                              start=True, stop=True), reads=[Yf[1], kb[1]], writes=[puR])
            yield
            u = u32.next()
            wT = kT.next()
            puv = pu[:, 0:512].rearrange("p (b h c) -> p b h c", b=2, h=2)
            S.op("act", lambda e: e.activation(out=u[0][:], in_=puv[:, :, 0, :], func=AF.Copy), reads=[puR], writes=[u[1]])
            S.op("act", lambda e: e.activation(out=wT[0][:], in_=puv[:, :, 1, :], func=AF.Copy), reads=[puR], writes=[wT[1]])
            C.psum.rel((pu, puR))
            for b in range(2):
                T = Ts[b]
                T.kd = (kd[0][:, b, :], kd[1])
                T.qd = (qd[0][:, b, :], qd[1])
                T.qk = (qk[0][:, b, :], qk[1])
                T.u = (u[0][:, b, :], u[1])
                T.wT = (wT[0][:, b, :], wT[1])
            yield

        def rec_chain(b, k):
            bb = B[b]
            F = bb.F[k % 3]
            gt, gtR = F.gt
            S32, S32R = bb.S32
            S16, S16R = bb.S16
            ot, otR = bb.ot
            for j in range(4):
                T = Tres.pop((b, k, j))
                cs = slice(j * 128, (j + 1) * 128)
                pv, pvR = yield from C.psum.acq()
                S.op("pe", lambda e: e.matmul(pv[:, 0:128], T.wT[0], S16[:], start=True, stop=True),
                     reads=[T.wT[1], S16R], writes=[pvR])
                yield
                vn = vn16.next()
                S.op("dve", lambda e: e.tensor_tensor(vn[0][:], T.u[0], pv[:, 0:128], ALU.subtract),
                     reads=[T.u[1], pvR], writes=[vn[1]])
                C.psum.rel((pv, pvR))
                yield
                pS, pSR = yield from C.psum.acq()
                S.op("pe", lambda e: e.matmul(pS[:, 0:128], T.kd[0], vn[0][:], start=True, stop=True),
                     reads=[T.kd[1], vn[1]], writes=[pSR])
                S.op("pe", lambda e: e.matmul(pS[:, 128:256], S16[:], T.qd[0], start=True, stop=False),
                     reads=[S16R, T.qd[1]], writes=[pSR])
                S.op("pe", lambda e: e.matmul(pS[:, 128:256], vn[0][:], T.qk[0], start=False, stop=True),
                     reads=[vn[1], T.qk[1]], writes=[pSR])
                yield
                gl = gt[:, 12 + j:13 + j]
                S.op("dve", lambda e: e.scalar_tensor_tensor(S16[:], S32[:], gl, pS[:, 0:128], ALU.mult, ALU.add),
                     reads=[S32R, gtR, pSR], writes=[S16R])
                S.op("dve", lambda e: e.scalar_tensor_tensor(S32[:], S32[:], gl, pS[:, 0:128], ALU.mult, ALU.add),
                     reads=[S32R, gtR, pSR], writes=[S32R])
                S.op("dve", lambda e: e.tensor_copy(ot[:, cs], pS[:, 128:256]), reads=[pSR], writes=[otR])
                C.psum.rel((pS, pSR))
                yield
            sq, sqR = bb.sq
            cv, cvR = bb.cv
            S.op("act", lambda e: e.activation(out=sq[:], in_=ot[:], func=AF.Square), reads=[otR], writes=[sqR])
            yield
            pn, pnR = yield from C.psum.acq()
            S.op("pe", lambda e: e.matmul(pn[:, 0:512], onesb, sq[:], start=True, stop=True),
                 reads=[sqR, cbR], writes=[pnR])
            yield
            S.op("act", lambda e: e.activation(out=cv[:], in_=pn[:, 0:512], func=AF.Ln, bias=C.eps[:, 2:3],
                                               scale=1.0 / DH), reads=[pnR, C.constR], writes=[cvR])
            C.psum.rel((pn, pnR))
            yield
            S.op("act", lambda e: e.activation(out=cv[:], in_=cv[:], func=AF.Exp, scale=-0.5), reads=[cvR], writes=[cvR])
            yield
            S.op("dve", lambda e: e.tensor_tensor(ot[:], ot[:], cv[:], ALU.mult), reads=[otR, cvR], writes=[otR])
            yield
            og, ogR = bb.og.next()
            S.op("dve", lambda e: e.scalar_tensor_tensor(og[:], ot[:], hp[:, 14:15], F.zs[0][:], ALU.mult, ALU.mult),
                 reads=[otR, hpR, F.zs[1]], writes=[ogR])
            S.dma("sp", og_dst.dst(b, k), og[:], reads=[ogR])
            og_dst.done(b, k, S.last_dma)
            yield

        for b in range(2):
            load_x(b, 0)
        for it in range(NB + 2):
            fronts = [front(b, it) for b in range(2)] if it < NB else []
            tasks = []
            if 0 <= it - 2 < NB:
                tasks += [rec_chain(b, it - 2) for b in range(2)]
            if 0 <= it - 1 < NB:
                tasks += [prep_pair(it - 1, j) for j in range(2)]
            while tasks:
                for g in list(tasks):
                    try:
                        next(g)
                    except StopIteration:
                        tasks.remove(g)
                for g in list(fronts):
                    try:
                        next(g)
                    except StopIteration:
                        fronts.remove(g)
            if 0 <= it - 2 < NB and (it - 2) % 4 == 3:
                og_dst.group_done((it - 2) // 4)
            tasks = list(fronts)
            if 0 <= it - 1 < NB:
                tasks += [prep_pair(it - 1, j) for j in range(2, 4)]
            _run_tasks(tasks)
        S.barrier(skip_cc=True)


def _band_mats(first_segment):
    band = np.zeros((128, 4, 3, 128), np.float32)
    tp = np.arange(128)[:, None]
    t = np.arange(128)[None, :]
    for g, w in enumerate((2, 4, 8, 16)):
        own = ((t - tp >= 0) & (t - tp < w)).astype(np.float32) / w - (t == tp).astype(np.float32)
        prev = ((t + 128 - tp) < w).astype(np.float32) / w
        band[:, g, 0, :] = prev
        band[:, g, 2, :] = own
        if first_segment:
            cnt = np.minimum(t + 1, w).astype(np.float32)
            band[:, g, 1, :] = ((t - tp >= 0) & (t - tp < w)).astype(np.float32) / cnt - (t == tp).astype(np.float32)
        else:
            band[:, g, 1, :] = own
    return band


def _cmat():
    c = np.zeros((128, 6, 128), np.float32)
    i = np.arange(128)[:, None]
    j = np.arange(128)[None, :]
    c[:, 0, :] = (i == j)
    c[:, 1, :] = 1.0
    c[:, 2, :] = (i <= j)
    c[:, 3, :] = (i > j)
    c[:, 4, :] = (i > j) & ((i // 64) == (j // 64))
    c[:, 5, :] = (i >= 64) & (j < 64)
    return c


def _dram_in(nc, name, shape, dt=F32):
    return nc.dram_tensor(name, list(shape), dt, kind="ExternalInput")


W_NAMES = ("pool_w", "mlp_w10", "mlp_w20", "ple_gate_w0", "ple_proj0",
           "gdn_w_out", "mlp_w11", "mlp_w21", "ple_gate_w1", "ple_proj1")
W_SHAPES = {"pool_w": (4, 256, 256), "mlp_w10": (D, DFF), "mlp_w20": (DFF, D), "ple_gate_w0": (D, D),
            "ple_proj0": (256, D), "gdn_w_out": (D, D), "mlp_w11": (D, DFF), "mlp_w21": (DFF, D),
            "ple_gate_w1": (D, D), "ple_proj1": (256, D)}


def build(mode):
    nc = bass.Bass("TRN2", target_bir_lowering=False)
    dd = {}
    with ExitStack() as es:
        S = Sched(nc, es)
        vecs_h = _dram_in(nc, "vecs", (12, D))
        cmat_h = _dram_in(nc, "cmat", (128, 6, 128))
        C = setup_common(nc, es, S, vecs_h, cmat_h.ap(), tok=(mode != "p2"))
        if mode in ("p1", "fused"):
            for n in ("pool_w", "mlp_w10", "mlp_w20", "ple_gate_w0", "ple_proj0"):
                dd[n] = _dram_in(nc, n, W_SHAPES[n]).ap()
            dd["x"] = _dram_in(nc, "x", (128 + TOK, D)).ap()
            dd["p0"] = _dram_in(nc, "p0", (TOK, 256)).ap()
            dd["band"] = _dram_in(nc, "band", (128, 4, 3, 128)).ap()
        if mode in ("p2", "fused"):
            dd["win"] = _dram_in(nc, "win", (D, 514)).ap()
            dd["hp"] = _dram_in(nc, "hp", (128, 16)).ap()
        if mode in ("p3", "fused"):
            for n in ("gdn_w_out", "mlp_w11", "mlp_w21", "ple_gate_w1", "ple_proj1"):
                dd[n] = _dram_in(nc, n, W_SHAPES[n]).ap()
            dd["p1"] = _dram_in(nc, "p1", (TOK, 256)).ap()
        if mode == "p1":
            accout = nc.dram_tensor("accout", [TOK, D], F32, kind="ExternalOutput").ap()
            x1T = nc.dram_tensor("x1T", [8, 128, TOK], BF16, kind="ExternalOutput").ap()

            def export_q(q):
                S.dma("sp", x1T[:, :, q * 512:(q + 1) * 512].rearrange("f p t -> p f t"),
                      C.xT[:, :, q * 512:(q + 1) * 512], reads=C.xTR[4 * q:4 * q + 4])
            emit_phase1(C, dd, export_q)
            for i in range(NT):
                S.dma("sp", accout[i * 128:(i + 1) * 128, :], C.acc[:, i, :], reads=[C.accR[i]])
        elif mode == "p2":
            xg = _dram_in(nc, "xg", (8, 8, 128, TOK), BF16).ap()
            og = nc.dram_tensor("og", [128, 2 * SEQ], BF16, kind="ExternalOutput").ap()

            class OGp2:
                def dst(self, b, k):
                    t0 = b * SEQ + k * 512
                    return og[:, t0:t0 + 512]

                def done(self, b, k, dep):
                    pass

                def group_done(self, g):
                    pass
            emit_phase2(C, dd, lambda rank, q: (xg[rank, :, :, q * 512:(q + 1) * 512], []), OGp2())
        elif mode == "fused":
            gidx_d = _dram_in(nc, "gidx", (128, 32), I32).ap()
            out = nc.dram_tensor("out", [TOK, D], F32, kind="ExternalOutput").ap()
            RG = [list(range(NCORES))]
            x1s = [nc.dram_tensor(f"x1T_src{q}", [8 * 128, 512], BF16).ap() for q in range(4)]
            xgs = [nc.dram_tensor(f"xg_all{q}", [8 * 8 * 128, 512], BF16, addr_space="Shared").ap() for q in range(4)]
            ogs = [nc.dram_tensor(f"og_src{g}", [128, 2 * TOK], BF16).ap() for g in range(4)]
            ogas = [nc.dram_tensor(f"og_all{g}", [8 * 128, 2 * TOK], BF16, addr_space="Shared").ap() for g in range(4)]
            x1R = [R(f"x1T_src{q}") for q in range(4)]
            xgR = [R(f"xg_all{q}") for q in range(4)]
            ogaR = [R(f"og_all{g}") for g in range(4)]
            gi = es.enter_context(nc.sbuf_tensor("gidx_sb", [128, 32], I32))
            giR = R("gidx")
            S.dma("sp", gi[:], gidx_d, writes=[giR])

            def export_q(q):
                S.dma("sp", x1s[q].rearrange("(f p) t -> p f t", p=128), C.xT[:, :, q * 512:(q + 1) * 512],
                      reads=C.xTR[4 * q:4 * q + 4], writes=[x1R[q]])
                S.collective(lambda e: e.collective_compute("AllGather", ALU.bypass, replica_groups=RG,
                                                            ins=[x1s[q].opt()], outs=[xgs[q].opt()]),
                             reads=[x1R[q]], writes=[xgR[q]])
            emit_phase1(C, dd, export_q)

            class OGf:
                def __init__(self):
                    self.deps = [dict() for _ in range(4)]

                def dst(self, b, k):
                    c0 = b * TOK + (k % 4) * 512
                    return ogs[k // 4][:, c0:c0 + 512]

                def done(self, b, k, dep):
                    src, v = dep
                    d = self.deps[k // 4]
                    d[src] = max(d.get(src, 0), v)

                def group_done(self, g):
                    S.collective(lambda e: e.collective_compute("AllGather", ALU.bypass, replica_groups=RG,
                                                                ins=[ogs[g].opt()], outs=[ogas[g].opt()]),
                                 writes=[ogaR[g]], extra=self.deps[g])

            def xg_blk(rank, q):
                return xgs[q].rearrange("(r f p) t -> r f p t", r=8, f=8)[rank], [xgR[q]]
            emit_phase2(C, dd, xg_blk, OGf())

            def load_og(C):
                for g in range(4):
                    rows = ogas[g].rearrange("r (c t) -> (r c) t", t=TOK)
                    for h in range(8):
                        S.dma("pool", None, None, reads=[ogaR[g], giR], writes=C.xTR,
                              fn=lambda e: e.indirect_dma_start(
                                  out=C.xT[:, h, :], out_offset=None, in_=rows,
                                  in_offset=bass.IndirectOffsetOnAxis(ap=gi[:, g * 8 + h:g * 8 + h + 1], axis=0),
                                  bounds_check=8 * 128 * 2 - 1, oob_is_err=False))
            emit_phase3(C, dd, load_og, out)
        elif mode == "p3":
            accin = _dram_in(nc, "accin", (TOK, D)).ap()
            ogin = _dram_in(nc, "ogin", (8, 128, TOK), BF16).ap()
            out = nc.dram_tensor("out", [TOK, D], F32, kind="ExternalOutput").ap()
            for i in range(NT):
                S.dma("sp", C.acc[:, i, :], accin[i * 128:(i + 1) * 128, :], writes=[C.accR[i]])

            def load_og(C):
                for h in range(8):
                    S.dma("sp", C.xT[:, h, :], ogin[h], writes=C.xTR)
            emit_phase3(C, dd, load_og, out)
        S.finish()
    return nc


def run_fused(x, p, ln_gain, ln_bias, pool_w, pool_b, pool_scale, gdn_w_in, gdn_conv, gdn_a_log, gdn_dt_bias,
              gdn_norm_w, gdn_w_out, mlp_w1, mlp_w2, ple_gate_w, ple_gate_b, ple_proj):
    com = _common_inputs(ln_gain, ln_bias, pool_b, pool_scale, ple_gate_b)
    shared = {"pool_w": np.ascontiguousarray(pool_w[0]), "mlp_w10": np.ascontiguousarray(mlp_w1[0]),
              "mlp_w20": np.ascontiguousarray(mlp_w2[0]), "ple_gate_w0": np.ascontiguousarray(ple_gate_w[0]),
              "ple_proj0": np.ascontiguousarray(ple_proj[0]), "gdn_w_out": np.ascontiguousarray(gdn_w_out[0]),
              "mlp_w11": np.ascontiguousarray(mlp_w1[1]), "mlp_w21": np.ascontiguousarray(mlp_w2[1]),
              "ple_gate_w1": np.ascontiguousarray(ple_gate_w[1]), "ple_proj1": np.ascontiguousarray(ple_proj[1])}
    ins = []
    for r in range(NCORES):
        b, s = r // 4, r % 4
        xs = np.zeros((128 + TOK, D), np.float32)
        xs[128:] = x[b, s * TOK:(s + 1) * TOK]
        if s > 0:
            xs[:128] = x[b, s * TOK - 128:s * TOK]
        m = dict(com)
        m.update(shared)
        m.update(_p2_inputs(r, gdn_w_in, gdn_conv, gdn_a_log, gdn_dt_bias, gdn_norm_w))
        gidx = np.full((128, 32), 1 << 24, np.int32)
        gidx[:, s * 8:(s + 1) * 8] = (np.arange(8)[None, :] * 128 + np.arange(128)[:, None]) * 2 + b
        m.update({"x": xs, "p0": np.ascontiguousarray(p[0, b, s * TOK:(s + 1) * TOK]),
                  "p1": np.ascontiguousarray(p[1, b, s * TOK:(s + 1) * TOK]), "band": _band_mats(s == 0),
                  "gidx": gidx})
        ins.append(m)
    res = run_bass_kernel_spmd(_get_nc("fused"), ins, core_ids=list(range(NCORES)))
    out = np.zeros((2, SEQ, D), np.float32)
    for r in range(NCORES):
        b, s = r // 4, r % 4
        out[b, s * TOK:(s + 1) * TOK] = res.results[r]["out"]
    return out


_NC_CACHE = {}


def _get_nc(mode):
    if mode not in _NC_CACHE:
        _NC_CACHE[mode] = build(mode)
    return _NC_CACHE[mode]


def _common_inputs(ln_gain, ln_bias, pool_b, pool_scale, ple_gate_b):
    vecs = np.concatenate([
        np.asarray(ln_gain, np.float32).reshape(4, D), np.asarray(ln_bias, np.float32).reshape(4, D),
        np.asarray(pool_b, np.float32).reshape(1, D), np.asarray(pool_scale, np.float32).reshape(1, D),
        np.asarray(ple_gate_b, np.float32).reshape(2, D)], axis=0)
    return {"vecs": np.ascontiguousarray(vecs), "cmat": _cmat()}


def run_p1(x, p, ln_gain, ln_bias, pool_w, pool_b, pool_scale, mlp_w1, mlp_w2, ple_gate_w, ple_gate_b, ple_proj):
    com = _common_inputs(ln_gain, ln_bias, pool_b, pool_scale, ple_gate_b)
    ins = []
    for r in range(NCORES):
        b, s = r // 4, r % 4
        xs = np.zeros((128 + TOK, D), np.float32)
        xs[128:] = x[b, s * TOK:(s + 1) * TOK]
        if s > 0:
            xs[:128] = x[b, s * TOK - 128:s * TOK]
        m = dict(com)
        m.update({"x": xs, "p0": np.ascontiguousarray(p[0, b, s * TOK:(s + 1) * TOK]), "band": _band_mats(s == 0),
                  "pool_w": np.ascontiguousarray(pool_w[0]), "mlp_w10": np.ascontiguousarray(mlp_w1[0]),
                  "mlp_w20": np.ascontiguousarray(mlp_w2[0]), "ple_gate_w0": np.ascontiguousarray(ple_gate_w[0]),
                  "ple_proj0": np.ascontiguousarray(ple_proj[0])})
        ins.append(m)
    res = run_bass_kernel_spmd(_get_nc("p1"), ins, core_ids=list(range(NCORES)))
    return [r_["accout"] for r_ in res.results], [r_["x1T"] for r_ in res.results]


def _p2_inputs(r, gdn_w_in, gdn_conv, gdn_a_log, gdn_dt_bias, gdn_norm_w):
    w = gdn_w_in[0]
    cols = np.concatenate([np.arange(r * 128, (r + 1) * 128), 1024 + np.arange(r * 128, (r + 1) * 128),
                           2048 + np.arange(r * 128, (r + 1) * 128), 3072 + np.arange(r * 128, (r + 1) * 128),
                           np.array([4096 + r, 4104 + r])])
    win = np.ascontiguousarray(w[:, cols])
    hp = np.zeros((128, 16), np.float32)
    for c in range(3):
        hp[:, 4 * c:4 * c + 4] = gdn_conv[0][:, c * 1024 + r * 128:c * 1024 + (r + 1) * 128].T
    hp[:, 12] = gdn_a_log[0, r]
    hp[:, 13] = gdn_dt_bias[0, r]
    hp[:, 14] = gdn_norm_w[0]
    return {"win": win, "hp": hp}


def run_p2(x1T_list, com, gdn_w_in, gdn_conv, gdn_a_log, gdn_dt_bias, gdn_norm_w):
    xg = np.ascontiguousarray(np.stack(x1T_list, axis=0))
    ins = []
    for r in range(NCORES):
        m = dict(com)
        m.update(_p2_inputs(r, gdn_w_in, gdn_conv, gdn_a_log, gdn_dt_bias, gdn_norm_w))
        m["xg"] = xg
        ins.append(m)
    res = run_bass_kernel_spmd(_get_nc("p2"), ins, core_ids=list(range(NCORES)))
    return [r_["og"] for r_ in res.results]


def run_p3(acc_list, og_list, com, p, gdn_w_out, mlp_w1, mlp_w2, ple_gate_w, ple_proj):
    ins = []
    for r in range(NCORES):
        b, s = r // 4, r % 4
        t0 = b * SEQ + s * TOK
        ogin = np.ascontiguousarray(np.stack([og_list[h][:, t0:t0 + TOK] for h in range(8)], axis=0))
        m = dict(com)
        m.update({"accin": acc_list[r], "ogin": ogin, "p1": np.ascontiguousarray(p[1, b, s * TOK:(s + 1) * TOK]),
                  "gdn_w_out": np.ascontiguousarray(gdn_w_out[0]), "mlp_w11": np.ascontiguousarray(mlp_w1[1]),
                  "mlp_w21": np.ascontiguousarray(mlp_w2[1]), "ple_gate_w1": np.ascontiguousarray(ple_gate_w[1]),
                  "ple_proj1": np.ascontiguousarray(ple_proj[1])})
        ins.append(m)
    res = run_bass_kernel_spmd(_get_nc("p3"), ins, core_ids=list(range(NCORES)))
    return [r_["out"] for r_ in res.results]


def kernel(x, p, ln_gain, ln_bias, pool_w, pool_b, pool_scale, gdn_w_in, gdn_conv, gdn_a_log, gdn_dt_bias,
           gdn_norm_w, gdn_w_out, mlp_w1, mlp_w2, ple_gate_w, ple_gate_b, ple_proj):
    args = [np.asarray(a) for a in (x, p, ln_gain, ln_bias, pool_w, pool_b, pool_scale, gdn_w_in, gdn_conv,
                                    gdn_a_log, gdn_dt_bias, gdn_norm_w, gdn_w_out, mlp_w1, mlp_w2, ple_gate_w,
                                    ple_gate_b, ple_proj)]
    (x, p, ln_gain, ln_bias, pool_w, pool_b, pool_scale, gdn_w_in, gdn_conv, gdn_a_log, gdn_dt_bias,
     gdn_norm_w, gdn_w_out, mlp_w1, mlp_w2, ple_gate_w, ple_gate_b, ple_proj) = args
    if FUSED:
        return run_fused(x, p, ln_gain, ln_bias, pool_w, pool_b, pool_scale, gdn_w_in, gdn_conv, gdn_a_log,
                         gdn_dt_bias, gdn_norm_w, gdn_w_out, mlp_w1, mlp_w2, ple_gate_w, ple_gate_b, ple_proj)
    com = _common_inputs(ln_gain, ln_bias, pool_b, pool_scale, ple_gate_b)
    accs, x1Ts = run_p1(x, p, ln_gain, ln_bias, pool_w, pool_b, pool_scale, mlp_w1, mlp_w2, ple_gate_w, ple_gate_b,
                        ple_proj)
    ogs = run_p2(x1Ts, com, gdn_w_in, gdn_conv, gdn_a_log, gdn_dt_bias, gdn_norm_w)
    outs = run_p3(accs, ogs, com, p, gdn_w_out, mlp_w1, mlp_w2, ple_gate_w, ple_proj)
    out = np.zeros((2, SEQ, D), np.float32)
    for r in range(NCORES):
        b, s = r // 4, r % 4
        out[b, s * TOK:(s + 1) * TOK] = outs[r]
    return out
```
